# Optimizing a Trainium2 kernel written in Bass

```python
import jax, jax.numpy as jnp
from jax import lax
import numpy as np


D_MODEL = 1024
BATCH = 8
SEQ = 2048
DEPTH = 1
DEC_BATCH = 128
DEC_SEQ = 8
PAST_LEN = 16384
PAGE_SIZE = 128

GDN_HEADS = 8
GDN_DK = 128
GDN_DV = 128
GDN_QK_W = GDN_HEADS * GDN_DK
GDN_V_W = GDN_HEADS * GDN_DV
GDN_CONV = 4
GDN_CHUNK = 64
SC_W = D_MODEL
SC_CONV = 3
X_HEADS = 4
X_HEAD_DIM = D_MODEL // X_HEADS
N_MEM = 256
D_FF = 4 * D_MODEL
EPS = 1e-6
IN_SIZES = (GDN_QK_W, GDN_QK_W, GDN_V_W, GDN_V_W, GDN_HEADS, GDN_HEADS, SC_W, SC_W, SC_W, D_MODEL, D_MODEL)
IN_COLS = sum(IN_SIZES)
GDN_CONV_CH = 2 * GDN_QK_W + GDN_V_W

kernel_name = 'hybrid_gdn_shortconv_xattn_step'


def rmsnorm(x, w):
    xf = x.astype(jnp.float32)
    xf = xf * lax.rsqrt(jnp.mean(xf * xf, axis=-1, keepdims=True) + EPS)
    return (xf * w.astype(jnp.float32)).astype(x.dtype)


def l2norm(x):
    xf = x.astype(jnp.float32)
    return xf * lax.rsqrt(jnp.sum(xf * xf, axis=-1, keepdims=True) + EPS)


def causal_dwconv(x, buf, w):
    width = w.shape[0]
    T = x.shape[1]
    xx = jnp.concatenate([buf.astype(x.dtype), x], axis=1)
    y = xx[:, 0:T] * w[0]
    for i in range(1, width):
        y = y + xx[:, i:i + T] * w[i]
    return y, xx[:, xx.shape[1] - (width - 1):]


def split_cols(proj):
    idx = []
    acc = 0
    for s in IN_SIZES[:-1]:
        acc += s
        idx.append(acc)
    return jnp.split(proj, idx, axis=-1)


def gated_delta_rule(q, k, v, g, beta, S0):
    B, T, H, DK = q.shape
    DV = v.shape[-1]
    C = min(GDN_CHUNK, T)
    n = -(-T // C)
    pad = n * C - T
    q = jnp.pad(q, ((0, 0), (0, pad), (0, 0), (0, 0)))
    k = jnp.pad(k, ((0, 0), (0, pad), (0, 0), (0, 0)))
    v = jnp.pad(v, ((0, 0), (0, pad), (0, 0), (0, 0)))
    g = jnp.pad(g, ((0, 0), (0, pad), (0, 0)))
    beta = jnp.pad(beta, ((0, 0), (0, pad), (0, 0)))
    to_chunks4 = lambda t: t.reshape(B, n, C, H, t.shape[-1]).transpose(1, 0, 3, 2, 4)
    to_chunks3 = lambda t: t.reshape(B, n, C, H).transpose(1, 0, 3, 2)
    q, k, v = to_chunks4(q), to_chunks4(k), to_chunks4(v)
    g, beta = to_chunks3(g), to_chunks3(beta)
    gc = jnp.cumsum(g, axis=-1)
    incl = jnp.tril(jnp.ones((C, C), dtype=bool))
    strict = jnp.tril(jnp.ones((C, C), dtype=bool), -1)
    decay = jnp.exp(jnp.where(incl, gc[..., :, None] - gc[..., None, :], -jnp.inf))
    kk = jnp.einsum('nbhck,nbhdk->nbhcd', k, k)
    L = jnp.where(strict, beta[..., :, None] * kk * decay, 0.0)
    A = L + jnp.eye(C, dtype=L.dtype)
    kb_e = k * (beta * jnp.exp(gc))[..., None]
    vb = v * beta[..., None]
    W = lax.linalg.triangular_solve(A, kb_e, left_side=True, lower=True, unit_diagonal=True)
    Uv = lax.linalg.triangular_solve(A, vb, left_side=True, lower=True, unit_diagonal=True)
    P = jnp.einsum('nbhck,nbhdk->nbhcd', q, k) * decay
    q_e = q * jnp.exp(gc)[..., None]
    k_e = k * jnp.exp(gc[..., -1:] - gc)[..., None]
    g_tot = jnp.exp(gc[..., -1])

    def step(S, xs):
        W_c, Uv_c, P_c, qe_c, ke_c, gt_c = xs
        U = Uv_c - jnp.einsum('bhck,bhkv->bhcv', W_c, S)
        O = jnp.einsum('bhck,bhkv->bhcv', qe_c, S) + jnp.einsum('bhcd,bhdv->bhcv', P_c, U)
        S_new = gt_c[..., None, None] * S + jnp.einsum('bhck,bhcv->bhkv', ke_c, U)
        return S_new, O

    S_fin, O = lax.scan(step, S0, (W, Uv, P, q_e, k_e, g_tot))
    O = O.transpose(1, 0, 3, 2, 4).reshape(B, n * C, H, DV)[:, :T]
    return O, S_fin


def memory_kv(mem, mem_norm_w, w_xkv):
    B = mem.shape[0]
    kv = rmsnorm(mem, mem_norm_w) @ w_xkv
    mk, mv = jnp.split(kv, 2, axis=-1)
    return mk.reshape(B, N_MEM, X_HEADS, X_HEAD_DIM), mv.reshape(B, N_MEM, X_HEADS, X_HEAD_DIM)


def hybrid_layer(x, mem_k, mem_v, gdn_conv_buf, gdn_S, sc_buf,
                 norm_mix_w, w_in, gdn_conv_w, gdn_A_log, gdn_dt_bias, gdn_out_norm_w, w_gdn_o,
                 sc_conv_w, w_sc_o, w_mix_out, norm_x_w, w_xq, w_xo, norm_mlp_w, w_mlp_up, w_mlp_down):
    B, T, _ = x.shape
    f32 = jnp.float32
    h = rmsnorm(x, norm_mix_w)
    q, k, v, z, a, b, bg, cg, xin, ga, gb = split_cols(h @ w_in)
    qkv, new_gdn_conv_buf = causal_dwconv(jnp.concatenate([q, k, v], axis=-1), gdn_conv_buf, gdn_conv_w)
    qkv = jax.nn.silu(qkv)
    q, k, v = jnp.split(qkv, [GDN_QK_W, 2 * GDN_QK_W], axis=-1)
    q = l2norm(q.reshape(B, T, GDN_HEADS, GDN_DK)) * (GDN_DK ** -0.5)
    k = l2norm(k.reshape(B, T, GDN_HEADS, GDN_DK))
    v = v.reshape(B, T, GDN_HEADS, GDN_DV).astype(f32)
    g = -jnp.exp(gdn_A_log.astype(f32)) * jax.nn.softplus(a.astype(f32) + gdn_dt_bias.astype(f32))
    beta = jax.nn.sigmoid(b.astype(f32))
    o, new_S = gated_delta_rule(q, k, v, g, beta, gdn_S.astype(f32))
    o = rmsnorm(o, gdn_out_norm_w) * jax.nn.silu(z.reshape(B, T, GDN_HEADS, GDN_DV).astype(f32))
    y_a = o.reshape(B, T, GDN_V_W).astype(x.dtype) @ w_gdn_o
    u, new_sc_buf = causal_dwconv(cg * xin, sc_buf, sc_conv_w)
    y_b = (bg * u) @ w_sc_o
    merged = jax.nn.sigmoid(ga) * y_a + jax.nn.sigmoid(gb) * y_b
    x = x + merged @ w_mix_out
    hq = (rmsnorm(x, norm_x_w) @ w_xq).reshape(B, T, X_HEADS, X_HEAD_DIM)
    s = jnp.einsum('bthd,bshd->bhts', hq.astype(f32), mem_k.astype(f32)) * (X_HEAD_DIM ** -0.5)
    p = jax.nn.softmax(s, axis=-1)
    ctx = jnp.einsum('bhts,bshd->bthd', p, mem_v.astype(f32)).reshape(B, T, D_MODEL).astype(x.dtype)
    x = x + ctx @ w_xo
    hm = rmsnorm(x, norm_mlp_w) @ w_mlp_up
    x = x + jnp.square(jax.nn.relu(hm)) @ w_mlp_down
    return x, new_gdn_conv_buf, new_S.astype(gdn_S.dtype), new_sc_buf


def setup_inputs(seed: int = 0) -> dict:
    key = jax.random.key(seed)
    ks = iter(jax.random.split(key, 40))
    nk = lambda: next(ks)
    L = DEPTH
    w = lambda shape, fan_in: jax.random.normal(nk(), shape, jnp.float32) * (fan_in ** -0.5)
    gain = lambda shape: 1.0 + 0.02 * jax.random.normal(nk(), shape, jnp.float32)
    dt = jax.random.uniform(nk(), (L, GDN_HEADS), jnp.float32, minval=0.001, maxval=0.1)
    return {
        'x_prompt': jax.random.normal(nk(), (BATCH, SEQ, D_MODEL), jnp.float32),
        'x_sample': jax.random.normal(nk(), (DEC_BATCH, DEC_SEQ, D_MODEL), jnp.float32),
        'mem_prompt': jax.random.normal(nk(), (BATCH, N_MEM, D_MODEL), jnp.float32),
        'cache_mem_k': jax.random.normal(nk(), (L, DEC_BATCH, N_MEM, X_HEADS, X_HEAD_DIM), jnp.float32),
        'cache_mem_v': jax.random.normal(nk(), (L, DEC_BATCH, N_MEM, X_HEADS, X_HEAD_DIM), jnp.float32),
        'state_gdn_conv': jax.random.normal(nk(), (L, DEC_BATCH, GDN_CONV - 1, GDN_CONV_CH), jnp.float32),
        'state_gdn': 0.1 * jax.random.normal(nk(), (L, DEC_BATCH, GDN_HEADS, GDN_DK, GDN_DV), jnp.float32),
        'state_sc_conv': jax.random.normal(nk(), (L, DEC_BATCH, SC_CONV - 1, SC_W), jnp.float32),
        'norm_mix_w': gain((L, D_MODEL)),
        'w_in': w((L, D_MODEL, IN_COLS), D_MODEL),
        'gdn_conv_w': w((L, GDN_CONV, GDN_CONV_CH), GDN_CONV),
        'gdn_A_log': jnp.log(jax.random.uniform(nk(), (L, GDN_HEADS), jnp.float32, minval=1.0, maxval=16.0)),
        'gdn_dt_bias': jnp.log(jnp.expm1(dt)),
        'gdn_out_norm_w': gain((L, GDN_DV)),
        'w_gdn_o': w((L, GDN_V_W, D_MODEL), GDN_V_W),
        'sc_conv_w': w((L, SC_CONV, SC_W), SC_CONV),
        'w_sc_o': w((L, SC_W, D_MODEL), SC_W),
        'w_mix_out': w((L, D_MODEL, D_MODEL), D_MODEL),
        'norm_x_w': gain((L, D_MODEL)),
        'mem_norm_w': gain((L, D_MODEL)),
        'w_xq': w((L, D_MODEL, D_MODEL), D_MODEL),
        'w_xkv': w((L, D_MODEL, 2 * D_MODEL), D_MODEL),
        'w_xo': w((L, D_MODEL, D_MODEL), D_MODEL),
        'norm_mlp_w': gain((L, D_MODEL)),
        'w_mlp_up': w((L, D_MODEL, D_FF), D_MODEL),
        'w_mlp_down': w((L, D_FF, D_MODEL), D_FF),
        'norm_f_w': gain((D_MODEL,)),
    }


def reference(x_prompt, x_sample, mem_prompt, cache_mem_k, cache_mem_v, state_gdn_conv, state_gdn, state_sc_conv,
              norm_mix_w, w_in, gdn_conv_w, gdn_A_log, gdn_dt_bias, gdn_out_norm_w, w_gdn_o, sc_conv_w, w_sc_o,
              w_mix_out, norm_x_w, mem_norm_w, w_xq, w_xkv, w_xo, norm_mlp_w, w_mlp_up, w_mlp_down, norm_f_w):
    hp, hs = x_prompt, x_sample
    p_mk, p_mv, p_conv, p_S, p_sc = [], [], [], [], []
    s_conv, s_S, s_sc = [], [], []
    for i in range(DEPTH):
        lw = (norm_mix_w[i], w_in[i], gdn_conv_w[i], gdn_A_log[i], gdn_dt_bias[i], gdn_out_norm_w[i], w_gdn_o[i],
              sc_conv_w[i], w_sc_o[i], w_mix_out[i], norm_x_w[i], w_xq[i], w_xo[i], norm_mlp_w[i],
              w_mlp_up[i], w_mlp_down[i])
        mk, mv = memory_kv(mem_prompt, mem_norm_w[i], w_xkv[i])
        zero_conv = jnp.zeros((BATCH, GDN_CONV - 1, GDN_CONV_CH), hp.dtype)
        zero_S = jnp.zeros((BATCH, GDN_HEADS, GDN_DK, GDN_DV), state_gdn.dtype)
        zero_sc = jnp.zeros((BATCH, SC_CONV - 1, SC_W), hp.dtype)
        hp, pc, pS, psc = hybrid_layer(hp, mk, mv, zero_conv, zero_S, zero_sc, *lw)
        p_mk.append(mk)
        p_mv.append(mv)
        p_conv.append(pc)
        p_S.append(pS)
        p_sc.append(psc)
        hs, sc, sS, ssc = hybrid_layer(hs, cache_mem_k[i], cache_mem_v[i], state_gdn_conv[i], state_gdn[i],
                                       state_sc_conv[i], *lw)
        s_conv.append(sc)
        s_S.append(sS)
        s_sc.append(ssc)
    y_prompt = rmsnorm(hp, norm_f_w)
    y_sample = rmsnorm(hs, norm_f_w)
    return (y_prompt, y_sample,
            jnp.stack(p_mk), jnp.stack(p_mv), jnp.stack(p_conv), jnp.stack(p_S), jnp.stack(p_sc),
            jnp.stack(s_conv), jnp.stack(s_S), jnp.stack(s_sc))
```

```python
import numpy as np
from contextlib import ExitStack
import concourse.bass as bass
import concourse.mybir as mybir
from concourse.bass_utils import run_bass_kernel_spmd

F32 = mybir.dt.float32
BF16 = mybir.dt.bfloat16
AF = mybir.ActivationFunctionType
ALU = mybir.AluOpType
AX = mybir.AxisListType

NCORES = 8
D = 1024
TP = 2048
NT = 17
TOK = NT * 128
EPS = 1e-6
NEG = -30000.0
GROUPS = [(0, 512), (512, 512), (1024, 512), (1536, 512), (2048, 128)]

C_ID, C_NMP, C_NMS, C_STP, C_STS, C_CUMP, C_CUMS, C_ONEP, C_ONES, C_BCOL, C_SEL = (
    0, 128, 256, 384, 512, 640, 768, 896, 1024, 1152, 1168)
CW = 1168 + 1024
CBW = 640


def _consts():
    c = np.zeros((128, CW), np.float32)
    p = np.arange(128)[:, None]
    f = np.arange(128)[None, :]
    same = (p // 8) == (f // 8)
    c[:, C_ID:C_ID + 128] = (p == f)
    c[:, C_NMP:C_NMP + 128] = np.where(f >= p, 0.0, NEG)
    c[:, C_NMS:C_NMS + 128] = np.where(same & (f >= p), 0.0, NEG)
    c[:, C_STP:C_STP + 128] = (f > p)
    c[:, C_STS:C_STS + 128] = same & (f > p)
    c[:, C_CUMP:C_CUMP + 128] = (p <= f)
    c[:, C_CUMS:C_CUMS + 128] = same & (p <= f)
    c[:, C_ONEP:C_ONEP + 128] = 1.0
    c[:, C_ONES:C_ONES + 128] = same
    c[:, C_BCOL:C_BCOL + 16] = (p // 8) == np.arange(16)[None, :]
    for h in range(8):
        c[h, C_SEL + h * 128:C_SEL + (h + 1) * 128] = 1.0
    return c


class _StopBuild(Exception):
    pass


def _chk(stage):
    import os
    if os.environ.get("KSTOP", "") == stage:
        raise _StopBuild()


class Prog:
    def __init__(self, nc, n_dma_sems=48):
        self.nc = nc
        self.eng = {'pe': nc.tensor, 'act': nc.scalar, 'dve': nc.vector, 'pool': nc.gpsimd, 'sp': nc.sync}
        self.csem = {e: nc.alloc_semaphore('c_' + e) for e in ('pe', 'act', 'dve', 'pool')}
        self.cc = {e: 0 for e in self.csem}
        self.dsems = [nc.alloc_semaphore('d_%d' % k) for k in range(n_dma_sems)]
        self.dval = [0] * n_dma_sems
        self.dpool = {'sp': list(range(0, n_dma_sems // 2)), 'pool': list(range(n_dma_sems // 2, n_dma_sems))}
        self.dnext = {'sp': 0, 'pool': 0}
        self.nd = 0
        self.last_w = {}
        self.readers = {}
        self.waited = {e: {} for e in self.eng}
        self.n_ops = 0
        self.n_wait = 0

    def _semof(self, s):
        return self.csem[s[1]] if s[0] == 'c' else self.dsems[s[1]]

    @staticmethod
    def _banks(key):
        if not isinstance(key, str):
            return ()
        if key.startswith('bank'):
            return ('B' + key[4],)
        if key.startswith('PK'):
            return ('BPK',)
        if key.startswith('PU'):
            return ('BPU',)
        if key.startswith('pt'):
            return ('BPT' + key[2],)
        if key == 'PS2':
            return ('B2', 'B3')
        return ()

    def rec_begin(self):
        self._rec = []

    def rec_end(self):
        r_ = self._rec
        self._rec = None
        return r_

    def emit_merged(self, streams):
        streams = [st for st in streams if st]
        pos = [0] * len(streams)
        while True:
            best = None
            for k, st in enumerate(streams):
                if pos[k] < len(st):
                    fr = pos[k] / len(st)
                    if best is None or fr < best[0]:
                        best = (fr, k)
            if best is None:
                break
            k = best[1]
            self.add(*streams[k][pos[k]][:6])
            pos[k] += 1

    def emit_scheduled(self, ops, lat=0.25):
        n = len(ops)
        preds = [set() for _ in range(n)]
        last_w = {}
        readers = {}
        for idx, (eng, fn, r, w, dma, inc, cost) in enumerate(ops):
            bk_ = []
            for x in list(r) + list(w):
                for b in self._banks(x):
                    if b not in bk_:
                        bk_.append(b)
            wf = list(w) + bk_
            pr = preds[idx]
            for x in r:
                if x in last_w:
                    pr.add(last_w[x])
            for x in wf:
                if x in last_w:
                    pr.add(last_w[x])
                for rd in readers.get(x, ()):
                    pr.add(rd)
            pr.discard(idx)
            for x in r:
                readers.setdefault(x, []).append(idx)
            for x in wf:
                last_w[x] = idx
                readers[x] = []
        succs = [[] for _ in range(n)]
        indeg = [0] * n
        for i in range(n):
            indeg[i] = len(preds[i])
            for p_ in preds[i]:
                succs[p_].append(i)
        cst_ = [(0.3 if o[6] is None else o[6]) for o in ops]
        prio = [0.0] * n
        for i in range(n - 1, -1, -1):
            m = 0.0
            for s_ in succs[i]:
                v = prio[s_] + lat
                if v > m:
                    m = v
            prio[i] = cst_[i] + m
        eng_free = {}
        dma_free = {}
        fin = [0.0] * n
        ready = set(i for i in range(n) if indeg[i] == 0)
        while ready:
            best = None
            for i in ready:
                eng = ops[i][0]
                est = eng_free.get(eng, 0.0)
                for p_ in preds[i]:
                    v = fin[p_] + (lat if ops[p_][0] != eng or ops[p_][4] else 0.0)
                    if v > est:
                        est = v
                key = (round(est * 5.0), -prio[i], i)
                if best is None or key < best[0]:
                    best = (key, i, est)
            _, i, est = best
            eng, fn, r, w, dma, inc, cost = ops[i]
            if dma:
                st_ = max(est, dma_free.get(eng, 0.0))
                fin[i] = st_ + cst_[i]
                dma_free[eng] = fin[i] - 1.5
                eng_free[eng] = est + 0.1
            else:
                fin[i] = est + cst_[i]
                eng_free[eng] = fin[i]
            self.add(eng, fn, r, w, dma, True)
            self.sim_t = max(getattr(self, 'sim_t', 0.0), fin[i])
            ready.remove(i)
            for s_ in succs[i]:
                indeg[s_] -= 1
                if indeg[s_] == 0:
                    ready.add(s_)

    def add(self, eng, fn, r=(), w=(), dma=False, inc=True, cost=None):
        if getattr(self, '_rec', None) is not None:
            self._rec.append((eng, fn, tuple(r), tuple(w), dma, inc, cost))
            return
        wt = {}
        bk_ = []
        for x in list(r) + list(w):
            for b in self._banks(x):
                if b not in bk_:
                    bk_.append(b)
        if bk_:
            w = list(w) + bk_

        def consider(rec, raw):
            reng, rdma, s, v = rec
            if (not rdma) and (not dma) and reng == eng and ((not raw) or eng == 'pe'):
                return
            if wt.get(s, 0) < v:
                wt[s] = v

        for x in r:
            rec = self.last_w.get(x)
            if rec is not None:
                consider(rec, True)
        for x in w:
            rec = self.last_w.get(x)
            if rec is not None:
                consider(rec, False)
            rd = self.readers.get(x)
            if rd:
                for rec in rd.values():
                    consider(rec, False)
        k = None
        if dma:
            lst = self.dpool[eng]
            k = lst[self.dnext[eng] % len(lst)]
            self.dnext[eng] += 1
            self.nd += 1
            if self.dval[k] > 0:
                s = ('d', k)
                if wt.get(s, 0) < self.dval[k]:
                    wt[s] = self.dval[k]
        e = self.eng[eng]
        wd = self.waited[eng]
        for s, v in wt.items():
            if wd.get(s, 0) >= v:
                continue
            wd[s] = v
            e.wait_ge(self._semof(s), v)
            self.n_wait += 1
        if fn is None:
            return
        ins = fn()
        self.n_ops += 1
        if dma:
            self.dval[k] += 16
            ins.then_inc(self.dsems[k], 16)
            rec = (eng, True, ('d', k), self.dval[k])
        else:
            if inc:
                self.cc[eng] += 1
                ins.then_inc(self.csem[eng], 1)
                rec = (eng, False, ('c', eng), self.cc[eng])
            else:
                rec = (eng, False, ('c', eng), self.cc[eng] + 1)
        for x in r:
            self.readers.setdefault(x, {})[rec[2]] = rec
        for x in w:
            self.last_w[x] = rec
            self.readers[x] = {}

    def barrier(self):
        for E, e in self.eng.items():
            wd = self.waited[E]
            for F in self.csem:
                if F == E:
                    continue
                v = self.cc[F]
                s = ('c', F)
                if v > 0 and wd.get(s, 0) < v:
                    wd[s] = v
                    e.wait_ge(self.csem[F], v)
            for k, v in enumerate(self.dval):
                s = ('d', k)
                if v > 0 and wd.get(s, 0) < v:
                    wd[s] = v
                    e.wait_ge(self.dsems[k], v)

    @staticmethod
    def _fcost(eng, out):
        n = 1
        for d_ in out.shape[1:]:
            n *= int(d_)
        if eng == 'act':
            return 0.2 + 0.0009 * n
        if eng == 'pool':
            return 0.15 + 0.002 * n
        return 0.08 + 0.0011 * n

    def mm(self, out, lhsT, rhs, start=True, stop=True, r=(), w=(), inc=None, skip=False):
        nc = self.nc
        if inc is None:
            inc = stop
        c_ = max(0.107, 0.00042 * out.shape[-1]) * (2.5 if lhsT.dtype == F32 else 1.0)
        if skip:
            self.add('pe', lambda: nc.tensor.matmul(out, lhsT=lhsT, rhs=rhs, start=start, stop=stop,
                                                    skip_group_check=True), r, w, inc=inc, cost=c_)
        else:
            self.add('pe', lambda: nc.tensor.matmul(out, lhsT=lhsT, rhs=rhs, start=start, stop=stop), r, w, inc=inc, cost=c_)

    def tr(self, out, in_, ident, r=(), w=(), inc=True):
        nc = self.nc
        self.add('pe', lambda: nc.tensor.transpose(out, in_, ident), r, w, inc=inc, cost=0.107)

    def act(self, out, in_, func, bias=None, scale=None, accum=None, r=(), w=()):
        nc = self.nc
        kw = {}
        if bias is not None:
            kw['bias'] = bias
        if scale is not None:
            kw['scale'] = scale
        if accum is not None:
            kw['accum_out'] = accum
        self.add('act', lambda: nc.scalar.activation(out=out, in_=in_, func=func, **kw), r, w, cost=self._fcost('act', out))

    def ts(self, eng, out, in0, s1, op0, s2=None, op1=None, r=(), w=()):
        e = self.eng[eng]
        kw = {}
        if op1 is not None:
            kw['op1'] = op1
        self.add(eng, lambda: e.tensor_scalar(out=out, in0=in0, scalar1=s1, scalar2=s2, op0=op0, **kw), r, w,
                 cost=self._fcost(eng, out))

    def stt(self, eng, out, in0, scalar, in1, op0, op1, r=(), w=()):
        e = self.eng[eng]
        self.add(eng, lambda: e.scalar_tensor_tensor(out=out, in0=in0, scalar=scalar, in1=in1, op0=op0, op1=op1), r, w,
                 cost=self._fcost(eng, out))

    def tt(self, eng, out, in0, in1, op, r=(), w=()):
        e = self.eng[eng]
        self.add(eng, lambda: e.tensor_tensor(out=out, in0=in0, in1=in1, op=op), r, w, cost=self._fcost(eng, out))

    def cp(self, eng, out, in_, r=(), w=()):
        if eng == 'act':
            nc = self.nc
            self.add('act', lambda: nc.scalar.copy(out=out, in_=in_), r, w, cost=self._fcost('act', out))
        else:
            e = self.eng[eng]
            self.add(eng, lambda: e.tensor_copy(out=out, in_=in_), r, w, cost=self._fcost(eng, out))

    def recip(self, out, in_, r=(), w=()):
        nc = self.nc
        self.add('dve', lambda: nc.vector.reciprocal(out=out, in_=in_), r, w)

    def memset(self, eng, ap, val, r=(), w=()):
        e = self.eng[eng]
        self.add(eng, lambda: e.memset(ap, val), r, w)

    def dma(self, q, out, in_, r=(), w=(), slow=False):
        e = self.eng[q]
        n = 1
        for d_ in out.shape:
            n *= int(d_)
        c_ = 2.0 + n * 4.0 / 150e3
        if slow:
            self.add(q, lambda: e.dma_start(out=out, in_=in_, allow_slow_non_contiguous=True), r, w, dma=True, cost=c_)
        else:
            self.add(q, lambda: e.dma_start(out=out, in_=in_), r, w, dma=True, cost=c_)


def build():
    nc = bass.Bass("TRN2", target_bir_lowering=False)

    def din(name, shape):
        return nc.dram_tensor(name, list(shape), F32, kind="ExternalInput").ap()

    def dout(name, shape):
        return nc.dram_tensor(name, list(shape), F32, kind="ExternalOutput").ap()

    x_d = din("x_all", [TOK, D])
    mem_d = din("mem", [256, D])
    ck_d = din("ck", [16, 256, D])
    cv_d = din("cv", [16, 256, D])
    sgc_d = din("sgc", [48, 3072])
    sS_d = din("sS", [16, 8, 128, 128])
    ssc_d = din("ssc", [32, D])
    cst_d = din("cst", [128, CW])
    W = {}
    for nm, shp in [("norm_mix_w", [D]), ("w_in", [D, 9232]), ("gdn_conv_w", [4, 3072]), ("gdn_A_log", [8]),
                    ("gdn_dt_bias", [8]), ("gdn_out_norm_w", [128]), ("w_gdn_o", [D, D]), ("sc_conv_w", [3, D]),
                    ("w_sc_o", [D, D]), ("w_mix_out", [D, D]), ("norm_x_w", [D]), ("mem_norm_w", [D]),
                    ("w_xq", [D, D]), ("w_xkv", [D, 2 * D]), ("w_xo", [D, D]), ("norm_mlp_w", [D]),
                    ("w_mlp_up", [D, 4 * D]), ("w_mlp_down", [4 * D, D]), ("norm_f_w", [D])]:
        W[nm] = din(nm, shp)
    y_d = dout("y", [TOK, D])
    mk_d = dout("mk", [256, D])
    mv_d = dout("mv", [256, D])
    gcp_d = dout("gcp", [3, 3072])
    Sp_d = dout("Sp", [8, 128, 128])
    scp_d = dout("scp", [2, D])
    gcs_d = dout("gcs", [48, 3072])
    Ss_d = dout("Ss", [16, 8, 128, 128])
    scs_d = dout("scs", [32, D])
    xres_d = nc.dram_tensor("xres", [TOK, D], F32, kind="Internal").ap()

    P = Prog(nc)
    outer = ExitStack()

    def sbt(es, name, shape, dt):
        return es.enter_context(nc.sbuf_tensor("sb_" + name, list(shape), dt))

    with outer:
        PAB = outer.enter_context(nc.psum_tensor("PAB", [128, 1024], F32))
        PS2 = outer.enter_context(nc.psum_tensor("PS2", [128, 1024], F32))
        PK = outer.enter_context(nc.psum_tensor("PK", [128, 512], F32))
        PU = outer.enter_context(nc.psum_tensor("PU", [128, 512], F32))
        PT0 = outer.enter_context(nc.psum_tensor("PT0", [128, 1024], BF16))
        PT1 = outer.enter_context(nc.psum_tensor("PT1", [128, 1024], BF16))
        PTs = [PT0, PT1]
        banks = [PAB[:, 0:512], PAB[:, 512:1024], PS2[:, 0:512], PS2[:, 512:1024]]
        bstate = {'i': 0, 'pt': 0}

        def nbank():
            i = bstate['i'] % 4
            bstate['i'] += 1
            return banks[i], 'bank%d' % i

        def nbank2():
            i = bstate['i'] % 2
            bstate['i'] += 1
            return banks[i], 'bank%d' % i

        def npt():
            i = bstate['pt'] % 2
            bstate['pt'] += 1
            return PTs[i], 'pt%d' % i

        cst = sbt(outer, "cst", [128, CW], F32)
        cstb = sbt(outer, "cstb", [128, CBW], BF16)
        ones1b = sbt(outer, "ones1b", [128, 128], BF16)
        ones128b = sbt(outer, "ones128b", [128, 128], BF16)
        onesi128b = sbt(outer, "onesi128b", [128, 128], BF16)
        wn_mix = sbt(outer, "wn_mix", [128, 8], F32)
        wn_x = sbt(outer, "wn_x", [128, 8], F32)
        wn_mlp = sbt(outer, "wn_mlp", [128, 8], F32)
        wn_mem = sbt(outer, "wn_mem", [128, 8], F32)
        KT = sbt(outer, "KT", [128, 8, 256], BF16)
        Vb = sbt(outer, "Vb", [128, 2, D], BF16)
        xts, njunk, nxs, nss = [], [], [], []
        nstate = {'i': 0, 'x': 0}

        def alloc_norm(es_, tag, with_x=True):
            if with_x:
                xts[:] = [sbt(es_, "xt%s%d" % (tag, i), [128, D], F32) for i in range(2)]
            njunk[:] = [sbt(es_, "njunk%s%d" % (tag, i), [128, D], BF16) for i in range(2)]
            nxs[:] = [sbt(es_, "nxs%s%d" % (tag, i), [128, D], BF16) for i in range(2)]
            nss[:] = [sbt(es_, "nss%s%d" % (tag, i), [128, 4], F32) for i in range(2)]

        esX = outer.enter_context(ExitStack())
        xnT = sbt(esX, "xnT", [128, 8, TOK], BF16)
        esN = outer.enter_context(ExitStack())
        alloc_norm(esN, "a")

        identf = cst[:, C_ID:C_ID + 128]
        identb = cstb[:, C_ID:C_ID + 128]

        P.dma('sp', cst[:], cst_d, w=['cst'])
        P.dma('pool', cstb[:], cst_d[:, 0:CBW], w=['cstb'])
        P.memset('dve', ones1b[:], 1.0, w=['ones1b'])
        P.memset('dve', ones128b[:], 128.0, w=['ones128b'])
        P.memset('dve', onesi128b[:], 1.0 / 128.0, w=['onesi128b'])
        for t, nm in [(wn_mix, "norm_mix_w"), (wn_x, "norm_x_w"), (wn_mlp, "norm_mlp_w"), (wn_mem, "mem_norm_w")]:
            P.dma('sp', t[:], W[nm].rearrange("(k p) -> p k", p=128), w=[nm], slow=True)

        def rstd_of(src, rk, si):
            ss = nss[si]
            P.act(njunk[si][:], src, AF.Square, accum=ss[:, 0:1], r=rk, w=['njunk%d' % si, 'nss%d' % si])
            P.act(ss[:, 1:2], ss[:, 0:1], AF.Ln, bias=EPS, scale=1.0 / D, r=['nss%d' % si], w=['nss%db' % si])
            P.act(ss[:, 2:3], ss[:, 1:2], AF.Exp, scale=-0.5, r=['nss%db' % si], w=['nss%dc' % si])
            return ss[:, 2:3], 'nss%dc' % si

        def norm_tile(src, rk, wn, wnk, dst, wk):
            si = nstate['i'] % 2
            nstate['i'] += 1
            rstd, rsk = rstd_of(src, rk, si)
            P.act(nxs[si][:], src, AF.Copy, scale=rstd, r=list(rk) + [rsk], w=['nxs%d' % si])
            pt, ptk = npt()
            ptv = pt[:].rearrange("p (k c) -> p k c", k=8)
            for k in range(8):
                P.tr(ptv[:, k, :], nxs[si][:, k * 128:(k + 1) * 128], identb, r=['nxs%d' % si, 'cstb'], w=[ptk],
                     inc=(k == 7))
            P.tt('dve', dst, ptv, wn[:].unsqueeze(2).to_broadcast([128, 8, 128]), ALU.mult,
                 r=[ptk, wnk], w=wk)

        def load_x(i, src_d):
            xi = nstate['x'] % 2
            nstate['x'] += 1
            P.dma('sp', xts[xi][:], src_d[i * 128:(i + 1) * 128, :], w=['xt%d' % xi])
            return xts[xi], 'xt%d' % xi

        def proj_fm(bank, bk, wsel, act_t, act_keys, tok0, n, wkeys):
            for kc in range(8):
                P.mm(bank[:, 0:n], wsel(kc), act_t[:, kc, tok0:tok0 + n], start=(kc == 0), stop=(kc == 7),
                     r=list(wkeys) + list(act_keys), w=[bk])

        try:
            with ExitStack() as es:
                Wkv = sbt(es, "Wkv", [128, 8, 2 * D], BF16)
                memnT = sbt(es, "memnT", [128, 8, 256], BF16)
                kvtok = [sbt(es, "kvtok%d" % i, [128, 2 * D], F32) for i in range(2)]
                P.rec_begin()
                P.dma('pool', Wkv[:], W["w_xkv"].rearrange("(k p) n -> p k n", p=128), w=['Wkv'])
                for st in range(2):
                    xt, xk = load_x(st, mem_d)
                    norm_tile(xt[:], [xk], wn_mem, "mem_norm_w", memnT[:, :, st * 128:(st + 1) * 128], ['memnT%d' % st])
                for st in range(2):
                    for cb in range(4):
                        bank, bk = nbank()
                        for kc in range(8):
                            P.mm(bank, memnT[:, kc, st * 128:(st + 1) * 128], Wkv[:, kc, cb * 512:(cb + 1) * 512],
                                 start=(kc == 0), stop=(kc == 7), r=['memnT%d' % st, 'Wkv'], w=[bk])
                        P.cp('act', kvtok[st][:, cb * 512:(cb + 1) * 512], bank, r=[bk], w=['kvtok%d_%d' % (st, cb)])
                    P.dma('sp', mk_d[st * 128:(st + 1) * 128, :], kvtok[st][:, 0:D],
                          r=['kvtok%d_0' % st, 'kvtok%d_1' % st], w=['mk_d'])
                    P.dma('sp', mv_d[st * 128:(st + 1) * 128, :], kvtok[st][:, D:2 * D],
                          r=['kvtok%d_2' % st, 'kvtok%d_3' % st], w=['mv_d'])
                    P.cp('dve', Vb[:, st, :], kvtok[st][:, D:2 * D], r=['kvtok%d_2' % st, 'kvtok%d_3' % st], w=['Vb'])
                for c in range(8):
                    bank, bk = nbank()
                    for kc in range(8):
                        P.mm(bank[:, 0:256], Wkv[:, kc, c * 128:(c + 1) * 128], memnT[:, kc, :],
                             start=(kc == 0), stop=(kc == 7), r=['memnT0', 'memnT1', 'Wkv'], w=[bk])
                    P.cp('act', KT[:, c, :], bank[:, 0:256], r=[bk], w=['KT'])
                for i in range(NT):
                    xt, xk = load_x(i, x_d)
                    norm_tile(xt[:], [xk], wn_mix, "norm_mix_w", xnT[:, :, i * 128:(i + 1) * 128], [('xnT', i)])
                P.emit_scheduled(P.rec_end())
                P.barrier()
            _chk("M")

            XN_ALL = [('xnT', i) for i in range(NT)]
            esN.close()
            _chk("A")

            def xn_keys(tok0, n):
                return [('xnT', i) for i in range(tok0 // 128, (tok0 + n) // 128)]

            with ExitStack() as esB:
                oTn = sbt(esB, "oTn", [128, 8, TOK], BF16)

                with ExitStack() as es:
                    gtab = {}
                    for nm in ("g", "beta", "gc", "negc", "egc", "kdec", "glast", "gt"):
                        gtab[nm] = sbt(es, "g_" + nm, [128, NT, 8], F32)
                    gcT = sbt(es, "gcT", [8, TOK], F32)
                    gtS = sbt(es, "gtS", [128, 8, 16], F32)
                    cw = sbt(es, "cw", [128, 24, 4], F32)
                    won = sbt(es, "won", [128, 1], F32)
                    es0 = es.enter_context(ExitStack())
                    for nm in ("tmpm", "tmpn"):
                        gtab[nm] = sbt(es0, "g_" + nm, [128, NT, 8], F32)
                    ab = sbt(es0, "ab", [128, NT, 16], F32)
                    Wab = sbt(es0, "Wab", [128, 8, 16], BF16)
                    dtb = sbt(es0, "dtb", [128, 8], F32)
                    negA = sbt(es0, "negA", [128, 8], F32)
                    grep = sbt(es0, "grep", [128, 8, 128], F32)

                    P.dma('pool', Wab[:], W["w_in"][:, 4096:4112].rearrange("(k p) n -> p k n", p=128), w=['Wab'])
                    P.dma('sp', dtb[:], W["gdn_dt_bias"].partition_broadcast(128), w=['dtb'])
                    P.dma('sp', negA[:], W["gdn_A_log"].partition_broadcast(128), w=['negA0'])
                    for i4 in range(4):
                        P.dma('sp', cw[:, :, i4], W["gdn_conv_w"][i4].rearrange("(c p) -> p c", p=128), w=['cw'], slow=True)
                    P.dma('sp', won[:], W["gdn_out_norm_w"].rearrange("(p o) -> p o", o=1), w=['won'], slow=True)
                    P.act(negA[:], negA[:], AF.Exp, r=['negA0'], w=['negA1'])
                    P.ts('dve', negA[:], negA[:], -1.0, ALU.mult, r=['negA1'], w=['negA'])

                    bank, bk = nbank()
                    for i in range(NT):
                        for kc in range(8):
                            P.mm(bank[:, i * 16:(i + 1) * 16], xnT[:, kc, i * 128:(i + 1) * 128], Wab[:, kc, :],
                                 start=(kc == 0), stop=(kc == 7), r=[('xnT', i), 'Wab'], w=[bk])
                    P.cp('act', ab[:].rearrange("p a b -> p (a b)"), bank[:, 0:NT * 16], r=[bk], w=['ab'])
                    _chk("B0a")
                    a_v = ab[:, :, 0:8]
                    b_v = ab[:, :, 8:16]
                    g = gtab["g"]
                    tm = gtab["tmpm"]
                    tn = gtab["tmpn"]
                    P.tt('dve', tn[:], a_v, dtb[:].unsqueeze(1).to_broadcast([128, NT, 8]), ALU.add, r=['ab', 'dtb'], w=['tn'])
                    P.ts('dve', tm[:], tn[:], 0.0, ALU.max, r=['tn'], w=['tm'])
                    P.stt('dve', tn[:], tm[:], -2.0, tn[:], ALU.mult, ALU.add, r=['tm', 'tn'], w=['tn2'])
                    P.act(tn[:], tn[:], AF.Exp, r=['tn2'], w=['tn3'])
                    P.act(tn[:], tn[:], AF.Ln, bias=1.0, r=['tn3'], w=['tn4'])
                    P.tt('dve', tn[:], tn[:], tm[:], ALU.add, r=['tn4', 'tm'], w=['tn5'])
                    P.tt('dve', g[:], tn[:], negA[:].unsqueeze(1).to_broadcast([128, NT, 8]), ALU.mult, r=['tn5', 'negA'], w=['g'])
                    P.act(gtab["beta"][:], b_v, AF.Sigmoid, r=['ab'], w=['beta'])
                    _chk("B0b")
                    bank, bk = nbank()
                    gp = g[:, 0:16, :].rearrange("p a b -> p (a b)")
                    gs = g[:, 16, :]
                    P.mm(bank[:, 0:128], cst[:, C_CUMP:C_CUMP + 128], gp, r=['cst', 'g'], w=[bk])
                    _chk("B0c0")
                    P.mm(bank[:, 128:136], cst[:, C_CUMS:C_CUMS + 128], gs, r=['cst', 'g'], w=[bk])
                    _chk("B0c00")
                    P.mm(bank[:, 256:384], cst[:, C_ONEP:C_ONEP + 128], gp, r=['cst', 'g'], w=[bk])
                    P.mm(bank[:, 384:392], cst[:, C_ONES:C_ONES + 128], gs, r=['cst', 'g'], w=[bk])
                    _chk("B0c01")
                    gc = gtab["gc"]
                    gl = gtab["glast"]
                    P.cp('act', gc[:].rearrange("p a b -> p (a b)"), bank[:, 0:136], r=[bk], w=['gc'])
                    _chk("B0c02")
                    P.cp('dve', gl[:].rearrange("p a b -> p (a b)"), bank[:, 256:392], r=[bk], w=['glast'])
                    _chk("B0c1")
                    P.ts('dve', gtab["negc"][:], gc[:], -1.0, ALU.mult, r=['gc'], w=['negc'])
                    P.act(gtab["egc"][:], gc[:], AF.Exp, r=['gc'], w=['egc'])
                    P.tt('dve', tm[:], gl[:], gc[:], ALU.subtract, r=['glast', 'gc', 'tn5'], w=['tm2'])
                    P.act(gtab["kdec"][:], tm[:], AF.Exp, r=['tm2'], w=['kdec'])
                    P.act(gtab["gt"][:], gl[:], AF.Exp, r=['glast'], w=['gt'])
                    _chk("B0c")
                    for g0 in range(0, NT, 4):
                        bank, bk = nbank()
                        ng = min(4, NT - g0)
                        for ii in range(ng):
                            i = g0 + ii
                            cm = C_CUMS if i == 16 else C_CUMP
                            P.mm(bank[0:8, ii * 128:(ii + 1) * 128], g[:, i, :], cst[:, cm:cm + 128], r=['g', 'cst'], w=[bk])
                        P.cp('act', gcT[0:8, g0 * 128:(g0 + ng) * 128], bank[0:8, 0:ng * 128], r=[bk], w=['gcT'])
                    _chk("B0d")
                    P.cp('dve', grep[:], g[:, 16, :].unsqueeze(2).to_broadcast([128, 8, 128]), r=['g'], w=['grep'])
                    bank, bk = nbank()
                    for h in range(8):
                        P.mm(bank[:, h * 16:(h + 1) * 16], grep[:, h, :], cst[:, C_BCOL:C_BCOL + 16], r=['grep', 'cst'], w=[bk])
                    P.act(gtS[:].rearrange("p a b -> p (a b)"), bank[:, 0:128], AF.Exp, r=[bk], w=['gtS'])

                    P.barrier()
                    es0.close()
                    _chk("B0")
                    wh = [sbt(es, "wh%d" % i, [128, 8, 4, 128], BF16) for i in range(2)]
                    pre2 = [sbt(es, "pre%d" % i_, [128, TP + 3], F32) for i_ in range(2)]
                    pres3 = [sbt(es, "pres%d" % i_, [128, 16, 11], F32) for i_ in range(3)]
                    tail = sbt(es, "tail", [128, 48], F32)
                    acc2 = [sbt(es, "acc%d" % i_, [128, TOK], F32) for i_ in range(2)]
                    acc = acc2[1]
                    qT = sbt(es, "qT", [128, TOK], BF16)
                    kT = sbt(es, "kT", [128, TOK], BF16)
                    vT = sbt(es, "vT", [128, TOK], BF16)
                    sq = sbt(es, "sq", [128, TOK], BF16)
                    rsbs = [sbt(es, "rsb%d" % i_, [128, 512], F32) for i_ in range(2)]
                    zsg = sbt(es, "zsg", [128, 512], BF16)
                    gst = sbt(es, "gst", [48, 3, 128], F32)
                    cso = [sbt(es, "cso%d" % i, [48, 128], F32) for i in range(2)]
                    cso3 = [sbt(es, "cso3_%d" % i, [3, 128], F32) for i in range(2)]
                    HB = []
                    for q_ in range(4):
                        HB.append({nm: sbt(es, "%s_%d" % (nm, q_), [128, 128], BF16)
                                   for nm in ("k_eg", "ke", "v_tok", "qe_t", "PTm", "R2")})
                    IB = []
                    for q_ in range(2):
                        d_ = {nm: sbt(es, "%s_%d" % (nm, q_), [128, 128], BF16) for nm in ("DT", "DTs", "egb", "Nm", "NTm")}
                        d_["YB"] = [sbt(es, "YB%d_%d" % (i_, q_), [128, 384], BF16) for i_ in range(2)]
                        IB.append(d_)
                    nW2T = sbt(es, "nW2T", [128, 128], BF16)
                    nW2Tm = sbt(es, "nW2Tm", [128, 16, 128], BF16)
                    Ut = sbt(es, "Ut", [128, 128], BF16)
                    Sf = sbt(es, "Sf", [128, 128], F32)
                    Sb = sbt(es, "Sb", [128, 128], BF16)
                    S0f = sbt(es, "S0f", [128, 16, 128], F32)
                    S0b = sbt(es, "S0b", [128, 16, 128], BF16)
                    kem = [sbt(es, "kem%d" % i, [128, 128], BF16) for i in range(2)]
                    for i_ in range(2):
                        P.memset('dve', pre2[i_][:, 0:3], 0.0, w=['pre%d' % i_])
                    P.memset('dve', nW2Tm[:].rearrange("p a b -> p (a b)"), 0.0, w=['nW2Tm'])

                    w_qkvz = W["w_in"][:, 0:4096].rearrange("(k p) (j hh c) -> p k j hh c", p=128, j=4, hh=8)

                    sgc_v = sgc_d.rearrange("r (j hh c) -> r j hh c", j=3, hh=8)

                    def load_wh(h_):
                        for j in range(4):
                            P.dma('pool', wh[h_ % 2][:, :, j, :], w_qkvz[:, :, j, h_, :], w=['wh%d' % (h_ % 2)])

                    load_wh(0)
                    P.dma('sp', gst[:], sgc_v[:, :, 0, :], w=['gst'])
                    for h in range(8):
                        whb = wh[h % 2]
                        whk = 'wh%d' % (h % 2)
                        if h + 1 < 8:
                            load_wh(h + 1)
                        P.dma('sp', S0f[:], sS_d[:, h, :, :].rearrange("s k v -> k s v"), w=['S0f'])
                        P.dma('pool', S0b[:], sS_d[:, h, :, :].rearrange("s k v -> k s v"), w=['S0b'])
                        for j in range(3):
                            P.mm(PK[:, j * 48:(j + 1) * 48], gst[0:48, j, :], cst[0:48, C_ID:C_ID + 48], r=['gst', 'cst'], w=['PKs'],
                                 inc=(j == 2))
                        for j in range(3):
                            P.cp('dve', pres3[j][:, :, 0:3], PK[:, j * 48:(j + 1) * 48].rearrange("p (s r) -> p s r", r=3),
                                 r=['PKs'], w=['pres%d' % j])
                        if h + 1 < 8:
                            P.dma('sp', gst[:], sgc_v[:, :, h + 1, :], w=['gst'])

                        def PA(j):
                            pre, prk = pre2[j % 2], 'pre%d' % (j % 2)
                            pres, psk = pres3[j], 'pres%d' % j
                            for (t0, n) in GROUPS:
                                bank, bk = nbank()
                                proj_fm(bank, bk, lambda kc: whb[:, kc, j, :], xnT, xn_keys(t0, n), t0, n, [whk])
                                if t0 < TP:
                                    P.cp('act', pre[:, 3 + t0:3 + t0 + n], bank[:, 0:n], r=[bk], w=[prk])
                                else:
                                    P.cp('act', pres[:, :, 3:11], bank[:, 0:128].rearrange("p (s t) -> p s t", t=8), r=[bk], w=[psk])

                        def RA(j):
                            pre, prk = pre2[j % 2], 'pre%d' % (j % 2)
                            pres, psk = pres3[j], 'pres%d' % j
                            P.cp('dve', tail[:].rearrange("p (s r) -> p s r", r=3), pres[:, :, 8:11], r=[psk], w=['tail'])
                            P.mm(PU[0:3, 0:128], pre[:, TP:TP + 3], identf, r=[prk, 'cst'], w=['PU0'])
                            P.mm(PU[0:48, 128:256], tail[:], identf, r=['tail', 'cst'], w=['PU1'])

                        def RB(j):
                            ch = j * 8 + h
                            c3 = cso3[j % 2]
                            P.cp('dve', c3[:], PU[0:3, 0:128], r=['PU0'], w=['cso3_%d' % (j % 2)])
                            P.dma('sp', gcp_d[:, ch * 128:(ch + 1) * 128], c3[:], r=['cso3_%d' % (j % 2)], w=['gcp_d'])
                            c48 = cso[j % 2]
                            P.cp('dve', c48[:], PU[0:48, 128:256], r=['PU1'], w=['cso%d' % (j % 2)])
                            P.dma('sp', gcs_d[:, ch * 128:(ch + 1) * 128], c48[:], r=['cso%d' % (j % 2)], w=['gcs_d'])

                        def CV(j):
                            ch = j * 8 + h
                            pre, prk = pre2[j % 2], 'pre%d' % (j % 2)
                            pres, psk = pres3[j], 'pres%d' % j
                            ac, ack = acc2[j % 2], 'acc%d' % (j % 2)
                            P.ts('dve', ac[:, 0:TP], pre[:, 0:TP], cw[:, ch, 0:1], ALU.mult, r=[prk, 'cw'], w=[ack])
                            for i4 in range(1, 4):
                                P.stt('dve', ac[:, 0:TP], pre[:, i4:i4 + TP], cw[:, ch, i4:i4 + 1], ac[:, 0:TP],
                                      ALU.mult, ALU.add, r=[prk, 'cw', ack], w=[ack])
                            accs = ac[:, TP:TOK].rearrange("p (s t) -> p s t", t=8)
                            P.ts('dve', accs, pres[:, :, 0:8], cw[:, ch, 0:1], ALU.mult, r=[psk, 'cw'], w=[ack])
                            for i4 in range(1, 4):
                                P.stt('dve', accs, pres[:, :, i4:i4 + 8], cw[:, ch, i4:i4 + 1], accs,
                                      ALU.mult, ALU.add, r=[psk, 'cw', ack], w=[ack])
                            dst = (qT, kT, vT)[j]
                            dk_ = ('qT', 'kT', 'vT')[j]
                            P.act(dst[:], ac[:], AF.Silu, r=[ack], w=[dk_])
                            if j < 2:
                                P.act(sq[:], dst[:], AF.Square, r=[dk_], w=['sq'])

                        def PC(j):
                            dst = (qT, kT, vT)[j]
                            dk_ = ('qT', 'kT', 'vT')[j]
                            for gi, (t0, n) in enumerate(GROUPS):
                                rsb = rsbs[gi % 2]
                                rk_ = 'rsb%d' % (gi % 2)
                                bank, bk = nbank()
                                P.mm(bank[:, 0:n], (ones128b if j == 0 else ones1b)[:], sq[:, t0:t0 + n],
                                     r=['sq', 'ones1b', 'ones128b'], w=[bk])
                                P.act(rsb[:, 0:n], bank[:, 0:n], AF.Ln, bias=(128.0 * EPS if j == 0 else EPS),
                                      r=[bk], w=[rk_])
                                P.act(rsb[:, 0:n], rsb[:, 0:n], AF.Exp, scale=-0.5, r=[rk_], w=[rk_])
                                P.tt('pool', dst[:, t0:t0 + n], dst[:, t0:t0 + n], rsb[:, 0:n], ALU.mult, r=[dk_, rk_], w=[dk_])

                        PA(0)
                        PA(1)
                        RA(0)
                        CV(0)
                        RB(0)
                        PA(2)
                        PC(0)
                        RA(1)
                        CV(1)
                        RB(1)
                        PC(1)
                        RA(2)
                        CV(2)
                        RB(2)
                        def prep(i):
                            smp = (i == 16)
                            c0 = i * 128
                            nmk = C_NMS if smp else C_NMP
                            stk = C_STS if smp else C_STP
                            hs = i % 4
                            q_ = i % 2
                            hb = HB[hs]
                            ib = IB[q_]
                            hk = lambda nm: '%s_%d' % (nm, hs)
                            ik = lambda nm: '%s_i%d' % (nm, q_)
                            sc = lambda nm: gtab[nm][:, i, h:h + 1]
                            PKb = PK if q_ == 0 else banks[2]
                            pkk = (lambda x_: 'PK%d' % x_) if q_ == 0 else (lambda x_: 'bank2k%d' % x_)
                            PDb = banks[q_]
                            pdk = 'bank%dd' % q_
                            pt = PTs[q_]
                            ptk = 'pt%d' % q_
                            P.tr(pt[:, 0:128], kT[:, c0:c0 + 128], identb, r=['kT', 'cstb'], w=[ptk], inc=False)
                            P.tr(pt[:, 128:256], vT[:, c0:c0 + 128], identb, r=['vT', 'cstb'], w=[ptk])
                            P.act(hb["k_eg"][:], pt[:, 0:128], AF.Copy, scale=sc("egc"), r=[ptk, 'egc'], w=[hk("k_eg")])
                            P.ts('dve', hb["ke"][:], pt[:, 0:128], sc("kdec"), ALU.mult, r=[ptk, 'kdec'], w=[hk("ke")])
                            P.cp('dve', hb["v_tok"][:], pt[:, 128:256], r=[ptk], w=[hk("v_tok")])
                            P.mm(PKb[:, 0:128], kT[:, c0:c0 + 128], kT[:, c0:c0 + 128], r=['kT'], w=[pkk(0)])
                            P.mm(PKb[:, 128:256], kT[:, c0:c0 + 128], qT[:, c0:c0 + 128], r=['kT', 'qT'], w=[pkk(1)])
                            P.mm(PKb[:, 256:384], cst[0:8, C_SEL + h * 128:C_SEL + (h + 1) * 128], gcT[0:8, c0:c0 + 128],
                                 start=True, stop=False, r=['cst', 'gcT'], w=[pkk(2)])
                            P.mm(PKb[:, 256:384], identb, cstb[:, nmk:nmk + 128], start=False, stop=True, r=['cstb'], w=[pkk(2)])
                            P.mm(PKb[:, 384:512], cst[0:8, C_SEL + h * 128:C_SEL + (h + 1) * 128], gcT[0:8, c0:c0 + 128],
                                 r=['cst', 'gcT'], w=[pkk(3)])
                            P.act(ib["DT"][:], PKb[:, 256:384], AF.Exp, bias=sc("negc"), r=[pkk(2), 'negc'], w=[ik("DT")])
                            P.act(ib["egb"][:], PKb[:, 384:512], AF.Exp, r=[pkk(3)], w=[ik("egb")])
                            P.tt('pool', hb["qe_t"][:], qT[:, c0:c0 + 128], ib["egb"][:], ALU.mult, r=['qT', ik("egb")], w=[hk("qe_t")])
                            P.tt('dve', ib["DTs"][:], ib["DT"][:], cstb[:, stk:stk + 128], ALU.mult, r=[ik("DT"), 'cstb'], w=[ik("DTs")])
                            P.stt('dve', ib["Nm"][:], PKb[:, 0:128], sc("beta"), ib["DTs"][:], ALU.mult, ALU.mult,
                                  r=[pkk(0), 'beta', ik("DTs")], w=[ik("Nm")])
                            P.tt('dve', hb["PTm"][:], PKb[:, 128:256], ib["DT"][:], ALU.mult, r=[pkk(1), ik("DT")], w=[hk("PTm")])
                            P.tr(pt[:, 256:384], ib["Nm"][:], identb, r=[ik("Nm"), 'cstb'], w=[ptk])
                            P.cp('act', ib["NTm"][:], pt[:, 256:384], r=[ptk], w=[ik("NTm")])
                            YB = ib["YB"]
                            yk = lambda nm, c_: '%s%d_i%d' % (nm, c_, q_)
                            P.tt('pool', YB[0][:, 256:384], identb, ib["Nm"][:], ALU.subtract, r=['cstb', ik("Nm")], w=[yk('R', 0)])
                            nst = 2 if smp else 6
                            P.mm(PDb[:, 0:128], ib["Nm"][:], ib["NTm"][:], r=[ik("NTm"), ik("Nm")], w=[pdk + 'a'], inc=False)
                            P.mm(PDb[:, 128:256], ib["NTm"][:], ib["Nm"][:], r=[ik("NTm"), ik("Nm")], w=[pdk + 'a'])
                            P.cp('act', YB[0][:, 0:256], PDb[:, 0:256], r=[pdk + 'a'], w=[yk('Y', 0)])
                            cur = 0
                            for kst in range(1, nst + 1):
                                nx = 1 - cur
                                needY = kst < nst - 1
                                needYT = kst < nst
                                last = (kst == nst)
                                if needYT:
                                    P.mm(PDb[:, 0:128], YB[cur][:, 128:256], YB[cur][:, 0:128], r=[yk('Y', cur)], w=[pdk + 'a'], inc=False)
                                if needY:
                                    P.mm(PDb[:, 128:384], YB[cur][:, 0:128], YB[cur][:, 128:384],
                                         r=[yk('Y', cur), yk('R', cur)], w=[pdk + 'a'])
                                else:
                                    P.mm(PDb[:, 256:384], YB[cur][:, 0:128], YB[cur][:, 256:384],
                                         r=[yk('Y', cur), yk('R', cur)], w=[pdk + 'a'])
                                if needY:
                                    P.cp('act', YB[nx][:, 0:256], PDb[:, 0:256], r=[pdk + 'a'], w=[yk('Y', nx)])
                                elif needYT:
                                    P.cp('act', YB[nx][:, 0:128], PDb[:, 0:128], r=[pdk + 'a'], w=[yk('Y', nx)])
                                if last:
                                    P.tt('dve', hb["R2"][:], YB[cur][:, 256:384], PDb[:, 256:384], ALU.add,
                                         r=[pdk + 'a', yk('R', cur)], w=[hk("R2")])
                                else:
                                    P.tt('dve', YB[nx][:, 256:384], YB[cur][:, 256:384], PDb[:, 256:384], ALU.add,
                                         r=[pdk + 'a', yk('R', cur)], w=[yk('R', nx)])
                                cur = nx

                        def rec(i):
                            smp = (i == 16)
                            c0 = i * 128
                            hs = i % 4
                            hb = HB[hs]
                            hk = lambda nm: '%s_%d' % (nm, hs)
                            sc = lambda nm: gtab[nm][:, i, h:h + 1]
                            R2 = hb["R2"][:]
                            R2k = hk("R2")
                            k_eg, ke, v_tok, qe_t, PTm = hb["k_eg"], hb["ke"], hb["v_tok"], hb["qe_t"], hb["PTm"]
                            P.mm(PU[:, 0:128], k_eg[:], R2, r=[hk('k_eg'), R2k], w=['PU0'])
                            P.act(nW2T[:], PU[:, 0:128], AF.Copy, scale=-1.0, r=['PU0'], w=['nW2T'])
                            if not smp:
                                P.mm(PU[:, 128:256], R2, v_tok[:], start=True, stop=(i == 0), r=[R2k, hk('v_tok')], w=['PU1'], inc=True)
                                if i > 0:
                                    P.mm(PU[:, 128:256], nW2T[:], Sb[:], start=False, stop=True, r=['nW2T', 'Sb'], w=['PU1'])
                            else:
                                for s in range(16):
                                    P.cp('dve', nW2Tm[:, s, 8 * s:8 * s + 8], nW2T[:, 8 * s:8 * s + 8], r=['nW2T'], w=['nW2Tm'])
                                P.mm(PU[:, 128:256], R2, v_tok[:], start=True, stop=False, r=[R2k, hk('v_tok')], w=['PU1'], inc=False)
                                for s in range(16):
                                    P.mm(PU[:, 128:256], nW2Tm[:, s, :], S0b[:, s, :], start=False, stop=(s == 15),
                                         r=['nW2Tm', 'S0b'], w=['PU1'])
                            P.act(Ut[:], PU[:, 128:256], AF.Copy, scale=sc("beta"), r=['PU1', 'beta'], w=['Ut'])
                            if not smp:
                                if i > 0:
                                    P.mm(PU[:, 256:384], Sb[:], qe_t[:], start=True, stop=False, r=['Sb', hk('qe_t')], w=['PU2'], inc=False)
                                P.mm(PU[:, 256:384], Ut[:], PTm[:], start=(i == 0), stop=True, r=['Ut', hk('PTm')], w=['PU2'])
                            else:
                                P.mm(PU[:, 256:384], Ut[:], PTm[:], start=True, stop=False, r=['Ut', hk('PTm')], w=['PU2'], inc=False)
                                for s in range(16):
                                    P.mm(PU[:, 256 + 8 * s:256 + 8 * s + 8], S0b[:, s, :], qe_t[:, 8 * s:8 * s + 8],
                                         start=False, stop=True, r=['S0b', hk('qe_t')], w=['PU2'], inc=(s == 15))
                            P.cp('act', acc[:, c0:c0 + 128], PU[:, 256:384], r=['PU2'], w=['acc1'])
                            if not smp:
                                P.mm(PU[:, 384:512], ke[:], Ut[:], r=[hk('ke'), 'Ut'], w=['PU3'])
                                if i == 0:
                                    P.cp('dve', Sf[:], PU[:, 384:512], r=['PU3'], w=['Sf'])
                                else:
                                    P.stt('dve', Sf[:], Sf[:], sc("gt"), PU[:, 384:512], ALU.mult, ALU.add, r=['PU3', 'gt', 'Sf'], w=['Sf'])
                                if i < 15:
                                    P.cp('act', Sb[:], Sf[:], r=['Sf'], w=['Sb'])
                                else:
                                    P.dma('sp', Sp_d[h, :, :], Sf[:], r=['Sf'], w=['Sp_d'])
                            else:
                                P.tt('pool', S0f[:], S0f[:], gtS[:, h, :].unsqueeze(2).to_broadcast([128, 16, 128]), ALU.mult,
                                     r=['S0f', 'gtS'], w=['S0f'])
                                for s4 in range(4):
                                    bank, bk = banks[3], 'bank3'
                                    for s_ in range(4):
                                        s = s4 * 4 + s_
                                        km = kem[s % 2]
                                        kmk = 'kem%d' % (s % 2)
                                        P.ts('dve', km[:], ke[:], cst[:, C_BCOL + s:C_BCOL + s + 1], ALU.mult, r=[hk('ke'), 'cst'], w=[kmk])
                                        P.mm(bank[:, s_ * 128:(s_ + 1) * 128], km[:], Ut[:], r=[kmk, 'Ut'], w=[bk])
                                    P.tt('dve', S0f[:, s4 * 4:(s4 + 1) * 4, :], S0f[:, s4 * 4:(s4 + 1) * 4, :],
                                         bank.rearrange("p (s v) -> p s v", s=4), ALU.add, r=[bk, 'S0f'], w=['S0f'])
                                P.dma('sp', Ss_d[:, h, :, :].rearrange("s k v -> k s v"), S0f[:], r=['S0f'], w=['Ss_d'])

                        P.rec_begin()
                        prep(0)
                        prep(1)
                        for j2 in range(0, NT, 2):
                            for i_ in (j2 + 2, j2 + 3):
                                if i_ < NT:
                                    prep(i_)
                            rec(j2)
                            if j2 + 1 < NT:
                                rec(j2 + 1)
                        P.emit_scheduled(P.rec_end())
                        P.act(sq[:], acc[:], AF.Square, r=['acc1'], w=['sq'])
                        for gi, (t0, n) in enumerate(GROUPS):
                            rsb = rsbs[gi % 2]
                            rk_ = 'rsb%d' % (gi % 2)
                            bank, bk = nbank()
                            P.mm(bank[:, 0:n], onesi128b[:], sq[:, t0:t0 + n], r=['sq', 'onesi128b'], w=[bk])
                            P.act(rsb[:, 0:n], bank[:, 0:n], AF.Ln, bias=EPS, r=[bk], w=[rk_])
                            P.act(rsb[:, 0:n], rsb[:, 0:n], AF.Exp, scale=-0.5, r=[rk_], w=[rk_])
                            P.tt('dve', rsb[:, 0:n], rsb[:, 0:n], acc[:, t0:t0 + n], ALU.mult, r=[rk_, 'acc1'], w=[rk_])
                            bank, bk = nbank()
                            proj_fm(bank, bk, lambda kc: whb[:, kc, 3, :], xnT, xn_keys(t0, n), t0, n, [whk])
                            P.act(zsg[:, 0:n], bank[:, 0:n], AF.Silu, r=[bk], w=['zsg'])
                            P.stt('dve', oTn[:, h, t0:t0 + n], rsb[:, 0:n], won[:, 0:1], zsg[:, 0:n], ALU.mult, ALU.mult,
                                  r=[rk_, 'won', 'zsg'], w=[('oTn', h)])
                    P.barrier()

                buT = sbt(esB, "buT", [128, 8, TOK], BF16)
                with ExitStack() as es:
                    wc = [sbt(es, "wc%d" % i, [128, 8, 3, 128], BF16) for i in range(2)]
                    cx = sbt(es, "cx", [128, TP + 2], F32)
                    cxs = sbt(es, "cxs", [128, 16, 10], F32)
                    cgf = [sbt(es, "cgf%d" % i, [128, 512], F32) for i in range(2)]
                    bgf = sbt(es, "bgf", [128, TOK], F32)
                    u = sbt(es, "u", [128, TOK], F32)
                    sst = sbt(es, "sst", [32, D], F32)
                    scw = sbt(es, "scw", [128, 8, 3], F32)
                    tail2 = sbt(es, "tail2", [128, 32], F32)
                    so2 = [sbt(es, "so2_%d" % i, [2, 128], F32) for i in range(2)]
                    so32 = [sbt(es, "so32_%d" % i, [32, 128], F32) for i in range(2)]
                    P.dma('sp', sst[:], ssc_d, w=['sst'])
                    for i3 in range(3):
                        P.dma('sp', scw[:, :, i3], W["sc_conv_w"][i3].rearrange("(c p) -> p c", p=128), w=['scw'], slow=True)
                    P.memset('dve', cx[:, 0:2], 0.0, w=['cx'])
                    w_sc = W["w_in"][:, 4112:7184].rearrange("(k p) (j cc n) -> p k j cc n", p=128, j=3, cc=8)
                    def load_wc(c_):
                        for j in range(3):
                            P.dma('pool', wc[c_ % 2][:, :, j, :], w_sc[:, :, j, c_, :], w=['wc%d' % (c_ % 2)])

                    load_wc(0)
                    for c in range(8):
                        wcb = wc[c % 2]
                        wck = 'wc%d' % (c % 2)
                        if c + 1 < 8:
                            load_wc(c + 1)
                        bank, bk = nbank()
                        P.mm(bank[:, 0:32], sst[0:32, c * 128:(c + 1) * 128], cst[0:32, C_ID:C_ID + 32], r=['sst', 'cst'], w=[bk])
                        P.cp('dve', cxs[:, :, 0:2], bank[:, 0:32].rearrange("p (s r) -> p s r", r=2), r=[bk], w=['cxs'])
                        for gi, (t0, n) in enumerate(GROUPS):
                            cg_ = cgf[gi % 2]
                            cgk = 'cgf%d' % (gi % 2)
                            bank, bk = nbank()
                            proj_fm(bank, bk, lambda kc: wcb[:, kc, 1, :], xnT, xn_keys(t0, n), t0, n, [wck])
                            P.cp('act', cg_[:, 0:n], bank[:, 0:n], r=[bk], w=[cgk])
                            bank, bk = nbank()
                            proj_fm(bank, bk, lambda kc: wcb[:, kc, 2, :], xnT, xn_keys(t0, n), t0, n, [wck])
                            if t0 < TP:
                                P.tt('dve', cx[:, 2 + t0:2 + t0 + n], bank[:, 0:n], cg_[:, 0:n], ALU.mult, r=[bk, cgk], w=['cx'])
                            else:
                                P.tt('dve', cxs[:, :, 2:10], bank[:, 0:128].rearrange("p (s t) -> p s t", t=8),
                                     cg_[:, 0:128].rearrange("p (s t) -> p s t", t=8), ALU.mult, r=[bk, cgk], w=['cxs'])
                            bank, bk = nbank()
                            proj_fm(bank, bk, lambda kc: wcb[:, kc, 0, :], xnT, xn_keys(t0, n), t0, n, [wck])
                            P.cp('act', bgf[:, t0:t0 + n], bank[:, 0:n], r=[bk], w=['bgf'])
                        bank, bk = nbank()
                        P.mm(bank[0:2, 0:128], cx[:, TP:TP + 2], identf, r=['cx', 'cst'], w=[bk])
                        P.cp('dve', so2[c % 2][:], bank[0:2, 0:128], r=[bk], w=['so2_%d' % (c % 2)])
                        P.dma('sp', scp_d[:, c * 128:(c + 1) * 128], so2[c % 2][:], r=['so2_%d' % (c % 2)], w=['scp_d'])
                        P.cp('dve', tail2[:].rearrange("p (s r) -> p s r", r=2), cxs[:, :, 8:10], r=['cxs'], w=['tail2'])
                        bank, bk = nbank()
                        P.mm(bank[0:32, 0:128], tail2[:], identf, r=['tail2', 'cst'], w=[bk])
                        P.cp('dve', so32[c % 2][:], bank[0:32, 0:128], r=[bk], w=['so32_%d' % (c % 2)])
                        P.dma('sp', scs_d[:, c * 128:(c + 1) * 128], so32[c % 2][:], r=['so32_%d' % (c % 2)], w=['scs_d'])
                        P.ts('dve', u[:, 0:TP], cx[:, 0:TP], scw[:, c, 0:1], ALU.mult, r=['cx', 'scw'], w=['u'])
                        for i3 in range(1, 3):
                            P.stt('dve', u[:, 0:TP], cx[:, i3:i3 + TP], scw[:, c, i3:i3 + 1], u[:, 0:TP], ALU.mult, ALU.add,
                                  r=['cx', 'scw', 'u'], w=['u'])
                        us = u[:, TP:TOK].rearrange("p (s t) -> p s t", t=8)
                        P.ts('dve', us, cxs[:, :, 0:8], scw[:, c, 0:1], ALU.mult, r=['cxs', 'scw'], w=['us'])
                        for i3 in range(1, 3):
                            P.stt('dve', us, cxs[:, :, i3:i3 + 8], scw[:, c, i3:i3 + 1], us, ALU.mult, ALU.add,
                                  r=['cxs', 'scw', 'us'], w=['us'])
                        P.tt('dve', buT[:, c, :], bgf[:], u[:], ALU.mult, r=['bgf', 'u', 'us'], w=[('buT', c)])
                    P.barrier()

                with ExitStack() as esM:
                    mT = sbt(esM, "mT", [128, 8, TOK], BF16)
                    Wmix = sbt(esM, "Wmix", [128, 8, D], BF16)
                    with ExitStack() as es:
                        w3 = [sbt(es, "w3_%d" % i, [128, 8, 4, 128], BF16) for i in range(2)]
                        sga = [sbt(es, "sga%d" % i, [128, 512], F32) for i in range(4)]
                        m1 = [sbt(es, "m1_%d" % i, [128, 512], F32) for i in range(4)]
                        OT_ALL = [('oTn', hh) for hh in range(8)]
                        BU_ALL = [('buT', cc) for cc in range(8)]
                        def load_w3(c_):
                            wb_ = w3[c_ % 2]
                            k_ = 'w3_%d' % (c_ % 2)
                            cs_ = slice(c_ * 128, (c_ + 1) * 128)
                            P.dma('pool', wb_[:, :, 2, :], W["w_in"][:, 7184 + c_ * 128:7184 + (c_ + 1) * 128].rearrange("(k p) n -> p k n", p=128), w=[k_])
                            P.dma('pool', wb_[:, :, 0, :], W["w_gdn_o"][:, cs_].rearrange("(k p) n -> p k n", p=128), w=[k_])
                            P.dma('pool', wb_[:, :, 3, :], W["w_in"][:, 8208 + c_ * 128:8208 + (c_ + 1) * 128].rearrange("(k p) n -> p k n", p=128), w=[k_])
                            P.dma('pool', wb_[:, :, 1, :], W["w_sc_o"][:, cs_].rearrange("(k p) n -> p k n", p=128), w=[k_])

                        load_w3(0)
                        load_w3(1)
                        P.dma('pool', Wmix[:], W["w_mix_out"].rearrange("(k p) n -> p k n", p=128), w=['Wmix'])
                        for c in range(8):
                            wb3 = w3[c % 2]
                            w3k = 'w3_%d' % (c % 2)
                            if c >= 1 and c + 1 < 8:
                                load_w3(c + 1)
                            for gi, (t0, n) in enumerate(GROUPS):
                                for br in range(2):
                                    src_t, src_k = (oTn, OT_ALL) if br == 0 else (buT, BU_ALL)
                                    bank, bk = nbank()
                                    proj_fm(bank, bk, lambda kc: wb3[:, kc, 2 + br, :], xnT, xn_keys(t0, n), t0, n, [w3k])
                                    bi_ = 2 * (gi % 2) + br
                                    sg = sga[bi_]
                                    sgk = 'sga%d' % bi_
                                    P.act(sg[:, 0:n], bank[:, 0:n], AF.Sigmoid, r=[bk], w=[sgk])
                                    bank, bk = nbank()
                                    proj_fm(bank, bk, lambda kc: wb3[:, kc, br, :], src_t, src_k, t0, n, [w3k])
                                    P.tt('dve', m1[bi_][:, 0:n], bank[:, 0:n], sg[:, 0:n], ALU.mult, r=[bk, sgk], w=['m1_%d' % bi_])
                                g0_ = 2 * (gi % 2)
                                P.tt('pool', mT[:, c, t0:t0 + n], m1[g0_][:, 0:n], m1[g0_ + 1][:, 0:n], ALU.add,
                                     r=['m1_%d' % g0_, 'm1_%d' % (g0_ + 1)], w=[('mT', c)])
                        P.barrier()

                    with ExitStack() as es:
                        alloc_norm(es, "b")
                        xo = [sbt(es, "xo%d" % i, [128, D], F32) for i in range(2)]
                        MT_ALL = [('mT', cc) for cc in range(8)]
                        for i in range(NT):
                            xt, xk = load_x(i, x_d)
                            xob = xo[i % 2]
                            xok = 'xo%d' % (i % 2)
                            for half in range(2):
                                bank, bk = nbank()
                                for kc in range(8):
                                    P.mm(bank, mT[:, kc, i * 128:(i + 1) * 128], Wmix[:, kc, half * 512:(half + 1) * 512],
                                         start=(kc == 0), stop=(kc == 7), r=MT_ALL + ['Wmix'], w=[bk])
                                P.tt('dve', xob[:, half * 512:(half + 1) * 512], xt[:, half * 512:(half + 1) * 512], bank, ALU.add,
                                     r=[bk, xk], w=[xok])
                            P.dma('pool', xres_d[i * 128:(i + 1) * 128, :], xob[:], r=[xok], w=[('xres', i)])
                        P.barrier()
            esX.close()

            with ExitStack() as es:
                alloc_norm(es, "c", False)
                Wxq = sbt(es, "Wxq", [128, 8, D], BF16)
                Wxo = sbt(es, "Wxo", [128, 8, D], BF16)
                xcTs = [sbt(es, "xcT%d" % i, [128, 8, 512], BF16) for i in range(2)]
                hqTs = [sbt(es, "hqT%d" % i, [128, 8, 512], BF16) for i in range(2)]
                hqm = sbt(es, "hqm", [128, 8, 16, 128], BF16)
                xg = [sbt(es, "xg%d" % i, [128, D], F32) for i in range(8)]
                efs = [sbt(es, "ef%d" % i, [128, 4, 256], F32) for i in range(2)]
                pbs = [sbt(es, "pb%d" % i, [128, 4, 256], BF16) for i in range(2)]
                pTbs = [sbt(es, "pTb%d" % i, [128, 8, 128], BF16) for i in range(2)]
                ctxTs = [sbt(es, "ctxT%d" % i, [128, 8, 128], BF16) for i in range(2)]
                smxs = [sbt(es, "smx%d" % i, [128, 16], F32) for i in range(2)]

                def SR(par_, hh_):
                    if par_ == 0:
                        return PS2[:, hh_ * 256:(hh_ + 1) * 256], 'PS2'
                    t_ = PK if hh_ < 2 else PU
                    return t_[:, (hh_ % 2) * 256:(hh_ % 2 + 1) * 256], ('PKx' if hh_ < 2 else 'PUx')

                def CR(par_, c_):
                    if par_ == 0:
                        return PS2[:, c_ * 128:(c_ + 1) * 128], 'PS2'
                    t_ = PK if c_ < 4 else PU
                    return t_[:, (c_ % 4) * 128:(c_ % 4 + 1) * 128], ('PKx' if c_ < 4 else 'PUx')
                ckfs = [sbt(es, "ckf%d" % i, [128, 2, D], F32) for i in range(1)]
                ckbs = [sbt(es, "ckb%d" % i, [128, 2, D], BF16) for i in range(2)]
                cvb = [sbt(es, "cvb%d" % i, [128, 2, D], BF16) for i in range(2)]
                KTs = [sbt(es, "KTs%d" % i, [128, 8, 256], BF16) for i in range(2)]
                P.dma('pool', Wxq[:], W["w_xq"].rearrange("(k p) n -> p k n", p=128), w=['Wxq'])
                P.dma('pool', Wxo[:], W["w_xo"].rearrange("(k p) n -> p k n", p=128), w=['Wxo'])
                P.memset('dve', hqm[:].rearrange("p a b c -> p (a b c)"), 0.0, w=['hqm'])
                tile_groups = [[0, 1, 2, 3], [4, 5, 6, 7], [8, 9, 10, 11], [12, 13, 14, 15], [16]]
                P.rec_begin()
                for gi_, tg in enumerate(tile_groups):
                    n = 128 * len(tg)
                    xcT = xcTs[gi_ % 2]
                    hqT = hqTs[gi_ % 2]
                    xck = 'xcT%d' % (gi_ % 2)
                    hqk = 'hqT%d' % (gi_ % 2)
                    xo_ = 4 * (gi_ % 2)
                    for sl, i in enumerate(tg):
                        P.dma('sp', xg[xo_ + sl][:], xres_d[i * 128:(i + 1) * 128, :], r=[('xres', i)], w=['xg%d' % (xo_ + sl)])
                        norm_tile(xg[xo_ + sl][:], ['xg%d' % (xo_ + sl)], wn_x, "norm_x_w", xcT[:, :, sl * 128:(sl + 1) * 128], [xck])
                    for c in range(8):
                        bank, bk = nbank2()
                        for kc in range(8):
                            P.mm(bank[:, 0:n], Wxq[:, kc, c * 128:(c + 1) * 128], xcT[:, kc, 0:n], start=(kc == 0), stop=(kc == 7),
                                 r=['Wxq', xck], w=[bk])
                        P.cp('act', hqT[:, c, 0:n], bank[:, 0:n], r=[bk], w=[hqk])
                    for sl, i in enumerate(tg):
                        smp = (i == 16)
                        t0 = sl * 128
                        par = 0 if smp else (i % 2)
                        ef, pb, pTb, ctxT, smx = efs[par], pbs[par], pTbs[par], ctxTs[par], smxs[par]
                        efk, pbk, pTk, ctk, sk_ = 'ef%d' % par, 'pb%d' % par, 'pTb%d' % par, 'ctxT%d' % par, 'smx%d' % par
                        if not smp:
                            for hh in range(4):
                                sr_, srk = SR(par, hh)
                                for dc in range(2):
                                    P.mm(sr_, hqT[:, 2 * hh + dc, t0:t0 + 128], KT[:, 2 * hh + dc, :],
                                         start=(dc == 0), stop=(dc == 1), r=[hqk, 'KT'], w=[srk])
                        else:
                            for s in range(16):
                                P.cp('dve', hqm[:, :, s, 8 * s:8 * s + 8], hqT[:, :, 8 * s:8 * s + 8], r=[hqk], w=['hqm'])
                            for s in range(16):
                                cf = ckfs[0]
                                cfk = 'ckf0'
                                cb = ckbs[s % 2]
                                cbk = 'ckb%d' % (s % 2)
                                P.dma('sp', cf[:], ck_d[s].rearrange("(sc p) n -> p sc n", p=128), w=[cfk])
                                P.cp('pool', cb[:, 0, :], cf[:, 0, :], r=[cfk], w=[cbk + 'a'])
                                P.cp('dve', cb[:, 1, :], cf[:, 1, :], r=[cfk], w=[cbk + 'b'])
                                kts = KTs[s % 2]
                                ktk = 'KTs%d' % (s % 2)
                                for half in range(2):
                                    pt, ptk = npt()
                                    ptv = pt[:].rearrange("p (c s) -> p c s", c=4)
                                    for cc in range(4):
                                        c = half * 4 + cc
                                        for scn in range(2):
                                            P.tr(ptv[:, cc, scn * 128:(scn + 1) * 128], cb[:, scn, c * 128:(c + 1) * 128], identb,
                                                 r=[cbk + 'a', cbk + 'b', 'cstb'], w=[ptk], inc=(cc == 3 and scn == 1))
                                    P.cp('act' if half == 0 else 'dve', kts[:, half * 4:(half + 1) * 4, :], ptv, r=[ptk], w=[ktk])
                                for hh in range(4):
                                    for dc in range(2):
                                        P.mm(PS2[:, hh * 256:(hh + 1) * 256], hqm[:, 2 * hh + dc, s, :], kts[:, 2 * hh + dc, :],
                                             start=(s == 0 and dc == 0 and hh % 2 == 0), stop=(s == 15 and dc == 1),
                                             r=['hqm', ktk], w=['PS2'], inc=(hh == 3 and dc == 1), skip=True)
                        if par == 0:
                            P.add('dve', lambda o_=smx[:, 0:4]: nc.vector.tensor_reduce(out=o_, in_=PS2[:].rearrange("p (h s) -> p h s", h=4),
                                                                                        axis=AX.X, op=ALU.max), r=['PS2'], w=[sk_ + 'm'])
                        else:
                            P.add('dve', lambda o_=smx[:, 0:2]: nc.vector.tensor_reduce(out=o_, in_=PK[:].rearrange("p (h s) -> p h s", h=2),
                                                                                        axis=AX.X, op=ALU.max), r=['PKx'], w=[sk_ + 'm'])
                            P.add('dve', lambda o_=smx[:, 2:4]: nc.vector.tensor_reduce(out=o_, in_=PU[:].rearrange("p (h s) -> p h s", h=2),
                                                                                        axis=AX.X, op=ALU.max), r=['PUx'], w=[sk_ + 'm2'])
                        P.ts('dve', smx[:, 4:8], smx[:, 0:4], -1.0 / 16.0, ALU.mult, r=[sk_ + 'm', sk_ + 'm2'], w=[sk_ + 'n'])
                        for hh in range(4):
                            sr_, srk = SR(par, hh)
                            P.act(ef[:, hh, :], sr_, AF.Exp, bias=smx[:, 4 + hh:5 + hh], scale=1.0 / 16.0,
                                  accum=smx[:, 8 + hh:9 + hh], r=[srk, sk_ + 'n'], w=[efk, sk_ + 's'])
                        P.recip(smx[:, 12:16], smx[:, 8:12], r=[sk_ + 's'], w=[sk_ + 'r'])
                        P.tt('dve', pb[:], ef[:], smx[:, 12:16].unsqueeze(2).to_broadcast([128, 4, 256]), ALU.mult,
                             r=[efk, sk_ + 'r'], w=[pbk])
                        pt, ptk = npt()
                        ptv = pt[:].rearrange("p (k c) -> p k c", k=8)
                        for hh in range(4):
                            for scn in range(2):
                                P.tr(ptv[:, hh * 2 + scn, :], pb[:, hh, scn * 128:(scn + 1) * 128], identb, r=[pbk, 'cstb'], w=[ptk],
                                     inc=(hh == 3 and scn == 1))
                        P.cp('act', pTb[:], ptv, r=[ptk], w=[pTk])
                        PSc = PS2[:].rearrange("p (c t) -> p c t", c=8)
                        if not smp:
                            for hh in range(4):
                                for dc in range(2):
                                    cr_, crk = CR(par, 2 * hh + dc)
                                    for scn in range(2):
                                        P.mm(cr_, Vb[:, scn, hh * 256 + dc * 128:hh * 256 + (dc + 1) * 128],
                                             pTb[:, hh * 2 + scn, :], start=(scn == 0), stop=(scn == 1), r=['Vb', pTk], w=[crk],
                                             inc=(hh == 3 and dc == 1 and scn == 1))
                        else:
                            for s in range(16):
                                cvt = cvb[s % 2]
                                cvk = 'cvb%d' % (s % 2)
                                P.dma('pool', cvt[:], cv_d[s].rearrange("(sc p) n -> p sc n", p=128), w=[cvk])
                                for hh in range(4):
                                    for dc in range(2):
                                        for scn in range(2):
                                            P.mm(PSc[:, 2 * hh + dc, 8 * s:8 * s + 8],
                                                 cvt[:, scn, hh * 256 + dc * 128:hh * 256 + (dc + 1) * 128],
                                                 pTb[:, hh * 2 + scn, 8 * s:8 * s + 8], start=(scn == 0), stop=(scn == 1),
                                                 r=[cvk, pTk], w=['PS2'], inc=(hh == 3 and dc == 1 and scn == 1))
                        if par == 0:
                            P.cp('act', ctxT[:], PSc, r=['PS2'], w=[ctk])
                        else:
                            P.cp('act', ctxT[:, 0:4, :], PK[:].rearrange("p (c t) -> p c t", c=4), r=['PKx'], w=[ctk])
                            P.cp('act', ctxT[:, 4:8, :], PU[:].rearrange("p (c t) -> p c t", c=4), r=['PUx'], w=[ctk + 'b'])
                        xgb = xg[xo_ + sl]
                        xgk = 'xg%d' % (xo_ + sl)
                        for half in range(2):
                            bank, bk = nbank2()
                            for kc in range(8):
                                P.mm(bank, ctxT[:, kc, :], Wxo[:, kc, half * 512:(half + 1) * 512], start=(kc == 0), stop=(kc == 7),
                                     r=[ctk, ctk + 'b', 'Wxo'], w=[bk])
                            P.tt('dve', xgb[:, half * 512:(half + 1) * 512], xgb[:, half * 512:(half + 1) * 512], bank, ALU.add,
                                 r=[bk, xgk], w=[xgk])
                        P.dma('sp', xres_d[i * 128:(i + 1) * 128, :], xgb[:], r=[xgk, ('xres', i)], w=[('xres2', i)])
                P.emit_scheduled(P.rec_end())
                P.barrier()

            with ExitStack() as es:
                alloc_norm(es, "d", False)
                Wup = sbt(es, "Wup", [128, 8, 4 * D], BF16)
                Wdn = sbt(es, "Wdn", [128, 32, D], BF16)
                hT = sbt(es, "hT", [128, 32, 256], BF16)
                rl = [sbt(es, "rl%d" % i, [128, 512], F32) for i in range(2)]
                xg2 = [sbt(es, "xd%d" % i, [128, D], F32) for i in range(4)]
                xdT = [sbt(es, "xdT%d" % i, [128, 8, 256], BF16) for i in range(2)]
                wf_bc = sbt(es, "wf_bc", [128, D], F32)
                P.dma('sp', wf_bc[:], W["norm_f_w"].partition_broadcast(128), w=['wf_bc'])
                for q4 in range(4):
                    P.dma('pool', Wup[:, :, q4 * D:(q4 + 1) * D], W["w_mlp_up"][:, q4 * D:(q4 + 1) * D].rearrange("(k p) n -> p k n", p=128),
                          w=['Wup%d' % q4])
                for q4 in range(4):
                    P.dma('pool', Wdn[:, q4 * 8:(q4 + 1) * 8, :], W["w_mlp_down"][q4 * D:(q4 + 1) * D, :].rearrange("(k p) n -> p k n", p=128),
                          w=['Wdn%d' % q4])
                pairs = [[2 * p_, 2 * p_ + 1] for p_ in range(8)] + [[16]]
                HT_ALL = [('hT', f2) for f2 in range(16)]

                def de_norm(pi):
                    tl = pairs[pi]
                    xd = xdT[pi % 2]
                    xdk = 'xdT%d' % (pi % 2)
                    for t_, i in enumerate(tl):
                        xb = xg2[i % 4]
                        xbk = 'xd%d' % (i % 4)
                        P.dma('sp', xb[:], xres_d[i * 128:(i + 1) * 128, :], r=[('xres2', i)], w=[xbk])
                        norm_tile(xb[:], [xbk], wn_mlp, "norm_mlp_w", xd[:, :, t_ * 128:(t_ + 1) * 128], [xdk])

                def de_up(pi):
                    tl = pairs[pi]
                    n = 128 * len(tl)
                    xd = xdT[pi % 2]
                    xdk = 'xdT%d' % (pi % 2)
                    for f2 in range(16):
                        bank, bk = nbank()
                        for fc_ in range(2):
                            fc = f2 * 2 + fc_
                            for kc in range(8):
                                P.mm(bank[:, fc_ * 256:fc_ * 256 + n], Wup[:, kc, fc * 128:(fc + 1) * 128], xd[:, kc, 0:n],
                                     start=(kc == 0), stop=(kc == 7), r=['Wup%d' % (fc // 8), xdk], w=[bk], inc=(kc == 7 and fc_ == 1))
                        rlb = rl[f2 % 2]
                        rlk = 'rl%d' % (f2 % 2)
                        bv = bank.rearrange("p (c t) -> p c t", c=2)[:, :, 0:n]
                        rv = rlb[:].rearrange("p (c t) -> p c t", c=2)[:, :, 0:n]
                        P.act(rv, bv, AF.Relu, r=[bk], w=[rlk])
                        P.tt('pool', hT[:, f2 * 2:(f2 + 1) * 2, 0:n], rv, rv, ALU.mult, r=[rlk], w=[('hT', f2)])

                def de_down(pi):
                    tl = pairs[pi]
                    for t_, i in enumerate(tl):
                        xb = xg2[i % 4]
                        xbk = 'xd%d' % (i % 4)
                        for half in range(2):
                            bank, bk = nbank()
                            for fc in range(32):
                                P.mm(bank, hT[:, fc, t_ * 128:(t_ + 1) * 128], Wdn[:, fc, half * 512:(half + 1) * 512],
                                     start=(fc == 0), stop=(fc == 31), r=HT_ALL + ['Wdn%d' % (fc // 8)], w=[bk])
                            P.tt('dve', xb[:, half * 512:(half + 1) * 512], xb[:, half * 512:(half + 1) * 512], bank, ALU.add,
                                 r=[bk, xbk], w=[xbk])
                        si = nstate['i'] % 2
                        nstate['i'] += 1
                        rstd, rsk = rstd_of(xb[:], [xbk], si)
                        P.stt('dve', xb[:], xb[:], rstd, wf_bc[:], ALU.mult, ALU.mult, r=[xbk, rsk, 'wf_bc'], w=[xbk])
                        P.dma('sp', y_d[i * 128:(i + 1) * 128, :], xb[:], r=[xbk], w=['y_d'])

                de_norm(0)
                for pi in range(len(pairs)):
                    de_up(pi)
                    if pi + 1 < len(pairs):
                        de_norm(pi + 1)
                    de_down(pi)
                P.barrier()
        except _StopBuild:
            P.barrier()
            outer.pop_all()
        P.barrier()
    return nc, P


_CACHE = {}


def kernel(**inputs):
    f32 = lambda a: np.ascontiguousarray(np.asarray(a, dtype=np.float32))
    xp = f32(inputs["x_prompt"])
    xs = f32(inputs["x_sample"])
    memp = f32(inputs["mem_prompt"])
    ck = f32(inputs["cache_mem_k"])[0]
    cv = f32(inputs["cache_mem_v"])[0]
    sgc = f32(inputs["state_gdn_conv"])[0]
    sS = f32(inputs["state_gdn"])[0]
    ssc = f32(inputs["state_sc_conv"])[0]
    wnames = ["norm_mix_w", "w_in", "gdn_conv_w", "gdn_A_log", "gdn_dt_bias", "gdn_out_norm_w", "w_gdn_o", "sc_conv_w",
              "w_sc_o", "w_mix_out", "norm_x_w", "mem_norm_w", "w_xq", "w_xkv", "w_xo", "norm_mlp_w", "w_mlp_up",
              "w_mlp_down"]
    wd = {nm: f32(inputs[nm])[0] for nm in wnames}
    wd["norm_f_w"] = f32(inputs["norm_f_w"])
    cst = _consts()
    if "nc" not in _CACHE:
        _CACHE["nc"] = build()
    nc, P = _CACHE["nc"]
    in_maps = []
    for b in range(NCORES):
        sl = slice(16 * b, 16 * (b + 1))
        m = dict(wd)
        m["x_all"] = np.ascontiguousarray(np.concatenate([xp[b], xs[sl].reshape(128, D)], axis=0))
        m["mem"] = memp[b]
        m["ck"] = np.ascontiguousarray(ck[sl].reshape(16, 256, D))
        m["cv"] = np.ascontiguousarray(cv[sl].reshape(16, 256, D))
        m["sgc"] = np.ascontiguousarray(sgc[sl].reshape(48, 3072))
        m["sS"] = np.ascontiguousarray(sS[sl])
        m["ssc"] = np.ascontiguousarray(ssc[sl].reshape(32, D))
        m["cst"] = cst
        in_maps.append(m)
    import os
    ncore_run = int(os.environ.get("KCORES", NCORES))
    res = run_bass_kernel_spmd(nc, in_maps[:ncore_run], core_ids=list(range(ncore_run)))
    R = list(res.results)
    while len(R) < NCORES:
        R.append({k: np.zeros_like(v) for k, v in R[0].items()})
    y = np.stack([r["y"] for r in R])
    y_prompt = np.ascontiguousarray(y[:, :TP, :])
    y_sample = np.ascontiguousarray(y[:, TP:, :].reshape(128, 8, D))
    mk = np.stack([r["mk"] for r in R]).reshape(1, 8, 256, 4, 256)
    mv = np.stack([r["mv"] for r in R]).reshape(1, 8, 256, 4, 256)
    gcp = np.stack([r["gcp"] for r in R]).reshape(1, 8, 3, 3072)
    Sp = np.stack([r["Sp"] for r in R]).reshape(1, 8, 8, 128, 128)
    scp = np.stack([r["scp"] for r in R]).reshape(1, 8, 2, D)
    gcs = np.stack([r["gcs"] for r in R]).reshape(1, 128, 3, 3072)
    Ss = np.stack([r["Ss"] for r in R]).reshape(1, 128, 8, 128, 128)
    scs = np.stack([r["scs"] for r in R]).reshape(1, 128, 2, D)
    return (y_prompt, y_sample, mk, mv, gcp, Sp, scp, gcs, Ss, scs)
```

```python
import numpy as np
from contextlib import ExitStack
import concourse.bass as bass
import concourse.mybir as mybir
from concourse.bass_utils import run_bass_kernel_spmd

F32 = mybir.dt.float32
BF16 = mybir.dt.bfloat16
AF = mybir.ActivationFunctionType
ALU = mybir.AluOpType
AX = mybir.AxisListType

NCORES = 8
D = 1024
TP = 2048
NT = 17
TOK = NT * 128
EPS = 1e-6
NEG = -30000.0
GROUPS = [(0, 512), (512, 512), (1024, 512), (1536, 512), (2048, 128)]

C_ID, C_NMP, C_NMS, C_STP, C_STS, C_CUMP, C_CUMS, C_ONEP, C_ONES, C_BCOL, C_SEL = (
    0, 128, 256, 384, 512, 640, 768, 896, 1024, 1152, 1168)
CW = 1168 + 1024
CBW = 640


def _consts():
    c = np.zeros((128, CW), np.float32)
    p = np.arange(128)[:, None]
    f = np.arange(128)[None, :]
    same = (p // 8) == (f // 8)
    c[:, C_ID:C_ID + 128] = (p == f)
    c[:, C_NMP:C_NMP + 128] = np.where(f >= p, 0.0, NEG)
    c[:, C_NMS:C_NMS + 128] = np.where(same & (f >= p), 0.0, NEG)
    c[:, C_STP:C_STP + 128] = (f > p)
    c[:, C_STS:C_STS + 128] = same & (f > p)
    c[:, C_CUMP:C_CUMP + 128] = (p <= f)
    c[:, C_CUMS:C_CUMS + 128] = same & (p <= f)
    c[:, C_ONEP:C_ONEP + 128] = 1.0
    c[:, C_ONES:C_ONES + 128] = same
    c[:, C_BCOL:C_BCOL + 16] = (p // 8) == np.arange(16)[None, :]
    for h in range(8):
        c[h, C_SEL + h * 128:C_SEL + (h + 1) * 128] = 1.0
    return c


class _StopBuild(Exception):
    pass


def _chk(stage):
    import os
    if os.environ.get("KSTOP", "") == stage:
        raise _StopBuild()


class Prog:
    def __init__(self, nc, n_dma_sems=48):
        self.nc = nc
        self.eng = {'pe': nc.tensor, 'act': nc.scalar, 'dve': nc.vector, 'pool': nc.gpsimd, 'sp': nc.sync}
        self.csem = {e: nc.alloc_semaphore('c_' + e) for e in ('pe', 'act', 'dve', 'pool')}
        self.cc = {e: 0 for e in self.csem}
        self.dsems = [nc.alloc_semaphore('d_%d' % k) for k in range(n_dma_sems)]
        self.dval = [0] * n_dma_sems
        self.dpool = {'sp': list(range(0, n_dma_sems // 2)), 'pool': list(range(n_dma_sems // 2, n_dma_sems))}
        self.dnext = {'sp': 0, 'pool': 0}
        self.nd = 0
        self.last_w = {}
        self.readers = {}
        self.waited = {e: {} for e in self.eng}
        self.n_ops = 0
        self.n_wait = 0

    def _semof(self, s):
        return self.csem[s[1]] if s[0] == 'c' else self.dsems[s[1]]

    @staticmethod
    def _banks(key):
        if not isinstance(key, str):
            return ()
        if key.startswith('bank'):
            return ('B' + key[4],)
        if key.startswith('PK'):
            return ('BPK',)
        if key.startswith('PU'):
            return ('BPU',)
        if key.startswith('pt'):
            return ('BPT' + key[2],)
        if key == 'PS2':
            return ('B2', 'B3')
        return ()

    def rec_begin(self):
        self._rec = []

    def rec_end(self):
        r_ = self._rec
        self._rec = None
        return r_

    def emit_merged(self, streams):
        streams = [st for st in streams if st]
        pos = [0] * len(streams)
        while True:
            best = None
            for k, st in enumerate(streams):
                if pos[k] < len(st):
                    fr = pos[k] / len(st)
                    if best is None or fr < best[0]:
                        best = (fr, k)
            if best is None:
                break
            k = best[1]
            self.add(*streams[k][pos[k]][:6])
            pos[k] += 1

    def emit_scheduled(self, ops, lat=0.25):
        n = len(ops)
        preds = [set() for _ in range(n)]
        last_w = {}
        readers = {}
        last_pe = {}
        for idx, (eng, fn, r, w, dma, inc, cost) in enumerate(ops):
            bk_ = []
            for x in list(r) + list(w):
                for b in self._banks(x):
                    if b not in bk_:
                        bk_.append(b)
            wf = list(w) + bk_
            pr = preds[idx]
            for x in r:
                if x in last_w:
                    pr.add(last_w[x])
            for x in wf:
                if x in last_w:
                    pr.add(last_w[x])
                for rd in readers.get(x, ()):
                    pr.add(rd)
            if eng == 'pe':
                for b in bk_:
                    if b in last_pe:
                        pr.add(last_pe[b])
                    last_pe[b] = idx
            pr.discard(idx)
            for x in r:
                readers.setdefault(x, []).append(idx)
            for x in wf:
                last_w[x] = idx
                readers[x] = []
        succs = [[] for _ in range(n)]
        indeg = [0] * n
        for i in range(n):
            indeg[i] = len(preds[i])
            for p_ in preds[i]:
                succs[p_].append(i)
        cst_ = [(0.3 if o[6] is None else o[6]) for o in ops]
        prio = [0.0] * n
        for i in range(n - 1, -1, -1):
            m = 0.0
            for s_ in succs[i]:
                v = prio[s_] + lat
                if v > m:
                    m = v
            prio[i] = cst_[i] + m
        eng_free = {}
        dma_free = {}
        fin = [0.0] * n
        ready = set(i for i in range(n) if indeg[i] == 0)
        while ready:
            best = None
            for i in ready:
                eng = ops[i][0]
                est = eng_free.get(eng, 0.0)
                for p_ in preds[i]:
                    v = fin[p_] + (lat if ops[p_][0] != eng or ops[p_][4] else 0.0)
                    if v > est:
                        est = v
                key = (round(est * 5.0), -prio[i], i)
                if best is None or key < best[0]:
                    best = (key, i, est)
            _, i, est = best
            eng, fn, r, w, dma, inc, cost = ops[i]
            if dma:
                st_ = max(est, dma_free.get(eng, 0.0))
                fin[i] = st_ + cst_[i]
                dma_free[eng] = fin[i] - 1.5
                eng_free[eng] = est + 0.1
            else:
                fin[i] = est + cst_[i]
                eng_free[eng] = fin[i]
            self.add(eng, fn, r, w, dma, True)
            self.sim_t = max(getattr(self, 'sim_t', 0.0), fin[i])
            ready.remove(i)
            for s_ in succs[i]:
                indeg[s_] -= 1
                if indeg[s_] == 0:
                    ready.add(s_)

    def add(self, eng, fn, r=(), w=(), dma=False, inc=True, cost=None):
        if getattr(self, '_rec', None) is not None:
            self._rec.append((eng, fn, tuple(r), tuple(w), dma, inc, cost))
            return
        wt = {}
        bk_ = []
        for x in list(r) + list(w):
            for b in self._banks(x):
                if b not in bk_:
                    bk_.append(b)
        if bk_:
            w = list(w) + bk_

        def consider(rec, raw):
            reng, rdma, s, v = rec
            if (not rdma) and (not dma) and reng == eng and ((not raw) or eng == 'pe'):
                return
            if wt.get(s, 0) < v:
                wt[s] = v

        for x in r:
            rec = self.last_w.get(x)
            if rec is not None:
                consider(rec, True)
        for x in w:
            rec = self.last_w.get(x)
            if rec is not None:
                consider(rec, False)
            rd = self.readers.get(x)
            if rd:
                for rec in rd.values():
                    consider(rec, False)
        k = None
        if dma:
            lst = self.dpool[eng]
            k = lst[self.dnext[eng] % len(lst)]
            self.dnext[eng] += 1
            self.nd += 1
            if self.dval[k] > 0:
                s = ('d', k)
                if wt.get(s, 0) < self.dval[k]:
                    wt[s] = self.dval[k]
        e = self.eng[eng]
        wd = self.waited[eng]
        for s, v in wt.items():
            if wd.get(s, 0) >= v:
                continue
            wd[s] = v
            e.wait_ge(self._semof(s), v)
            self.n_wait += 1
        if fn is None:
            return
        ins = fn()
        self.n_ops += 1
        if dma:
            self.dval[k] += 16
            ins.then_inc(self.dsems[k], 16)
            rec = (eng, True, ('d', k), self.dval[k])
        else:
            if inc:
                self.cc[eng] += 1
                ins.then_inc(self.csem[eng], 1)
                rec = (eng, False, ('c', eng), self.cc[eng])
            else:
                rec = (eng, False, ('c', eng), self.cc[eng] + 1)
        for x in r:
            self.readers.setdefault(x, {})[rec[2]] = rec
        for x in w:
            self.last_w[x] = rec
            self.readers[x] = {}

    def barrier(self):
        for E, e in self.eng.items():
            wd = self.waited[E]
            for F in self.csem:
                if F == E:
                    continue
                v = self.cc[F]
                s = ('c', F)
                if v > 0 and wd.get(s, 0) < v:
                    wd[s] = v
                    e.wait_ge(self.csem[F], v)
            for k, v in enumerate(self.dval):
                s = ('d', k)
                if v > 0 and wd.get(s, 0) < v:
                    wd[s] = v
                    e.wait_ge(self.dsems[k], v)

    @staticmethod
    def _fcost(eng, out):
        n = 1
        for d_ in out.shape[1:]:
            n *= int(d_)
        if eng == 'act':
            return 0.2 + 0.0009 * n
        if eng == 'pool':
            return 0.15 + 0.002 * n
        return 0.08 + 0.0011 * n

    def mm(self, out, lhsT, rhs, start=True, stop=True, r=(), w=(), inc=None, skip=False):
        nc = self.nc
        if inc is None:
            inc = stop
        c_ = max(0.107, 0.00042 * out.shape[-1]) * (2.5 if lhsT.dtype == F32 else 1.0)
        if skip:
            self.add('pe', lambda: nc.tensor.matmul(out, lhsT=lhsT, rhs=rhs, start=start, stop=stop,
                                                    skip_group_check=True), r, w, inc=inc, cost=c_)
        else:
            self.add('pe', lambda: nc.tensor.matmul(out, lhsT=lhsT, rhs=rhs, start=start, stop=stop), r, w, inc=inc, cost=c_)

    def tr(self, out, in_, ident, r=(), w=(), inc=True):
        nc = self.nc
        self.add('pe', lambda: nc.tensor.transpose(out, in_, ident), r, w, inc=inc, cost=0.107)

    def act(self, out, in_, func, bias=None, scale=None, accum=None, r=(), w=()):
        nc = self.nc
        kw = {}
        if bias is not None:
            kw['bias'] = bias
        if scale is not None:
            kw['scale'] = scale
        if accum is not None:
            kw['accum_out'] = accum
        self.add('act', lambda: nc.scalar.activation(out=out, in_=in_, func=func, **kw), r, w, cost=self._fcost('act', out))

    def ts(self, eng, out, in0, s1, op0, s2=None, op1=None, r=(), w=()):
        e = self.eng[eng]
        kw = {}
        if op1 is not None:
            kw['op1'] = op1
        self.add(eng, lambda: e.tensor_scalar(out=out, in0=in0, scalar1=s1, scalar2=s2, op0=op0, **kw), r, w,
                 cost=self._fcost(eng, out))

    def stt(self, eng, out, in0, scalar, in1, op0, op1, r=(), w=()):
        e = self.eng[eng]
        self.add(eng, lambda: e.scalar_tensor_tensor(out=out, in0=in0, scalar=scalar, in1=in1, op0=op0, op1=op1), r, w,
                 cost=self._fcost(eng, out))

    def tt(self, eng, out, in0, in1, op, r=(), w=()):
        e = self.eng[eng]
        self.add(eng, lambda: e.tensor_tensor(out=out, in0=in0, in1=in1, op=op), r, w, cost=self._fcost(eng, out))

    def cp(self, eng, out, in_, r=(), w=()):
        if eng == 'act':
            nc = self.nc
            self.add('act', lambda: nc.scalar.copy(out=out, in_=in_), r, w, cost=self._fcost('act', out))
        else:
            e = self.eng[eng]
            self.add(eng, lambda: e.tensor_copy(out=out, in_=in_), r, w, cost=self._fcost(eng, out))

    def recip(self, out, in_, r=(), w=()):
        nc = self.nc
        self.add('dve', lambda: nc.vector.reciprocal(out=out, in_=in_), r, w)

    def memset(self, eng, ap, val, r=(), w=()):
        e = self.eng[eng]
        self.add(eng, lambda: e.memset(ap, val), r, w)

    def dma(self, q, out, in_, r=(), w=(), slow=False):
        e = self.eng[q]
        n = 1
        for d_ in out.shape:
            n *= int(d_)
        c_ = 2.0 + n * 4.0 / 150e3
        if slow:
            self.add(q, lambda: e.dma_start(out=out, in_=in_, allow_slow_non_contiguous=True), r, w, dma=True, cost=c_)
        else:
            self.add(q, lambda: e.dma_start(out=out, in_=in_), r, w, dma=True, cost=c_)


def build():
    nc = bass.Bass("TRN2", target_bir_lowering=False)

    def din(name, shape):
        return nc.dram_tensor(name, list(shape), F32, kind="ExternalInput").ap()

    def dout(name, shape):
        return nc.dram_tensor(name, list(shape), F32, kind="ExternalOutput").ap()

    x_d = din("x_all", [TOK, D])
    mem_d = din("mem", [256, D])
    ck_d = din("ck", [16, 256, D])
    cv_d = din("cv", [16, 256, D])
    sgc_d = din("sgc", [48, 3072])
    sS_d = din("sS", [16, 8, 128, 128])
    ssc_d = din("ssc", [32, D])
    cst_d = din("cst", [128, CW])
    W = {}
    for nm, shp in [("norm_mix_w", [D]), ("w_in", [D, 9232]), ("gdn_conv_w", [4, 3072]), ("gdn_A_log", [8]),
                    ("gdn_dt_bias", [8]), ("gdn_out_norm_w", [128]), ("w_gdn_o", [D, D]), ("sc_conv_w", [3, D]),
                    ("w_sc_o", [D, D]), ("w_mix_out", [D, D]), ("norm_x_w", [D]), ("mem_norm_w", [D]),
                    ("w_xq", [D, D]), ("w_xkv", [D, 2 * D]), ("w_xo", [D, D]), ("norm_mlp_w", [D]),
                    ("w_mlp_up", [D, 4 * D]), ("w_mlp_down", [4 * D, D]), ("norm_f_w", [D])]:
        W[nm] = din(nm, shp)
    y_d = dout("y", [TOK, D])
    mk_d = dout("mk", [256, D])
    mv_d = dout("mv", [256, D])
    gcp_d = dout("gcp", [3, 3072])
    Sp_d = dout("Sp", [8, 128, 128])
    scp_d = dout("scp", [2, D])
    gcs_d = dout("gcs", [48, 3072])
    Ss_d = dout("Ss", [16, 8, 128, 128])
    scs_d = dout("scs", [32, D])
    xres_d = nc.dram_tensor("xres", [TOK, D], F32, kind="Internal").ap()

    P = Prog(nc)
    outer = ExitStack()

    def sbt(es, name, shape, dt):
        return es.enter_context(nc.sbuf_tensor("sb_" + name, list(shape), dt))

    with outer:
        PAB = outer.enter_context(nc.psum_tensor("PAB", [128, 1024], F32))
        PS2 = outer.enter_context(nc.psum_tensor("PS2", [128, 1024], F32))
        PK = outer.enter_context(nc.psum_tensor("PK", [128, 512], F32))
        PU = outer.enter_context(nc.psum_tensor("PU", [128, 512], F32))
        PT0 = outer.enter_context(nc.psum_tensor("PT0", [128, 1024], BF16))
        PT1 = outer.enter_context(nc.psum_tensor("PT1", [128, 1024], BF16))
        PTs = [PT0, PT1]
        banks = [PAB[:, 0:512], PAB[:, 512:1024], PS2[:, 0:512], PS2[:, 512:1024]]
        bstate = {'i': 0, 'pt': 0}

        def nbank():
            i = bstate['i'] % 4
            bstate['i'] += 1
            return banks[i], 'bank%d' % i

        def nbank2():
            i = bstate['i'] % 2
            bstate['i'] += 1
            return banks[i], 'bank%d' % i

        def npt():
            i = bstate['pt'] % 2
            bstate['pt'] += 1
            return PTs[i], 'pt%d' % i

        cst = sbt(outer, "cst", [128, CW], F32)
        cstb = sbt(outer, "cstb", [128, CBW], BF16)
        ones1b = sbt(outer, "ones1b", [128, 128], BF16)
        ones128b = sbt(outer, "ones128b", [128, 128], BF16)
        onesi128b = sbt(outer, "onesi128b", [128, 128], BF16)
        wn_mix = sbt(outer, "wn_mix", [128, 8], F32)
        wn_x = sbt(outer, "wn_x", [128, 8], F32)
        wn_mlp = sbt(outer, "wn_mlp", [128, 8], F32)
        wn_mem = sbt(outer, "wn_mem", [128, 8], F32)
        KT = sbt(outer, "KT", [128, 8, 256], BF16)
        Vb = sbt(outer, "Vb", [128, 2, D], BF16)
        xts, njunk, nxs, nss = [], [], [], []
        nstate = {'i': 0, 'x': 0}

        def alloc_norm(es_, tag, with_x=True):
            if with_x:
                xts[:] = [sbt(es_, "xt%s%d" % (tag, i), [128, D], F32) for i in range(2)]
            njunk[:] = [sbt(es_, "njunk%s%d" % (tag, i), [128, D], BF16) for i in range(2)]
            nxs[:] = [sbt(es_, "nxs%s%d" % (tag, i), [128, D], BF16) for i in range(2)]
            nss[:] = [sbt(es_, "nss%s%d" % (tag, i), [128, 4], F32) for i in range(2)]

        esX = outer.enter_context(ExitStack())
        xnT = sbt(esX, "xnT", [128, 8, TOK], BF16)
        esN = outer.enter_context(ExitStack())
        alloc_norm(esN, "a")

        identf = cst[:, C_ID:C_ID + 128]
        identb = cstb[:, C_ID:C_ID + 128]

        P.dma('sp', cst[:], cst_d, w=['cst'])
        P.dma('pool', cstb[:], cst_d[:, 0:CBW], w=['cstb'])
        P.memset('dve', ones1b[:], 1.0, w=['ones1b'])
        P.memset('dve', ones128b[:], 128.0, w=['ones128b'])
        P.memset('dve', onesi128b[:], 1.0 / 128.0, w=['onesi128b'])
        for t, nm in [(wn_mix, "norm_mix_w"), (wn_x, "norm_x_w"), (wn_mlp, "norm_mlp_w"), (wn_mem, "mem_norm_w")]:
            P.dma('sp', t[:], W[nm].rearrange("(k p) -> p k", p=128), w=[nm], slow=True)

        def rstd_of(src, rk, si):
            ss = nss[si]
            P.act(njunk[si][:], src, AF.Square, accum=ss[:, 0:1], r=rk, w=['njunk%d' % si, 'nss%d' % si])
            P.act(ss[:, 1:2], ss[:, 0:1], AF.Ln, bias=EPS, scale=1.0 / D, r=['nss%d' % si], w=['nss%db' % si])
            P.act(ss[:, 2:3], ss[:, 1:2], AF.Exp, scale=-0.5, r=['nss%db' % si], w=['nss%dc' % si])
            return ss[:, 2:3], 'nss%dc' % si

        def norm_tile(src, rk, wn, wnk, dst, wk):
            si = nstate['i'] % 2
            nstate['i'] += 1
            rstd, rsk = rstd_of(src, rk, si)
            P.act(nxs[si][:], src, AF.Copy, scale=rstd, r=list(rk) + [rsk], w=['nxs%d' % si])
            pt, ptk = npt()
            ptv = pt[:].rearrange("p (k c) -> p k c", k=8)
            for k in range(8):
                P.tr(ptv[:, k, :], nxs[si][:, k * 128:(k + 1) * 128], identb, r=['nxs%d' % si, 'cstb'], w=[ptk],
                     inc=(k == 7))
            P.tt('dve', dst, ptv, wn[:].unsqueeze(2).to_broadcast([128, 8, 128]), ALU.mult,
                 r=[ptk, wnk], w=wk)

        def load_x(i, src_d):
            xi = nstate['x'] % 2
            nstate['x'] += 1
            P.dma('sp', xts[xi][:], src_d[i * 128:(i + 1) * 128, :], w=['xt%d' % xi])
            return xts[xi], 'xt%d' % xi

        def proj_fm(bank, bk, wsel, act_t, act_keys, tok0, n, wkeys):
            for kc in range(8):
                P.mm(bank[:, 0:n], wsel(kc), act_t[:, kc, tok0:tok0 + n], start=(kc == 0), stop=(kc == 7),
                     r=list(wkeys) + list(act_keys), w=[bk])

        try:
            with ExitStack() as es:
                Wkv = sbt(es, "Wkv", [128, 8, 2 * D], BF16)
                memnT = sbt(es, "memnT", [128, 8, 256], BF16)
                kvtok = [sbt(es, "kvtok%d" % i, [128, 2 * D], F32) for i in range(2)]
                P.rec_begin()
                P.dma('pool', Wkv[:], W["w_xkv"].rearrange("(k p) n -> p k n", p=128), w=['Wkv'])
                for st in range(2):
                    xt, xk = load_x(st, mem_d)
                    norm_tile(xt[:], [xk], wn_mem, "mem_norm_w", memnT[:, :, st * 128:(st + 1) * 128], ['memnT%d' % st])
                for st in range(2):
                    for cb in range(4):
                        bank, bk = nbank()
                        for kc in range(8):
                            P.mm(bank, memnT[:, kc, st * 128:(st + 1) * 128], Wkv[:, kc, cb * 512:(cb + 1) * 512],
                                 start=(kc == 0), stop=(kc == 7), r=['memnT%d' % st, 'Wkv'], w=[bk])
                        P.cp('act', kvtok[st][:, cb * 512:(cb + 1) * 512], bank, r=[bk], w=['kvtok%d_%d' % (st, cb)])
                    P.dma('sp', mk_d[st * 128:(st + 1) * 128, :], kvtok[st][:, 0:D],
                          r=['kvtok%d_0' % st, 'kvtok%d_1' % st], w=['mk_d'])
                    P.dma('sp', mv_d[st * 128:(st + 1) * 128, :], kvtok[st][:, D:2 * D],
                          r=['kvtok%d_2' % st, 'kvtok%d_3' % st], w=['mv_d'])
                    P.cp('dve', Vb[:, st, :], kvtok[st][:, D:2 * D], r=['kvtok%d_2' % st, 'kvtok%d_3' % st], w=['Vb'])
                for c in range(8):
                    bank, bk = nbank()
                    for kc in range(8):
                        P.mm(bank[:, 0:256], Wkv[:, kc, c * 128:(c + 1) * 128], memnT[:, kc, :],
                             start=(kc == 0), stop=(kc == 7), r=['memnT0', 'memnT1', 'Wkv'], w=[bk])
                    P.cp('act', KT[:, c, :], bank[:, 0:256], r=[bk], w=['KT'])
                for i in range(NT):
                    xt, xk = load_x(i, x_d)
                    norm_tile(xt[:], [xk], wn_mix, "norm_mix_w", xnT[:, :, i * 128:(i + 1) * 128], [('xnT', i)])
                P.emit_scheduled(P.rec_end())
                P.barrier()
            _chk("M")

            XN_ALL = [('xnT', i) for i in range(NT)]
            esN.close()
            _chk("A")

            def xn_keys(tok0, n):
                return [('xnT', i) for i in range(tok0 // 128, (tok0 + n) // 128)]

            with ExitStack() as esB:
                oTn = sbt(esB, "oTn", [128, 8, TOK], BF16)

                with ExitStack() as es:
                    gtab = {}
                    for nm in ("g", "beta", "gc", "negc", "egc", "kdec", "glast", "gt"):
                        gtab[nm] = sbt(es, "g_" + nm, [128, NT, 8], F32)
                    gcT = sbt(es, "gcT", [8, TOK], F32)
                    gtS = sbt(es, "gtS", [128, 8, 16], F32)
                    cw = sbt(es, "cw", [128, 24, 4], F32)
                    won = sbt(es, "won", [128, 1], F32)
                    es0 = es.enter_context(ExitStack())
                    for nm in ("tmpm", "tmpn"):
                        gtab[nm] = sbt(es0, "g_" + nm, [128, NT, 8], F32)
                    ab = sbt(es0, "ab", [128, NT, 16], F32)
                    Wab = sbt(es0, "Wab", [128, 8, 16], BF16)
                    dtb = sbt(es0, "dtb", [128, 8], F32)
                    negA = sbt(es0, "negA", [128, 8], F32)
                    grep = sbt(es0, "grep", [128, 8, 128], F32)

                    P.dma('pool', Wab[:], W["w_in"][:, 4096:4112].rearrange("(k p) n -> p k n", p=128), w=['Wab'])
                    P.dma('sp', dtb[:], W["gdn_dt_bias"].partition_broadcast(128), w=['dtb'])
                    P.dma('sp', negA[:], W["gdn_A_log"].partition_broadcast(128), w=['negA0'])
                    for i4 in range(4):
                        P.dma('sp', cw[:, :, i4], W["gdn_conv_w"][i4].rearrange("(c p) -> p c", p=128), w=['cw'], slow=True)
                    P.dma('sp', won[:], W["gdn_out_norm_w"].rearrange("(p o) -> p o", o=1), w=['won'], slow=True)
                    P.act(negA[:], negA[:], AF.Exp, r=['negA0'], w=['negA1'])
                    P.ts('dve', negA[:], negA[:], -1.0, ALU.mult, r=['negA1'], w=['negA'])

                    bank, bk = nbank()
                    for i in range(NT):
                        for kc in range(8):
                            P.mm(bank[:, i * 16:(i + 1) * 16], xnT[:, kc, i * 128:(i + 1) * 128], Wab[:, kc, :],
                                 start=(kc == 0), stop=(kc == 7), r=[('xnT', i), 'Wab'], w=[bk])
                    P.cp('act', ab[:].rearrange("p a b -> p (a b)"), bank[:, 0:NT * 16], r=[bk], w=['ab'])
                    _chk("B0a")
                    a_v = ab[:, :, 0:8]
                    b_v = ab[:, :, 8:16]
                    g = gtab["g"]
                    tm = gtab["tmpm"]
                    tn = gtab["tmpn"]
                    P.tt('dve', tn[:], a_v, dtb[:].unsqueeze(1).to_broadcast([128, NT, 8]), ALU.add, r=['ab', 'dtb'], w=['tn'])
                    P.ts('dve', tm[:], tn[:], 0.0, ALU.max, r=['tn'], w=['tm'])
                    P.stt('dve', tn[:], tm[:], -2.0, tn[:], ALU.mult, ALU.add, r=['tm', 'tn'], w=['tn2'])
                    P.act(tn[:], tn[:], AF.Exp, r=['tn2'], w=['tn3'])
                    P.act(tn[:], tn[:], AF.Ln, bias=1.0, r=['tn3'], w=['tn4'])
                    P.tt('dve', tn[:], tn[:], tm[:], ALU.add, r=['tn4', 'tm'], w=['tn5'])
                    P.tt('dve', g[:], tn[:], negA[:].unsqueeze(1).to_broadcast([128, NT, 8]), ALU.mult, r=['tn5', 'negA'], w=['g'])
                    P.act(gtab["beta"][:], b_v, AF.Sigmoid, r=['ab'], w=['beta'])
                    _chk("B0b")
                    bank, bk = nbank()
                    gp = g[:, 0:16, :].rearrange("p a b -> p (a b)")
                    gs = g[:, 16, :]
                    P.mm(bank[:, 0:128], cst[:, C_CUMP:C_CUMP + 128], gp, r=['cst', 'g'], w=[bk])
                    _chk("B0c0")
                    P.mm(bank[:, 128:136], cst[:, C_CUMS:C_CUMS + 128], gs, r=['cst', 'g'], w=[bk])
                    _chk("B0c00")
                    P.mm(bank[:, 256:384], cst[:, C_ONEP:C_ONEP + 128], gp, r=['cst', 'g'], w=[bk])
                    P.mm(bank[:, 384:392], cst[:, C_ONES:C_ONES + 128], gs, r=['cst', 'g'], w=[bk])
                    _chk("B0c01")
                    gc = gtab["gc"]
                    gl = gtab["glast"]
                    P.cp('act', gc[:].rearrange("p a b -> p (a b)"), bank[:, 0:136], r=[bk], w=['gc'])
                    _chk("B0c02")
                    P.cp('dve', gl[:].rearrange("p a b -> p (a b)"), bank[:, 256:392], r=[bk], w=['glast'])
                    _chk("B0c1")
                    P.ts('dve', gtab["negc"][:], gc[:], -1.0, ALU.mult, r=['gc'], w=['negc'])
                    P.act(gtab["egc"][:], gc[:], AF.Exp, r=['gc'], w=['egc'])
                    P.tt('dve', tm[:], gl[:], gc[:], ALU.subtract, r=['glast', 'gc', 'tn5'], w=['tm2'])
                    P.act(gtab["kdec"][:], tm[:], AF.Exp, r=['tm2'], w=['kdec'])
                    P.act(gtab["gt"][:], gl[:], AF.Exp, r=['glast'], w=['gt'])
                    _chk("B0c")
                    for g0 in range(0, NT, 4):
                        bank, bk = nbank()
                        ng = min(4, NT - g0)
                        for ii in range(ng):
                            i = g0 + ii
                            cm = C_CUMS if i == 16 else C_CUMP
                            P.mm(bank[0:8, ii * 128:(ii + 1) * 128], g[:, i, :], cst[:, cm:cm + 128], r=['g', 'cst'], w=[bk])
                        P.cp('act', gcT[0:8, g0 * 128:(g0 + ng) * 128], bank[0:8, 0:ng * 128], r=[bk], w=['gcT'])
                    _chk("B0d")
                    P.cp('dve', grep[:], g[:, 16, :].unsqueeze(2).to_broadcast([128, 8, 128]), r=['g'], w=['grep'])
                    bank, bk = nbank()
                    for h in range(8):
                        P.mm(bank[:, h * 16:(h + 1) * 16], grep[:, h, :], cst[:, C_BCOL:C_BCOL + 16], r=['grep', 'cst'], w=[bk])
                    P.act(gtS[:].rearrange("p a b -> p (a b)"), bank[:, 0:128], AF.Exp, r=[bk], w=['gtS'])

                    P.barrier()
                    es0.close()
                    _chk("B0")
                    wh = [sbt(es, "wh%d" % i, [128, 8, 4, 128], BF16) for i in range(2)]
                    pre2 = [sbt(es, "pre%d" % i_, [128, TP + 3], F32) for i_ in range(2)]
                    pres3 = [sbt(es, "pres%d" % i_, [128, 16, 11], F32) for i_ in range(3)]
                    tail = sbt(es, "tail", [128, 48], F32)
                    acc2 = [sbt(es, "acc%d" % i_, [128, TOK], F32) for i_ in range(2)]
                    acc = acc2[1]
                    qT = sbt(es, "qT", [128, TOK], BF16)
                    kT = sbt(es, "kT", [128, TOK], BF16)
                    vT = sbt(es, "vT", [128, TOK], BF16)
                    sq = sbt(es, "sq", [128, TOK], BF16)
                    rsbs = [sbt(es, "rsb%d" % i_, [128, 512], F32) for i_ in range(2)]
                    zsg = sbt(es, "zsg", [128, 512], BF16)
                    gst = sbt(es, "gst", [48, 3, 128], F32)
                    cso = [sbt(es, "cso%d" % i, [48, 128], F32) for i in range(2)]
                    cso3 = [sbt(es, "cso3_%d" % i, [3, 128], F32) for i in range(2)]
                    HB = []
                    for q_ in range(4):
                        HB.append({nm: sbt(es, "%s_%d" % (nm, q_), [128, 128], BF16)
                                   for nm in ("k_eg", "ke", "v_tok", "qe_t", "PTm", "R2")})
                    IB = []
                    for q_ in range(2):
                        d_ = {nm: sbt(es, "%s_%d" % (nm, q_), [128, 128], BF16) for nm in ("DT", "DTs", "egb", "Nm", "NTm")}
                        d_["YB"] = [sbt(es, "YB%d_%d" % (i_, q_), [128, 384], BF16) for i_ in range(2)]
                        IB.append(d_)
                    nW2T = sbt(es, "nW2T", [128, 128], BF16)
                    nW2Tm = sbt(es, "nW2Tm", [128, 16, 128], BF16)
                    Ut = sbt(es, "Ut", [128, 128], BF16)
                    Sf = sbt(es, "Sf", [128, 128], F32)
                    Sb = sbt(es, "Sb", [128, 128], BF16)
                    S0f = sbt(es, "S0f", [128, 16, 128], F32)
                    S0b = sbt(es, "S0b", [128, 16, 128], BF16)
                    kem = [sbt(es, "kem%d" % i, [128, 128], BF16) for i in range(2)]
                    for i_ in range(2):
                        P.memset('dve', pre2[i_][:, 0:3], 0.0, w=['pre%d' % i_])
                    P.memset('dve', nW2Tm[:].rearrange("p a b -> p (a b)"), 0.0, w=['nW2Tm'])

                    w_qkvz = W["w_in"][:, 0:4096].rearrange("(k p) (j hh c) -> p k j hh c", p=128, j=4, hh=8)

                    sgc_v = sgc_d.rearrange("r (j hh c) -> r j hh c", j=3, hh=8)

                    def load_wh(h_):
                        for j in range(4):
                            P.dma('pool', wh[h_ % 2][:, :, j, :], w_qkvz[:, :, j, h_, :], w=['wh%d' % (h_ % 2)])

                    load_wh(0)
                    P.dma('sp', gst[:], sgc_v[:, :, 0, :], w=['gst'])
                    for h in range(8):
                        whb = wh[h % 2]
                        whk = 'wh%d' % (h % 2)
                        if h + 1 < 8:
                            load_wh(h + 1)
                        P.dma('sp', S0f[:], sS_d[:, h, :, :].rearrange("s k v -> k s v"), w=['S0f'])
                        P.dma('pool', S0b[:], sS_d[:, h, :, :].rearrange("s k v -> k s v"), w=['S0b'])
                        for j in range(3):
                            P.mm(PK[:, j * 48:(j + 1) * 48], gst[0:48, j, :], cst[0:48, C_ID:C_ID + 48], r=['gst', 'cst'], w=['PKs'],
                                 inc=(j == 2))
                        for j in range(3):
                            P.cp('dve', pres3[j][:, :, 0:3], PK[:, j * 48:(j + 1) * 48].rearrange("p (s r) -> p s r", r=3),
                                 r=['PKs'], w=['pres%d' % j])
                        if h + 1 < 8:
                            P.dma('sp', gst[:], sgc_v[:, :, h + 1, :], w=['gst'])

                        def PA(j):
                            pre, prk = pre2[j % 2], 'pre%d' % (j % 2)
                            pres, psk = pres3[j], 'pres%d' % j
                            for (t0, n) in GROUPS:
                                bank, bk = nbank()
                                proj_fm(bank, bk, lambda kc: whb[:, kc, j, :], xnT, xn_keys(t0, n), t0, n, [whk])
                                if t0 < TP:
                                    P.cp('act', pre[:, 3 + t0:3 + t0 + n], bank[:, 0:n], r=[bk], w=[prk])
                                else:
                                    P.cp('act', pres[:, :, 3:11], bank[:, 0:128].rearrange("p (s t) -> p s t", t=8), r=[bk], w=[psk])

                        def RA(j):
                            pre, prk = pre2[j % 2], 'pre%d' % (j % 2)
                            pres, psk = pres3[j], 'pres%d' % j
                            P.cp('dve', tail[:].rearrange("p (s r) -> p s r", r=3), pres[:, :, 8:11], r=[psk], w=['tail'])
                            P.mm(PU[0:3, 0:128], pre[:, TP:TP + 3], identf, r=[prk, 'cst'], w=['PU0'])
                            P.mm(PU[0:48, 128:256], tail[:], identf, r=['tail', 'cst'], w=['PU1'])

                        def RB(j):
                            ch = j * 8 + h
                            c3 = cso3[j % 2]
                            P.cp('dve', c3[:], PU[0:3, 0:128], r=['PU0'], w=['cso3_%d' % (j % 2)])
                            P.dma('sp', gcp_d[:, ch * 128:(ch + 1) * 128], c3[:], r=['cso3_%d' % (j % 2)], w=['gcp_d'])
                            c48 = cso[j % 2]
                            P.cp('dve', c48[:], PU[0:48, 128:256], r=['PU1'], w=['cso%d' % (j % 2)])
                            P.dma('sp', gcs_d[:, ch * 128:(ch + 1) * 128], c48[:], r=['cso%d' % (j % 2)], w=['gcs_d'])

                        def CV(j):
                            ch = j * 8 + h
                            pre, prk = pre2[j % 2], 'pre%d' % (j % 2)
                            pres, psk = pres3[j], 'pres%d' % j
                            ac, ack = acc2[j % 2], 'acc%d' % (j % 2)
                            P.ts('dve', ac[:, 0:TP], pre[:, 0:TP], cw[:, ch, 0:1], ALU.mult, r=[prk, 'cw'], w=[ack])
                            for i4 in range(1, 4):
                                P.stt('dve', ac[:, 0:TP], pre[:, i4:i4 + TP], cw[:, ch, i4:i4 + 1], ac[:, 0:TP],
                                      ALU.mult, ALU.add, r=[prk, 'cw', ack], w=[ack])
                            accs = ac[:, TP:TOK].rearrange("p (s t) -> p s t", t=8)
                            P.ts('dve', accs, pres[:, :, 0:8], cw[:, ch, 0:1], ALU.mult, r=[psk, 'cw'], w=[ack])
                            for i4 in range(1, 4):
                                P.stt('dve', accs, pres[:, :, i4:i4 + 8], cw[:, ch, i4:i4 + 1], accs,
                                      ALU.mult, ALU.add, r=[psk, 'cw', ack], w=[ack])
                            dst = (qT, kT, vT)[j]
                            dk_ = ('qT', 'kT', 'vT')[j]
                            P.act(dst[:], ac[:], AF.Silu, r=[ack], w=[dk_])
                            if j < 2:
                                P.act(sq[:], dst[:], AF.Square, r=[dk_], w=['sq'])

                        def PC(j):
                            dst = (qT, kT, vT)[j]
                            dk_ = ('qT', 'kT', 'vT')[j]
                            for gi, (t0, n) in enumerate(GROUPS):
                                rsb = rsbs[gi % 2]
                                rk_ = 'rsb%d' % (gi % 2)
                                bank, bk = nbank()
                                P.mm(bank[:, 0:n], (ones128b if j == 0 else ones1b)[:], sq[:, t0:t0 + n],
                                     r=['sq', 'ones1b', 'ones128b'], w=[bk])
                                P.act(rsb[:, 0:n], bank[:, 0:n], AF.Ln, bias=(128.0 * EPS if j == 0 else EPS),
                                      r=[bk], w=[rk_])
                                P.act(rsb[:, 0:n], rsb[:, 0:n], AF.Exp, scale=-0.5, r=[rk_], w=[rk_])
                                P.tt('pool', dst[:, t0:t0 + n], dst[:, t0:t0 + n], rsb[:, 0:n], ALU.mult, r=[dk_, rk_], w=[dk_])

                        PA(0)
                        PA(1)
                        RA(0)
                        CV(0)
                        RB(0)
                        PA(2)
                        PC(0)
                        RA(1)
                        CV(1)
                        RB(1)
                        PC(1)
                        RA(2)
                        CV(2)
                        RB(2)
                        def prep(i):
                            smp = (i == 16)
                            c0 = i * 128
                            nmk = C_NMS if smp else C_NMP
                            stk = C_STS if smp else C_STP
                            hs = i % 4
                            q_ = i % 2
                            hb = HB[hs]
                            ib = IB[q_]
                            hk = lambda nm: '%s_%d' % (nm, hs)
                            ik = lambda nm: '%s_i%d' % (nm, q_)
                            sc = lambda nm: gtab[nm][:, i, h:h + 1]
                            PKb = PK if q_ == 0 else banks[2]
                            pkk = (lambda x_: 'PK%d' % x_) if q_ == 0 else (lambda x_: 'bank2k%d' % x_)
                            PDb = banks[q_]
                            pdk = 'bank%dd' % q_
                            pt = PTs[q_]
                            ptk = 'pt%d' % q_
                            P.tr(pt[:, 0:128], kT[:, c0:c0 + 128], identb, r=['kT', 'cstb'], w=[ptk], inc=False)
                            P.tr(pt[:, 128:256], vT[:, c0:c0 + 128], identb, r=['vT', 'cstb'], w=[ptk])
                            P.act(hb["k_eg"][:], pt[:, 0:128], AF.Copy, scale=sc("egc"), r=[ptk, 'egc'], w=[hk("k_eg")])
                            P.ts('dve', hb["ke"][:], pt[:, 0:128], sc("kdec"), ALU.mult, r=[ptk, 'kdec'], w=[hk("ke")])
                            P.cp('dve', hb["v_tok"][:], pt[:, 128:256], r=[ptk], w=[hk("v_tok")])
                            P.mm(PKb[:, 0:128], kT[:, c0:c0 + 128], kT[:, c0:c0 + 128], r=['kT'], w=[pkk(0)])
                            P.mm(PKb[:, 128:256], kT[:, c0:c0 + 128], qT[:, c0:c0 + 128], r=['kT', 'qT'], w=[pkk(1)])
                            P.mm(PKb[:, 256:384], cst[0:8, C_SEL + h * 128:C_SEL + (h + 1) * 128], gcT[0:8, c0:c0 + 128],
                                 start=True, stop=False, r=['cst', 'gcT'], w=[pkk(2)])
                            P.mm(PKb[:, 256:384], identb, cstb[:, nmk:nmk + 128], start=False, stop=True, r=['cstb'], w=[pkk(2)])
                            P.mm(PKb[:, 384:512], cst[0:8, C_SEL + h * 128:C_SEL + (h + 1) * 128], gcT[0:8, c0:c0 + 128],
                                 r=['cst', 'gcT'], w=[pkk(3)])
                            P.act(ib["DT"][:], PKb[:, 256:384], AF.Exp, bias=sc("negc"), r=[pkk(2), 'negc'], w=[ik("DT")])
                            P.act(ib["egb"][:], PKb[:, 384:512], AF.Exp, r=[pkk(3)], w=[ik("egb")])
                            P.tt('pool', hb["qe_t"][:], qT[:, c0:c0 + 128], ib["egb"][:], ALU.mult, r=['qT', ik("egb")], w=[hk("qe_t")])
                            P.tt('dve', ib["DTs"][:], ib["DT"][:], cstb[:, stk:stk + 128], ALU.mult, r=[ik("DT"), 'cstb'], w=[ik("DTs")])
                            P.stt('dve', ib["Nm"][:], PKb[:, 0:128], sc("beta"), ib["DTs"][:], ALU.mult, ALU.mult,
                                  r=[pkk(0), 'beta', ik("DTs")], w=[ik("Nm")])
                            P.tt('dve', hb["PTm"][:], PKb[:, 128:256], ib["DT"][:], ALU.mult, r=[pkk(1), ik("DT")], w=[hk("PTm")])
                            P.tr(pt[:, 256:384], ib["Nm"][:], identb, r=[ik("Nm"), 'cstb'], w=[ptk])
                            P.cp('act', ib["NTm"][:], pt[:, 256:384], r=[ptk], w=[ik("NTm")])
                            YB = ib["YB"]
                            yk = lambda nm, c_: '%s%d_i%d' % (nm, c_, q_)
                            P.tt('pool', YB[0][:, 256:384], identb, ib["Nm"][:], ALU.subtract, r=['cstb', ik("Nm")], w=[yk('R', 0)])
                            nst = 2 if smp else 6
                            P.mm(PDb[:, 0:128], ib["Nm"][:], ib["NTm"][:], r=[ik("NTm"), ik("Nm")], w=[pdk + 'a'], inc=False)
                            P.mm(PDb[:, 128:256], ib["NTm"][:], ib["Nm"][:], r=[ik("NTm"), ik("Nm")], w=[pdk + 'a'])
                            P.cp('act', YB[0][:, 0:256], PDb[:, 0:256], r=[pdk + 'a'], w=[yk('Y', 0)])
                            cur = 0
                            for kst in range(1, nst + 1):
                                nx = 1 - cur
                                needY = kst < nst - 1
                                needYT = kst < nst
                                last = (kst == nst)
                                if needYT:
                                    P.mm(PDb[:, 0:128], YB[cur][:, 128:256], YB[cur][:, 0:128], r=[yk('Y', cur)], w=[pdk + 'a'], inc=False)
                                if needY:
                                    P.mm(PDb[:, 128:384], YB[cur][:, 0:128], YB[cur][:, 128:384],
                                         r=[yk('Y', cur), yk('R', cur)], w=[pdk + 'a'])
                                else:
                                    P.mm(PDb[:, 256:384], YB[cur][:, 0:128], YB[cur][:, 256:384],
                                         r=[yk('Y', cur), yk('R', cur)], w=[pdk + 'a'])
                                if needY:
                                    P.cp('act', YB[nx][:, 0:256], PDb[:, 0:256], r=[pdk + 'a'], w=[yk('Y', nx)])
                                elif needYT:
                                    P.cp('act', YB[nx][:, 0:128], PDb[:, 0:128], r=[pdk + 'a'], w=[yk('Y', nx)])
                                if last:
                                    P.tt('dve', hb["R2"][:], YB[cur][:, 256:384], PDb[:, 256:384], ALU.add,
                                         r=[pdk + 'a', yk('R', cur)], w=[hk("R2")])
                                else:
                                    P.tt('dve', YB[nx][:, 256:384], YB[cur][:, 256:384], PDb[:, 256:384], ALU.add,
                                         r=[pdk + 'a', yk('R', cur)], w=[yk('R', nx)])
                                cur = nx

                        def rec(i):
                            smp = (i == 16)
                            c0 = i * 128
                            hs = i % 4
                            hb = HB[hs]
                            hk = lambda nm: '%s_%d' % (nm, hs)
                            sc = lambda nm: gtab[nm][:, i, h:h + 1]
                            R2 = hb["R2"][:]
                            R2k = hk("R2")
                            k_eg, ke, v_tok, qe_t, PTm = hb["k_eg"], hb["ke"], hb["v_tok"], hb["qe_t"], hb["PTm"]
                            P.mm(PU[:, 0:128], k_eg[:], R2, r=[hk('k_eg'), R2k], w=['PU0'])
                            P.act(nW2T[:], PU[:, 0:128], AF.Copy, scale=-1.0, r=['PU0'], w=['nW2T'])
                            if not smp:
                                P.mm(PU[:, 128:256], R2, v_tok[:], start=True, stop=(i == 0), r=[R2k, hk('v_tok')], w=['PU1'], inc=True)
                                if i > 0:
                                    P.mm(PU[:, 128:256], nW2T[:], Sb[:], start=False, stop=True, r=['nW2T', 'Sb'], w=['PU1'])
                            else:
                                for s in range(16):
                                    P.cp('dve', nW2Tm[:, s, 8 * s:8 * s + 8], nW2T[:, 8 * s:8 * s + 8], r=['nW2T'], w=['nW2Tm'])
                                P.mm(PU[:, 128:256], R2, v_tok[:], start=True, stop=False, r=[R2k, hk('v_tok')], w=['PU1'], inc=False)
                                for s in range(16):
                                    P.mm(PU[:, 128:256], nW2Tm[:, s, :], S0b[:, s, :], start=False, stop=(s == 15),
                                         r=['nW2Tm', 'S0b'], w=['PU1'])
                            P.act(Ut[:], PU[:, 128:256], AF.Copy, scale=sc("beta"), r=['PU1', 'beta'], w=['Ut'])
                            if not smp:
                                if i > 0:
                                    P.mm(PU[:, 256:384], Sb[:], qe_t[:], start=True, stop=False, r=['Sb', hk('qe_t')], w=['PU2'], inc=False)
                                P.mm(PU[:, 256:384], Ut[:], PTm[:], start=(i == 0), stop=True, r=['Ut', hk('PTm')], w=['PU2'])
                            else:
                                P.mm(PU[:, 256:384], Ut[:], PTm[:], start=True, stop=False, r=['Ut', hk('PTm')], w=['PU2'], inc=False)
                                for s in range(16):
                                    P.mm(PU[:, 256 + 8 * s:256 + 8 * s + 8], S0b[:, s, :], qe_t[:, 8 * s:8 * s + 8],
                                         start=False, stop=True, r=['S0b', hk('qe_t')], w=['PU2'], inc=(s == 15))
                            P.cp('act', acc[:, c0:c0 + 128], PU[:, 256:384], r=['PU2'], w=['acc1'])
                            if not smp:
                                P.mm(PU[:, 384:512], ke[:], Ut[:], r=[hk('ke'), 'Ut'], w=['PU3'])
                                if i == 0:
                                    P.cp('dve', Sf[:], PU[:, 384:512], r=['PU3'], w=['Sf'])
                                else:
                                    P.stt('dve', Sf[:], Sf[:], sc("gt"), PU[:, 384:512], ALU.mult, ALU.add, r=['PU3', 'gt', 'Sf'], w=['Sf'])
                                if i < 15:
                                    P.cp('act', Sb[:], Sf[:], r=['Sf'], w=['Sb'])
                                else:
                                    P.dma('sp', Sp_d[h, :, :], Sf[:], r=['Sf'], w=['Sp_d'])
                            else:
                                P.tt('pool', S0f[:], S0f[:], gtS[:, h, :].unsqueeze(2).to_broadcast([128, 16, 128]), ALU.mult,
                                     r=['S0f', 'gtS'], w=['S0f'])
                                for s4 in range(4):
                                    bank, bk = banks[3], 'bank3'
                                    for s_ in range(4):
                                        s = s4 * 4 + s_
                                        km = kem[s % 2]
                                        kmk = 'kem%d' % (s % 2)
                                        P.ts('dve', km[:], ke[:], cst[:, C_BCOL + s:C_BCOL + s + 1], ALU.mult, r=[hk('ke'), 'cst'], w=[kmk])
                                        P.mm(bank[:, s_ * 128:(s_ + 1) * 128], km[:], Ut[:], r=[kmk, 'Ut'], w=[bk])
                                    P.tt('dve', S0f[:, s4 * 4:(s4 + 1) * 4, :], S0f[:, s4 * 4:(s4 + 1) * 4, :],
                                         bank.rearrange("p (s v) -> p s v", s=4), ALU.add, r=[bk, 'S0f'], w=['S0f'])
                                P.dma('sp', Ss_d[:, h, :, :].rearrange("s k v -> k s v"), S0f[:], r=['S0f'], w=['Ss_d'])

                        P.rec_begin()
                        prep(0)
                        prep(1)
                        for j2 in range(0, NT, 2):
                            for i_ in (j2 + 2, j2 + 3):
                                if i_ < NT:
                                    prep(i_)
                            rec(j2)
                            if j2 + 1 < NT:
                                rec(j2 + 1)
                        P.emit_scheduled(P.rec_end())
                        P.act(sq[:], acc[:], AF.Square, r=['acc1'], w=['sq'])
                        for gi, (t0, n) in enumerate(GROUPS):
                            rsb = rsbs[gi % 2]
                            rk_ = 'rsb%d' % (gi % 2)
                            bank, bk = nbank()
                            P.mm(bank[:, 0:n], onesi128b[:], sq[:, t0:t0 + n], r=['sq', 'onesi128b'], w=[bk])
                            P.act(rsb[:, 0:n], bank[:, 0:n], AF.Ln, bias=EPS, r=[bk], w=[rk_])
                            P.act(rsb[:, 0:n], rsb[:, 0:n], AF.Exp, scale=-0.5, r=[rk_], w=[rk_])
                            P.tt('dve', rsb[:, 0:n], rsb[:, 0:n], acc[:, t0:t0 + n], ALU.mult, r=[rk_, 'acc1'], w=[rk_])
                            bank, bk = nbank()
                            proj_fm(bank, bk, lambda kc: whb[:, kc, 3, :], xnT, xn_keys(t0, n), t0, n, [whk])
                            P.act(zsg[:, 0:n], bank[:, 0:n], AF.Silu, r=[bk], w=['zsg'])
                            P.stt('dve', oTn[:, h, t0:t0 + n], rsb[:, 0:n], won[:, 0:1], zsg[:, 0:n], ALU.mult, ALU.mult,
                                  r=[rk_, 'won', 'zsg'], w=[('oTn', h)])
                    P.barrier()

                buT = sbt(esB, "buT", [128, 8, TOK], BF16)
                with ExitStack() as es:
                    wc = [sbt(es, "wc%d" % i, [128, 8, 3, 128], BF16) for i in range(2)]
                    cx = sbt(es, "cx", [128, TP + 2], F32)
                    cxs = sbt(es, "cxs", [128, 16, 10], F32)
                    cgf = [sbt(es, "cgf%d" % i, [128, 512], F32) for i in range(2)]
                    bgf = sbt(es, "bgf", [128, TOK], F32)
                    u = sbt(es, "u", [128, TOK], F32)
                    sst = sbt(es, "sst", [32, D], F32)
                    scw = sbt(es, "scw", [128, 8, 3], F32)
                    tail2 = sbt(es, "tail2", [128, 32], F32)
                    so2 = [sbt(es, "so2_%d" % i, [2, 128], F32) for i in range(2)]
                    so32 = [sbt(es, "so32_%d" % i, [32, 128], F32) for i in range(2)]
                    P.dma('sp', sst[:], ssc_d, w=['sst'])
                    for i3 in range(3):
                        P.dma('sp', scw[:, :, i3], W["sc_conv_w"][i3].rearrange("(c p) -> p c", p=128), w=['scw'], slow=True)
                    P.memset('dve', cx[:, 0:2], 0.0, w=['cx'])
                    w_sc = W["w_in"][:, 4112:7184].rearrange("(k p) (j cc n) -> p k j cc n", p=128, j=3, cc=8)
                    def load_wc(c_):
                        for j in range(3):
                            P.dma('pool', wc[c_ % 2][:, :, j, :], w_sc[:, :, j, c_, :], w=['wc%d' % (c_ % 2)])

                    load_wc(0)
                    for c in range(8):
                        wcb = wc[c % 2]
                        wck = 'wc%d' % (c % 2)
                        if c + 1 < 8:
                            load_wc(c + 1)
                        bank, bk = nbank()
                        P.mm(bank[:, 0:32], sst[0:32, c * 128:(c + 1) * 128], cst[0:32, C_ID:C_ID + 32], r=['sst', 'cst'], w=[bk])
                        P.cp('dve', cxs[:, :, 0:2], bank[:, 0:32].rearrange("p (s r) -> p s r", r=2), r=[bk], w=['cxs'])
                        for gi, (t0, n) in enumerate(GROUPS):
                            cg_ = cgf[gi % 2]
                            cgk = 'cgf%d' % (gi % 2)
                            bank, bk = nbank()
                            proj_fm(bank, bk, lambda kc: wcb[:, kc, 1, :], xnT, xn_keys(t0, n), t0, n, [wck])
                            P.cp('act', cg_[:, 0:n], bank[:, 0:n], r=[bk], w=[cgk])
                            bank, bk = nbank()
                            proj_fm(bank, bk, lambda kc: wcb[:, kc, 2, :], xnT, xn_keys(t0, n), t0, n, [wck])
                            if t0 < TP:
                                P.tt('dve', cx[:, 2 + t0:2 + t0 + n], bank[:, 0:n], cg_[:, 0:n], ALU.mult, r=[bk, cgk], w=['cx'])
                            else:
                                P.tt('dve', cxs[:, :, 2:10], bank[:, 0:128].rearrange("p (s t) -> p s t", t=8),
                                     cg_[:, 0:128].rearrange("p (s t) -> p s t", t=8), ALU.mult, r=[bk, cgk], w=['cxs'])
                            bank, bk = nbank()
                            proj_fm(bank, bk, lambda kc: wcb[:, kc, 0, :], xnT, xn_keys(t0, n), t0, n, [wck])
                            P.cp('act', bgf[:, t0:t0 + n], bank[:, 0:n], r=[bk], w=['bgf'])
                        bank, bk = nbank()
                        P.mm(bank[0:2, 0:128], cx[:, TP:TP + 2], identf, r=['cx', 'cst'], w=[bk])
                        P.cp('dve', so2[c % 2][:], bank[0:2, 0:128], r=[bk], w=['so2_%d' % (c % 2)])
                        P.dma('sp', scp_d[:, c * 128:(c + 1) * 128], so2[c % 2][:], r=['so2_%d' % (c % 2)], w=['scp_d'])
                        P.cp('dve', tail2[:].rearrange("p (s r) -> p s r", r=2), cxs[:, :, 8:10], r=['cxs'], w=['tail2'])
                        bank, bk = nbank()
                        P.mm(bank[0:32, 0:128], tail2[:], identf, r=['tail2', 'cst'], w=[bk])
                        P.cp('dve', so32[c % 2][:], bank[0:32, 0:128], r=[bk], w=['so32_%d' % (c % 2)])
                        P.dma('sp', scs_d[:, c * 128:(c + 1) * 128], so32[c % 2][:], r=['so32_%d' % (c % 2)], w=['scs_d'])
                        P.ts('dve', u[:, 0:TP], cx[:, 0:TP], scw[:, c, 0:1], ALU.mult, r=['cx', 'scw'], w=['u'])
                        for i3 in range(1, 3):
                            P.stt('dve', u[:, 0:TP], cx[:, i3:i3 + TP], scw[:, c, i3:i3 + 1], u[:, 0:TP], ALU.mult, ALU.add,
                                  r=['cx', 'scw', 'u'], w=['u'])
                        us = u[:, TP:TOK].rearrange("p (s t) -> p s t", t=8)
                        P.ts('dve', us, cxs[:, :, 0:8], scw[:, c, 0:1], ALU.mult, r=['cxs', 'scw'], w=['us'])
                        for i3 in range(1, 3):
                            P.stt('dve', us, cxs[:, :, i3:i3 + 8], scw[:, c, i3:i3 + 1], us, ALU.mult, ALU.add,
                                  r=['cxs', 'scw', 'us'], w=['us'])
                        P.tt('dve', buT[:, c, :], bgf[:], u[:], ALU.mult, r=['bgf', 'u', 'us'], w=[('buT', c)])
                    P.barrier()

                with ExitStack() as esM:
                    mT = sbt(esM, "mT", [128, 8, TOK], BF16)
                    Wmix = sbt(esM, "Wmix", [128, 8, D], BF16)
                    with ExitStack() as es:
                        w3 = [sbt(es, "w3_%d" % i, [128, 8, 4, 128], BF16) for i in range(2)]
                        sga = [sbt(es, "sga%d" % i, [128, 512], F32) for i in range(4)]
                        m1 = [sbt(es, "m1_%d" % i, [128, 512], F32) for i in range(4)]
                        OT_ALL = [('oTn', hh) for hh in range(8)]
                        BU_ALL = [('buT', cc) for cc in range(8)]
                        def load_w3(c_):
                            wb_ = w3[c_ % 2]
                            k_ = 'w3_%d' % (c_ % 2)
                            cs_ = slice(c_ * 128, (c_ + 1) * 128)
                            P.dma('pool', wb_[:, :, 2, :], W["w_in"][:, 7184 + c_ * 128:7184 + (c_ + 1) * 128].rearrange("(k p) n -> p k n", p=128), w=[k_])
                            P.dma('pool', wb_[:, :, 0, :], W["w_gdn_o"][:, cs_].rearrange("(k p) n -> p k n", p=128), w=[k_])
                            P.dma('pool', wb_[:, :, 3, :], W["w_in"][:, 8208 + c_ * 128:8208 + (c_ + 1) * 128].rearrange("(k p) n -> p k n", p=128), w=[k_])
                            P.dma('pool', wb_[:, :, 1, :], W["w_sc_o"][:, cs_].rearrange("(k p) n -> p k n", p=128), w=[k_])

                        load_w3(0)
                        load_w3(1)
                        P.dma('pool', Wmix[:], W["w_mix_out"].rearrange("(k p) n -> p k n", p=128), w=['Wmix'])
                        for c in range(8):
                            wb3 = w3[c % 2]
                            w3k = 'w3_%d' % (c % 2)
                            if c >= 1 and c + 1 < 8:
                                load_w3(c + 1)
                            for gi, (t0, n) in enumerate(GROUPS):
                                for br in range(2):
                                    src_t, src_k = (oTn, OT_ALL) if br == 0 else (buT, BU_ALL)
                                    bank, bk = nbank()
                                    proj_fm(bank, bk, lambda kc: wb3[:, kc, 2 + br, :], xnT, xn_keys(t0, n), t0, n, [w3k])
                                    bi_ = 2 * (gi % 2) + br
                                    sg = sga[bi_]
                                    sgk = 'sga%d' % bi_
                                    P.act(sg[:, 0:n], bank[:, 0:n], AF.Sigmoid, r=[bk], w=[sgk])
                                    bank, bk = nbank()
                                    proj_fm(bank, bk, lambda kc: wb3[:, kc, br, :], src_t, src_k, t0, n, [w3k])
                                    P.tt('dve', m1[bi_][:, 0:n], bank[:, 0:n], sg[:, 0:n], ALU.mult, r=[bk, sgk], w=['m1_%d' % bi_])
                                g0_ = 2 * (gi % 2)
                                P.tt('pool', mT[:, c, t0:t0 + n], m1[g0_][:, 0:n], m1[g0_ + 1][:, 0:n], ALU.add,
                                     r=['m1_%d' % g0_, 'm1_%d' % (g0_ + 1)], w=[('mT', c)])
                        P.barrier()

                    with ExitStack() as es:
                        alloc_norm(es, "b")
                        xo = [sbt(es, "xo%d" % i, [128, D], F32) for i in range(2)]
                        MT_ALL = [('mT', cc) for cc in range(8)]
                        for i in range(NT):
                            xt, xk = load_x(i, x_d)
                            xob = xo[i % 2]
                            xok = 'xo%d' % (i % 2)
                            for half in range(2):
                                bank, bk = nbank()
                                for kc in range(8):
                                    P.mm(bank, mT[:, kc, i * 128:(i + 1) * 128], Wmix[:, kc, half * 512:(half + 1) * 512],
                                         start=(kc == 0), stop=(kc == 7), r=MT_ALL + ['Wmix'], w=[bk])
                                P.tt('dve', xob[:, half * 512:(half + 1) * 512], xt[:, half * 512:(half + 1) * 512], bank, ALU.add,
                                     r=[bk, xk], w=[xok])
                            P.dma('pool', xres_d[i * 128:(i + 1) * 128, :], xob[:], r=[xok], w=[('xres', i)])
                        P.barrier()
            esX.close()

            with ExitStack() as es:
                alloc_norm(es, "c", False)
                Wxq = sbt(es, "Wxq", [128, 8, D], BF16)
                Wxo = sbt(es, "Wxo", [128, 8, D], BF16)
                xcTs = [sbt(es, "xcT%d" % i, [128, 8, 512], BF16) for i in range(2)]
                hqTs = [sbt(es, "hqT%d" % i, [128, 8, 512], BF16) for i in range(2)]
                hqm = sbt(es, "hqm", [128, 8, 16, 128], BF16)
                xg = [sbt(es, "xg%d" % i, [128, D], F32) for i in range(8)]
                efs = [sbt(es, "ef%d" % i, [128, 4, 256], F32) for i in range(2)]
                pbs = [sbt(es, "pb%d" % i, [128, 4, 256], BF16) for i in range(2)]
                pTbs = [sbt(es, "pTb%d" % i, [128, 8, 128], BF16) for i in range(2)]
                ctxTs = [sbt(es, "ctxT%d" % i, [128, 8, 128], BF16) for i in range(2)]
                smxs = [sbt(es, "smx%d" % i, [128, 16], F32) for i in range(2)]

                def SR(par_, hh_):
                    if par_ == 0:
                        return PS2[:, hh_ * 256:(hh_ + 1) * 256], 'PS2'
                    t_ = PK if hh_ < 2 else PU
                    return t_[:, (hh_ % 2) * 256:(hh_ % 2 + 1) * 256], ('PKx' if hh_ < 2 else 'PUx')

                def CR(par_, c_):
                    if par_ == 0:
                        return PS2[:, c_ * 128:(c_ + 1) * 128], 'PS2'
                    t_ = PK if c_ < 4 else PU
                    return t_[:, (c_ % 4) * 128:(c_ % 4 + 1) * 128], ('PKx' if c_ < 4 else 'PUx')
                ckfs = [sbt(es, "ckf%d" % i, [128, 2, D], F32) for i in range(1)]
                ckbs = [sbt(es, "ckb%d" % i, [128, 2, D], BF16) for i in range(2)]
                cvb = [sbt(es, "cvb%d" % i, [128, 2, D], BF16) for i in range(2)]
                KTs = [sbt(es, "KTs%d" % i, [128, 8, 256], BF16) for i in range(2)]
                P.dma('pool', Wxq[:], W["w_xq"].rearrange("(k p) n -> p k n", p=128), w=['Wxq'])
                P.dma('pool', Wxo[:], W["w_xo"].rearrange("(k p) n -> p k n", p=128), w=['Wxo'])
                P.memset('dve', hqm[:].rearrange("p a b c -> p (a b c)"), 0.0, w=['hqm'])
                tile_groups = [[0, 1, 2, 3], [4, 5, 6, 7], [8, 9, 10, 11], [12, 13, 14, 15], [16]]
                P.rec_begin()
                for gi_, tg in enumerate(tile_groups):
                    n = 128 * len(tg)
                    xcT = xcTs[gi_ % 2]
                    hqT = hqTs[gi_ % 2]
                    xck = 'xcT%d' % (gi_ % 2)
                    hqk = 'hqT%d' % (gi_ % 2)
                    xo_ = 4 * (gi_ % 2)
                    for sl, i in enumerate(tg):
                        P.dma('sp', xg[xo_ + sl][:], xres_d[i * 128:(i + 1) * 128, :], r=[('xres', i)], w=['xg%d' % (xo_ + sl)])
                        norm_tile(xg[xo_ + sl][:], ['xg%d' % (xo_ + sl)], wn_x, "norm_x_w", xcT[:, :, sl * 128:(sl + 1) * 128], [xck])
                    for c in range(8):
                        bank, bk = nbank2()
                        for kc in range(8):
                            P.mm(bank[:, 0:n], Wxq[:, kc, c * 128:(c + 1) * 128], xcT[:, kc, 0:n], start=(kc == 0), stop=(kc == 7),
                                 r=['Wxq', xck], w=[bk])
                        P.cp('act', hqT[:, c, 0:n], bank[:, 0:n], r=[bk], w=[hqk])
                    for sl, i in enumerate(tg):
                        smp = (i == 16)
                        t0 = sl * 128
                        par = 0 if smp else (i % 2)
                        ef, pb, pTb, ctxT, smx = efs[par], pbs[par], pTbs[par], ctxTs[par], smxs[par]
                        efk, pbk, pTk, ctk, sk_ = 'ef%d' % par, 'pb%d' % par, 'pTb%d' % par, 'ctxT%d' % par, 'smx%d' % par
                        if not smp:
                            for hh in range(4):
                                sr_, srk = SR(par, hh)
                                for dc in range(2):
                                    P.mm(sr_, hqT[:, 2 * hh + dc, t0:t0 + 128], KT[:, 2 * hh + dc, :],
                                         start=(dc == 0), stop=(dc == 1), r=[hqk, 'KT'], w=[srk])
                        else:
                            for s in range(16):
                                P.cp('dve', hqm[:, :, s, 8 * s:8 * s + 8], hqT[:, :, 8 * s:8 * s + 8], r=[hqk], w=['hqm'])
                            for s in range(16):
                                cf = ckfs[0]
                                cfk = 'ckf0'
                                cb = ckbs[s % 2]
                                cbk = 'ckb%d' % (s % 2)
                                P.dma('sp', cf[:], ck_d[s].rearrange("(sc p) n -> p sc n", p=128), w=[cfk])
                                P.cp('pool', cb[:, 0, :], cf[:, 0, :], r=[cfk], w=[cbk + 'a'])
                                P.cp('dve', cb[:, 1, :], cf[:, 1, :], r=[cfk], w=[cbk + 'b'])
                                kts = KTs[s % 2]
                                ktk = 'KTs%d' % (s % 2)
                                for half in range(2):
                                    pt, ptk = npt()
                                    ptv = pt[:].rearrange("p (c s) -> p c s", c=4)
                                    for cc in range(4):
                                        c = half * 4 + cc
                                        for scn in range(2):
                                            P.tr(ptv[:, cc, scn * 128:(scn + 1) * 128], cb[:, scn, c * 128:(c + 1) * 128], identb,
                                                 r=[cbk + 'a', cbk + 'b', 'cstb'], w=[ptk], inc=(cc == 3 and scn == 1))
                                    P.cp('act' if half == 0 else 'dve', kts[:, half * 4:(half + 1) * 4, :], ptv, r=[ptk], w=[ktk])
                                for hh in range(4):
                                    for dc in range(2):
                                        P.mm(PS2[:, hh * 256:(hh + 1) * 256], hqm[:, 2 * hh + dc, s, :], kts[:, 2 * hh + dc, :],
                                             start=(s == 0 and dc == 0 and hh % 2 == 0), stop=(s == 15 and dc == 1),
                                             r=['hqm', ktk], w=['PS2'], inc=(hh == 3 and dc == 1), skip=True)
                        if par == 0:
                            P.add('dve', lambda o_=smx[:, 0:4]: nc.vector.tensor_reduce(out=o_, in_=PS2[:].rearrange("p (h s) -> p h s", h=4),
                                                                                        axis=AX.X, op=ALU.max), r=['PS2'], w=[sk_ + 'm'])
                        else:
                            P.add('dve', lambda o_=smx[:, 0:2]: nc.vector.tensor_reduce(out=o_, in_=PK[:].rearrange("p (h s) -> p h s", h=2),
                                                                                        axis=AX.X, op=ALU.max), r=['PKx'], w=[sk_ + 'm'])
                            P.add('dve', lambda o_=smx[:, 2:4]: nc.vector.tensor_reduce(out=o_, in_=PU[:].rearrange("p (h s) -> p h s", h=2),
                                                                                        axis=AX.X, op=ALU.max), r=['PUx'], w=[sk_ + 'm2'])
                        P.ts('dve', smx[:, 4:8], smx[:, 0:4], -1.0 / 16.0, ALU.mult, r=[sk_ + 'm', sk_ + 'm2'], w=[sk_ + 'n'])
                        for hh in range(4):
                            sr_, srk = SR(par, hh)
                            P.act(ef[:, hh, :], sr_, AF.Exp, bias=smx[:, 4 + hh:5 + hh], scale=1.0 / 16.0,
                                  accum=smx[:, 8 + hh:9 + hh], r=[srk, sk_ + 'n'], w=[efk, sk_ + 's'])
                        P.recip(smx[:, 12:16], smx[:, 8:12], r=[sk_ + 's'], w=[sk_ + 'r'])
                        P.tt('dve', pb[:], ef[:], smx[:, 12:16].unsqueeze(2).to_broadcast([128, 4, 256]), ALU.mult,
                             r=[efk, sk_ + 'r'], w=[pbk])
                        pt, ptk = npt()
                        ptv = pt[:].rearrange("p (k c) -> p k c", k=8)
                        for hh in range(4):
                            for scn in range(2):
                                P.tr(ptv[:, hh * 2 + scn, :], pb[:, hh, scn * 128:(scn + 1) * 128], identb, r=[pbk, 'cstb'], w=[ptk],
                                     inc=(hh == 3 and scn == 1))
                        P.cp('act', pTb[:], ptv, r=[ptk], w=[pTk])
                        PSc = PS2[:].rearrange("p (c t) -> p c t", c=8)
                        if not smp:
                            for hh in range(4):
                                for dc in range(2):
                                    cr_, crk = CR(par, 2 * hh + dc)
                                    for scn in range(2):
                                        P.mm(cr_, Vb[:, scn, hh * 256 + dc * 128:hh * 256 + (dc + 1) * 128],
                                             pTb[:, hh * 2 + scn, :], start=(scn == 0), stop=(scn == 1), r=['Vb', pTk], w=[crk],
                                             inc=(hh == 3 and dc == 1 and scn == 1))
                        else:
                            for s in range(16):
                                cvt = cvb[s % 2]
                                cvk = 'cvb%d' % (s % 2)
                                P.dma('pool', cvt[:], cv_d[s].rearrange("(sc p) n -> p sc n", p=128), w=[cvk])
                                for hh in range(4):
                                    for dc in range(2):
                                        for scn in range(2):
                                            P.mm(PSc[:, 2 * hh + dc, 8 * s:8 * s + 8],
                                                 cvt[:, scn, hh * 256 + dc * 128:hh * 256 + (dc + 1) * 128],
                                                 pTb[:, hh * 2 + scn, 8 * s:8 * s + 8], start=(scn == 0), stop=(scn == 1),
                                                 r=[cvk, pTk], w=['PS2'], inc=(hh == 3 and dc == 1 and scn == 1))
                        if par == 0:
                            P.cp('act', ctxT[:], PSc, r=['PS2'], w=[ctk])
                        else:
                            P.cp('act', ctxT[:, 0:4, :], PK[:].rearrange("p (c t) -> p c t", c=4), r=['PKx'], w=[ctk])
                            P.cp('act', ctxT[:, 4:8, :], PU[:].rearrange("p (c t) -> p c t", c=4), r=['PUx'], w=[ctk + 'b'])
                        xgb = xg[xo_ + sl]
                        xgk = 'xg%d' % (xo_ + sl)
                        for half in range(2):
                            bank, bk = nbank2()
                            for kc in range(8):
                                P.mm(bank, ctxT[:, kc, :], Wxo[:, kc, half * 512:(half + 1) * 512], start=(kc == 0), stop=(kc == 7),
                                     r=[ctk, ctk + 'b', 'Wxo'], w=[bk])
                            P.tt('dve', xgb[:, half * 512:(half + 1) * 512], xgb[:, half * 512:(half + 1) * 512], bank, ALU.add,
                                 r=[bk, xgk], w=[xgk])
                        P.dma('sp', xres_d[i * 128:(i + 1) * 128, :], xgb[:], r=[xgk, ('xres', i)], w=[('xres2', i)])
                P.emit_scheduled(P.rec_end())
                P.barrier()

            with ExitStack() as es:
                alloc_norm(es, "d", False)
                Wup = sbt(es, "Wup", [128, 8, 4 * D], BF16)
                Wdn = sbt(es, "Wdn", [128, 32, D], BF16)
                hT = sbt(es, "hT", [128, 32, 256], BF16)
                rl = [sbt(es, "rl%d" % i, [128, 512], F32) for i in range(2)]
                xg2 = [sbt(es, "xd%d" % i, [128, D], F32) for i in range(4)]
                xdT = [sbt(es, "xdT%d" % i, [128, 8, 256], BF16) for i in range(2)]
                wf_bc = sbt(es, "wf_bc", [128, D], F32)
                P.dma('sp', wf_bc[:], W["norm_f_w"].partition_broadcast(128), w=['wf_bc'])
                for q4 in range(4):
                    P.dma('pool', Wup[:, :, q4 * D:(q4 + 1) * D], W["w_mlp_up"][:, q4 * D:(q4 + 1) * D].rearrange("(k p) n -> p k n", p=128),
                          w=['Wup%d' % q4])
                for q4 in range(4):
                    P.dma('pool', Wdn[:, q4 * 8:(q4 + 1) * 8, :], W["w_mlp_down"][q4 * D:(q4 + 1) * D, :].rearrange("(k p) n -> p k n", p=128),
                          w=['Wdn%d' % q4])
                pairs = [[2 * p_, 2 * p_ + 1] for p_ in range(8)] + [[16]]
                HT_ALL = [('hT', f2) for f2 in range(16)]

                def de_norm(pi):
                    tl = pairs[pi]
                    xd = xdT[pi % 2]
                    xdk = 'xdT%d' % (pi % 2)
                    for t_, i in enumerate(tl):
                        xb = xg2[i % 4]
                        xbk = 'xd%d' % (i % 4)
                        P.dma('sp', xb[:], xres_d[i * 128:(i + 1) * 128, :], r=[('xres2', i)], w=[xbk])
                        norm_tile(xb[:], [xbk], wn_mlp, "norm_mlp_w", xd[:, :, t_ * 128:(t_ + 1) * 128], [xdk])

                def de_up(pi):
                    tl = pairs[pi]
                    n = 128 * len(tl)
                    xd = xdT[pi % 2]
                    xdk = 'xdT%d' % (pi % 2)
                    for f2 in range(16):
                        bank, bk = nbank()
                        for fc_ in range(2):
                            fc = f2 * 2 + fc_
                            for kc in range(8):
                                P.mm(bank[:, fc_ * 256:fc_ * 256 + n], Wup[:, kc, fc * 128:(fc + 1) * 128], xd[:, kc, 0:n],
                                     start=(kc == 0), stop=(kc == 7), r=['Wup%d' % (fc // 8), xdk], w=[bk], inc=(kc == 7 and fc_ == 1))
                        rlb = rl[f2 % 2]
                        rlk = 'rl%d' % (f2 % 2)
                        bv = bank.rearrange("p (c t) -> p c t", c=2)[:, :, 0:n]
                        rv = rlb[:].rearrange("p (c t) -> p c t", c=2)[:, :, 0:n]
                        P.act(rv, bv, AF.Relu, r=[bk], w=[rlk])
                        P.tt('pool', hT[:, f2 * 2:(f2 + 1) * 2, 0:n], rv, rv, ALU.mult, r=[rlk], w=[('hT', f2)])

                def de_down(pi):
                    tl = pairs[pi]
                    for t_, i in enumerate(tl):
                        xb = xg2[i % 4]
                        xbk = 'xd%d' % (i % 4)
                        for half in range(2):
                            bank, bk = nbank()
                            for fc in range(32):
                                P.mm(bank, hT[:, fc, t_ * 128:(t_ + 1) * 128], Wdn[:, fc, half * 512:(half + 1) * 512],
                                     start=(fc == 0), stop=(fc == 31), r=HT_ALL + ['Wdn%d' % (fc // 8)], w=[bk])
                            P.tt('dve', xb[:, half * 512:(half + 1) * 512], xb[:, half * 512:(half + 1) * 512], bank, ALU.add,
                                 r=[bk, xbk], w=[xbk])
                        si = nstate['i'] % 2
                        nstate['i'] += 1
                        rstd, rsk = rstd_of(xb[:], [xbk], si)
                        P.stt('dve', xb[:], xb[:], rstd, wf_bc[:], ALU.mult, ALU.mult, r=[xbk, rsk, 'wf_bc'], w=[xbk])
                        P.dma('sp', y_d[i * 128:(i + 1) * 128, :], xb[:], r=[xbk], w=['y_d'])

                de_norm(0)
                for pi in range(len(pairs)):
                    de_up(pi)
                    if pi + 1 < len(pairs):
                        de_norm(pi + 1)
                    de_down(pi)
                P.barrier()
        except _StopBuild:
            P.barrier()
            outer.pop_all()
        P.barrier()
    return nc, P


_CACHE = {}


def kernel(**inputs):
    f32 = lambda a: np.ascontiguousarray(np.asarray(a, dtype=np.float32))
    xp = f32(inputs["x_prompt"])
    xs = f32(inputs["x_sample"])
    memp = f32(inputs["mem_prompt"])
    ck = f32(inputs["cache_mem_k"])[0]
    cv = f32(inputs["cache_mem_v"])[0]
    sgc = f32(inputs["state_gdn_conv"])[0]
    sS = f32(inputs["state_gdn"])[0]
    ssc = f32(inputs["state_sc_conv"])[0]
    wnames = ["norm_mix_w", "w_in", "gdn_conv_w", "gdn_A_log", "gdn_dt_bias", "gdn_out_norm_w", "w_gdn_o", "sc_conv_w",
              "w_sc_o", "w_mix_out", "norm_x_w", "mem_norm_w", "w_xq", "w_xkv", "w_xo", "norm_mlp_w", "w_mlp_up",
              "w_mlp_down"]
    wd = {nm: f32(inputs[nm])[0] for nm in wnames}
    wd["norm_f_w"] = f32(inputs["norm_f_w"])
    cst = _consts()
    if "nc" not in _CACHE:
        _CACHE["nc"] = build()
    nc, P = _CACHE["nc"]
    in_maps = []
    for b in range(NCORES):
        sl = slice(16 * b, 16 * (b + 1))
        m = dict(wd)
        m["x_all"] = np.ascontiguousarray(np.concatenate([xp[b], xs[sl].reshape(128, D)], axis=0))
        m["mem"] = memp[b]
        m["ck"] = np.ascontiguousarray(ck[sl].reshape(16, 256, D))
        m["cv"] = np.ascontiguousarray(cv[sl].reshape(16, 256, D))
        m["sgc"] = np.ascontiguousarray(sgc[sl].reshape(48, 3072))
        m["sS"] = np.ascontiguousarray(sS[sl])
        m["ssc"] = np.ascontiguousarray(ssc[sl].reshape(32, D))
        m["cst"] = cst
        in_maps.append(m)
    import os
    ncore_run = int(os.environ.get("KCORES", NCORES))
    res = run_bass_kernel_spmd(nc, in_maps[:ncore_run], core_ids=list(range(ncore_run)))
    R = list(res.results)
    while len(R) < NCORES:
        R.append({k: np.zeros_like(v) for k, v in R[0].items()})
    y = np.stack([r["y"] for r in R])
    y_prompt = np.ascontiguousarray(y[:, :TP, :])
    y_sample = np.ascontiguousarray(y[:, TP:, :].reshape(128, 8, D))
    mk = np.stack([r["mk"] for r in R]).reshape(1, 8, 256, 4, 256)
    mv = np.stack([r["mv"] for r in R]).reshape(1, 8, 256, 4, 256)
    gcp = np.stack([r["gcp"] for r in R]).reshape(1, 8, 3, 3072)
    Sp = np.stack([r["Sp"] for r in R]).reshape(1, 8, 8, 128, 128)
    scp = np.stack([r["scp"] for r in R]).reshape(1, 8, 2, D)
    gcs = np.stack([r["gcs"] for r in R]).reshape(1, 128, 3, 3072)
    Ss = np.stack([r["Ss"] for r in R]).reshape(1, 128, 8, 128, 128)
    scs = np.stack([r["scs"] for r in R]).reshape(1, 128, 2, D)
    return (y_prompt, y_sample, mk, mv, gcp, Sp, scp, gcs, Ss, scs)
```

```python
import numpy as np
from contextlib import ExitStack
import concourse.bass as bass
import concourse.mybir as mybir
from concourse.bass_utils import run_bass_kernel_spmd

F32 = mybir.dt.float32
BF16 = mybir.dt.bfloat16
AF = mybir.ActivationFunctionType
ALU = mybir.AluOpType
AX = mybir.AxisListType

NCORES = 8
D = 1024
TP = 2048
NT = 17
TOK = NT * 128
EPS = 1e-6
NEG = -30000.0
GROUPS = [(0, 512), (512, 512), (1024, 512), (1536, 512), (2048, 128)]

C_ID, C_NMP, C_NMS, C_STP, C_STS, C_CUMP, C_CUMS, C_ONEP, C_ONES, C_BCOL, C_SEL = (
    0, 128, 256, 384, 512, 640, 768, 896, 1024, 1152, 1168)
CW = 1168 + 1024
CBW = 640


def _consts():
    c = np.zeros((128, CW), np.float32)
    p = np.arange(128)[:, None]
    f = np.arange(128)[None, :]
    same = (p // 8) == (f // 8)
    c[:, C_ID:C_ID + 128] = (p == f)
    c[:, C_NMP:C_NMP + 128] = np.where(f >= p, 0.0, NEG)
    c[:, C_NMS:C_NMS + 128] = np.where(same & (f >= p), 0.0, NEG)
    c[:, C_STP:C_STP + 128] = (f > p)
    c[:, C_STS:C_STS + 128] = same & (f > p)
    c[:, C_CUMP:C_CUMP + 128] = (p <= f)
    c[:, C_CUMS:C_CUMS + 128] = same & (p <= f)
    c[:, C_ONEP:C_ONEP + 128] = 1.0
    c[:, C_ONES:C_ONES + 128] = same
    c[:, C_BCOL:C_BCOL + 16] = (p // 8) == np.arange(16)[None, :]
    for h in range(8):
        c[h, C_SEL + h * 128:C_SEL + (h + 1) * 128] = 1.0
    return c


class _StopBuild(Exception):
    pass


def _chk(stage):
    import os
    if os.environ.get("KSTOP", "") == stage:
        raise _StopBuild()


class Prog:
    def __init__(self, nc, n_dma_sems=48):
        self.nc = nc
        self.eng = {'pe': nc.tensor, 'act': nc.scalar, 'dve': nc.vector, 'pool': nc.gpsimd, 'sp': nc.sync}
        self.csem = {e: nc.alloc_semaphore('c_' + e) for e in ('pe', 'act', 'dve', 'pool')}
        self.cc = {e: 0 for e in self.csem}
        self.dsems = [nc.alloc_semaphore('d_%d' % k) for k in range(n_dma_sems)]
        self.dval = [0] * n_dma_sems
        self.dpool = {'sp': list(range(0, n_dma_sems // 2)), 'pool': list(range(n_dma_sems // 2, n_dma_sems))}
        self.dnext = {'sp': 0, 'pool': 0}
        self.nd = 0
        self.last_w = {}
        self.readers = {}
        self.waited = {e: {} for e in self.eng}
        self.n_ops = 0
        self.n_wait = 0

    def _semof(self, s):
        return self.csem[s[1]] if s[0] == 'c' else self.dsems[s[1]]

    @staticmethod
    def _banks(key):
        if not isinstance(key, str):
            return ()
        if key.startswith('bank'):
            return ('B' + key[4],)
        if key.startswith('PK'):
            return ('BPK',)
        if key.startswith('PU'):
            return ('BPU',)
        if key.startswith('pt'):
            return ('BPT' + key[2],)
        if key == 'PS2':
            return ('B2', 'B3')
        if key == 'PS2a':
            return ('B2',)
        if key == 'PS2b':
            return ('B3',)
        return ()

    def rec_begin(self):
        self._rec = []

    def rec_end(self):
        r_ = self._rec
        self._rec = None
        return r_

    def emit_merged(self, streams):
        streams = [st for st in streams if st]
        pos = [0] * len(streams)
        while True:
            best = None
            for k, st in enumerate(streams):
                if pos[k] < len(st):
                    fr = pos[k] / len(st)
                    if best is None or fr < best[0]:
                        best = (fr, k)
            if best is None:
                break
            k = best[1]
            self.add(*streams[k][pos[k]][:6])
            pos[k] += 1

    def emit_scheduled(self, ops, lat=0.25):
        n = len(ops)
        preds = [set() for _ in range(n)]
        last_w = {}
        readers = {}
        last_pe = {}
        for idx, (eng, fn, r, w, dma, inc, cost) in enumerate(ops):
            bk_ = []
            for x in list(r) + list(w):
                for b in self._banks(x):
                    if b not in bk_:
                        bk_.append(b)
            wf = list(w) + bk_
            pr = preds[idx]
            for x in r:
                if x in last_w:
                    pr.add(last_w[x])
            for x in wf:
                if x in last_w:
                    pr.add(last_w[x])
                for rd in readers.get(x, ()):
                    pr.add(rd)
            if eng == 'pe':
                for b in bk_:
                    if b in last_pe:
                        pr.add(last_pe[b])
                    last_pe[b] = idx
            pr.discard(idx)
            for x in r:
                readers.setdefault(x, []).append(idx)
            for x in wf:
                last_w[x] = idx
                readers[x] = []
        succs = [[] for _ in range(n)]
        indeg = [0] * n
        for i in range(n):
            indeg[i] = len(preds[i])
            for p_ in preds[i]:
                succs[p_].append(i)
        cst_ = [(0.3 if o[6] is None else o[6]) for o in ops]
        prio = [0.0] * n
        for i in range(n - 1, -1, -1):
            m = 0.0
            for s_ in succs[i]:
                v = prio[s_] + lat
                if v > m:
                    m = v
            prio[i] = cst_[i] + m
        eng_free = {}
        dma_free = {}
        fin = [0.0] * n
        ready = set(i for i in range(n) if indeg[i] == 0)
        while ready:
            best = None
            for i in ready:
                eng = ops[i][0]
                est = eng_free.get(eng, 0.0)
                for p_ in preds[i]:
                    v = fin[p_] + (lat if ops[p_][0] != eng or ops[p_][4] else 0.0)
                    if v > est:
                        est = v
                key = (round(est * 5.0), -prio[i], i)
                if best is None or key < best[0]:
                    best = (key, i, est)
            _, i, est = best
            eng, fn, r, w, dma, inc, cost = ops[i]
            if dma:
                st_ = max(est, dma_free.get(eng, 0.0))
                fin[i] = st_ + cst_[i]
                dma_free[eng] = fin[i] - 1.5
                eng_free[eng] = est + 0.1
            else:
                fin[i] = est + cst_[i]
                eng_free[eng] = fin[i]
            self.add(eng, fn, r, w, dma, True)
            self.sim_t = max(getattr(self, 'sim_t', 0.0), fin[i])
            ready.remove(i)
            for s_ in succs[i]:
                indeg[s_] -= 1
                if indeg[s_] == 0:
                    ready.add(s_)

    def add(self, eng, fn, r=(), w=(), dma=False, inc=True, cost=None):
        if getattr(self, '_rec', None) is not None:
            self._rec.append((eng, fn, tuple(r), tuple(w), dma, inc, cost))
            return
        wt = {}
        bk_ = []
        for x in list(r) + list(w):
            for b in self._banks(x):
                if b not in bk_:
                    bk_.append(b)
        if bk_:
            w = list(w) + bk_

        def consider(rec, raw):
            reng, rdma, s, v = rec
            if (not rdma) and (not dma) and reng == eng and ((not raw) or eng == 'pe'):
                return
            if wt.get(s, 0) < v:
                wt[s] = v

        for x in r:
            rec = self.last_w.get(x)
            if rec is not None:
                consider(rec, True)
        for x in w:
            rec = self.last_w.get(x)
            if rec is not None:
                consider(rec, False)
            rd = self.readers.get(x)
            if rd:
                for rec in rd.values():
                    consider(rec, False)
        k = None
        if dma:
            lst = self.dpool[eng]
            k = lst[self.dnext[eng] % len(lst)]
            self.dnext[eng] += 1
            self.nd += 1
            if self.dval[k] > 0:
                s = ('d', k)
                if wt.get(s, 0) < self.dval[k]:
                    wt[s] = self.dval[k]
        e = self.eng[eng]
        wd = self.waited[eng]
        for s, v in wt.items():
            if wd.get(s, 0) >= v:
                continue
            wd[s] = v
            e.wait_ge(self._semof(s), v)
            self.n_wait += 1
        if fn is None:
            return
        ins = fn()
        self.n_ops += 1
        if dma:
            self.dval[k] += 16
            ins.then_inc(self.dsems[k], 16)
            rec = (eng, True, ('d', k), self.dval[k])
        else:
            if inc:
                self.cc[eng] += 1
                ins.then_inc(self.csem[eng], 1)
                rec = (eng, False, ('c', eng), self.cc[eng])
            else:
                rec = (eng, False, ('c', eng), self.cc[eng] + 1)
        for x in r:
            self.readers.setdefault(x, {})[rec[2]] = rec
        for x in w:
            self.last_w[x] = rec
            self.readers[x] = {}

    def barrier(self):
        for E, e in self.eng.items():
            wd = self.waited[E]
            for F in self.csem:
                if F == E:
                    continue
                v = self.cc[F]
                s = ('c', F)
                if v > 0 and wd.get(s, 0) < v:
                    wd[s] = v
                    e.wait_ge(self.csem[F], v)
            for k, v in enumerate(self.dval):
                s = ('d', k)
                if v > 0 and wd.get(s, 0) < v:
                    wd[s] = v
                    e.wait_ge(self.dsems[k], v)

    @staticmethod
    def _fcost(eng, out):
        n = 1
        for d_ in out.shape[1:]:
            n *= int(d_)
        if eng == 'act':
            return 0.2 + 0.0009 * n
        if eng == 'pool':
            return 0.15 + 0.002 * n
        return 0.08 + 0.0011 * n

    def mm(self, out, lhsT, rhs, start=True, stop=True, r=(), w=(), inc=None, skip=False):
        nc = self.nc
        if inc is None:
            inc = stop
        c_ = max(0.107, 0.00042 * out.shape[-1]) * (2.5 if lhsT.dtype == F32 else 1.0)
        if skip:
            self.add('pe', lambda: nc.tensor.matmul(out, lhsT=lhsT, rhs=rhs, start=start, stop=stop,
                                                    skip_group_check=True), r, w, inc=inc, cost=c_)
        else:
            self.add('pe', lambda: nc.tensor.matmul(out, lhsT=lhsT, rhs=rhs, start=start, stop=stop), r, w, inc=inc, cost=c_)

    def tr(self, out, in_, ident, r=(), w=(), inc=True):
        nc = self.nc
        self.add('pe', lambda: nc.tensor.transpose(out, in_, ident), r, w, inc=inc, cost=0.107)

    def act(self, out, in_, func, bias=None, scale=None, accum=None, r=(), w=()):
        nc = self.nc
        kw = {}
        if bias is not None:
            kw['bias'] = bias
        if scale is not None:
            kw['scale'] = scale
        if accum is not None:
            kw['accum_out'] = accum
        self.add('act', lambda: nc.scalar.activation(out=out, in_=in_, func=func, **kw), r, w, cost=self._fcost('act', out))

    def ts(self, eng, out, in0, s1, op0, s2=None, op1=None, r=(), w=()):
        e = self.eng[eng]
        kw = {}
        if op1 is not None:
            kw['op1'] = op1
        self.add(eng, lambda: e.tensor_scalar(out=out, in0=in0, scalar1=s1, scalar2=s2, op0=op0, **kw), r, w,
                 cost=self._fcost(eng, out))

    def stt(self, eng, out, in0, scalar, in1, op0, op1, r=(), w=()):
        e = self.eng[eng]
        self.add(eng, lambda: e.scalar_tensor_tensor(out=out, in0=in0, scalar=scalar, in1=in1, op0=op0, op1=op1), r, w,
                 cost=self._fcost(eng, out))

    def tt(self, eng, out, in0, in1, op, r=(), w=()):
        e = self.eng[eng]
        self.add(eng, lambda: e.tensor_tensor(out=out, in0=in0, in1=in1, op=op), r, w, cost=self._fcost(eng, out))

    def cp(self, eng, out, in_, r=(), w=()):
        if eng == 'act':
            nc = self.nc
            self.add('act', lambda: nc.scalar.copy(out=out, in_=in_), r, w, cost=self._fcost('act', out))
        else:
            e = self.eng[eng]
            self.add(eng, lambda: e.tensor_copy(out=out, in_=in_), r, w, cost=self._fcost(eng, out))

    def recip(self, out, in_, r=(), w=()):
        nc = self.nc
        self.add('dve', lambda: nc.vector.reciprocal(out=out, in_=in_), r, w)

    def memset(self, eng, ap, val, r=(), w=()):
        e = self.eng[eng]
        self.add(eng, lambda: e.memset(ap, val), r, w)

    def dma(self, q, out, in_, r=(), w=(), slow=False):
        e = self.eng[q]
        n = 1
        for d_ in out.shape:
            n *= int(d_)
        c_ = 2.0 + n * 4.0 / 150e3
        if slow:
            self.add(q, lambda: e.dma_start(out=out, in_=in_, allow_slow_non_contiguous=True), r, w, dma=True, cost=c_)
        else:
            self.add(q, lambda: e.dma_start(out=out, in_=in_), r, w, dma=True, cost=c_)


def build():
    nc = bass.Bass("TRN2", target_bir_lowering=False)

    def din(name, shape):
        return nc.dram_tensor(name, list(shape), F32, kind="ExternalInput").ap()

    def dout(name, shape):
        return nc.dram_tensor(name, list(shape), F32, kind="ExternalOutput").ap()

    x_d = din("x_all", [TOK, D])
    mem_d = din("mem", [256, D])
    ck_d = din("ck", [16, 256, D])
    cv_d = din("cv", [16, 256, D])
    sgc_d = din("sgc", [48, 3072])
    sS_d = din("sS", [16, 8, 128, 128])
    ssc_d = din("ssc", [32, D])
    cst_d = din("cst", [128, CW])
    W = {}
    for nm, shp in [("norm_mix_w", [D]), ("w_in", [D, 9232]), ("gdn_conv_w", [4, 3072]), ("gdn_A_log", [8]),
                    ("gdn_dt_bias", [8]), ("gdn_out_norm_w", [128]), ("w_gdn_o", [D, D]), ("sc_conv_w", [3, D]),
                    ("w_sc_o", [D, D]), ("w_mix_out", [D, D]), ("norm_x_w", [D]), ("mem_norm_w", [D]),
                    ("w_xq", [D, D]), ("w_xkv", [D, 2 * D]), ("w_xo", [D, D]), ("norm_mlp_w", [D]),
                    ("w_mlp_up", [D, 4 * D]), ("w_mlp_down", [4 * D, D]), ("norm_f_w", [D])]:
        W[nm] = din(nm, shp)
    y_d = dout("y", [TOK, D])
    mk_d = dout("mk", [256, D])
    mv_d = dout("mv", [256, D])
    gcp_d = dout("gcp", [3, 3072])
    Sp_d = dout("Sp", [8, 128, 128])
    scp_d = dout("scp", [2, D])
    gcs_d = dout("gcs", [48, 3072])
    Ss_d = dout("Ss", [16, 8, 128, 128])
    scs_d = dout("scs", [32, D])
    xres_d = nc.dram_tensor("xres", [TOK, D], F32, kind="Internal").ap()

    P = Prog(nc)
    outer = ExitStack()

    def sbt(es, name, shape, dt):
        return es.enter_context(nc.sbuf_tensor("sb_" + name, list(shape), dt))

    with outer:
        PAB = outer.enter_context(nc.psum_tensor("PAB", [128, 1024], F32))
        PS2 = outer.enter_context(nc.psum_tensor("PS2", [128, 1024], F32))
        PK = outer.enter_context(nc.psum_tensor("PK", [128, 512], F32))
        PU = outer.enter_context(nc.psum_tensor("PU", [128, 512], F32))
        PT0 = outer.enter_context(nc.psum_tensor("PT0", [128, 1024], BF16))
        PT1 = outer.enter_context(nc.psum_tensor("PT1", [128, 1024], BF16))
        PTs = [PT0, PT1]
        banks = [PAB[:, 0:512], PAB[:, 512:1024], PS2[:, 0:512], PS2[:, 512:1024]]
        bstate = {'i': 0, 'pt': 0}

        def nbank():
            i = bstate['i'] % 4
            bstate['i'] += 1
            return banks[i], 'bank%d' % i

        def nbank2():
            i = bstate['i'] % 2
            bstate['i'] += 1
            return banks[i], 'bank%d' % i

        def npt():
            i = bstate['pt'] % 2
            bstate['pt'] += 1
            return PTs[i], 'pt%d' % i

        cst = sbt(outer, "cst", [128, CW], F32)
        cstb = sbt(outer, "cstb", [128, CBW], BF16)
        ones1b = sbt(outer, "ones1b", [128, 128], BF16)
        ones128b = sbt(outer, "ones128b", [128, 128], BF16)
        onesi128b = sbt(outer, "onesi128b", [128, 128], BF16)
        wn_mix = sbt(outer, "wn_mix", [128, 8], F32)
        wn_x = sbt(outer, "wn_x", [128, 8], F32)
        wn_mlp = sbt(outer, "wn_mlp", [128, 8], F32)
        wn_mem = sbt(outer, "wn_mem", [128, 8], F32)
        KT = sbt(outer, "KT", [128, 8, 256], BF16)
        Vb = sbt(outer, "Vb", [128, 2, D], BF16)
        xts, njunk, nxs, nss = [], [], [], []
        nstate = {'i': 0, 'x': 0}

        def alloc_norm(es_, tag, with_x=True):
            if with_x:
                xts[:] = [sbt(es_, "xt%s%d" % (tag, i), [128, D], F32) for i in range(2)]
            njunk[:] = [sbt(es_, "njunk%s%d" % (tag, i), [128, D], BF16) for i in range(2)]
            nxs[:] = [sbt(es_, "nxs%s%d" % (tag, i), [128, D], BF16) for i in range(2)]
            nss[:] = [sbt(es_, "nss%s%d" % (tag, i), [128, 4], F32) for i in range(2)]

        esX = outer.enter_context(ExitStack())
        xnT = sbt(esX, "xnT", [128, 8, TOK], BF16)
        esN = outer.enter_context(ExitStack())
        alloc_norm(esN, "a")

        identf = cst[:, C_ID:C_ID + 128]
        identb = cstb[:, C_ID:C_ID + 128]

        P.dma('sp', cst[:], cst_d, w=['cst'])
        P.dma('pool', cstb[:], cst_d[:, 0:CBW], w=['cstb'])
        P.memset('dve', ones1b[:], 1.0, w=['ones1b'])
        P.memset('dve', ones128b[:], 128.0, w=['ones128b'])
        P.memset('dve', onesi128b[:], 1.0 / 128.0, w=['onesi128b'])
        for t, nm in [(wn_mix, "norm_mix_w"), (wn_x, "norm_x_w"), (wn_mlp, "norm_mlp_w"), (wn_mem, "mem_norm_w")]:
            P.dma('sp', t[:], W[nm].rearrange("(k p) -> p k", p=128), w=[nm], slow=True)

        def rstd_of(src, rk, si):
            ss = nss[si]
            P.act(njunk[si][:], src, AF.Square, accum=ss[:, 0:1], r=rk, w=['njunk%d' % si, 'nss%d' % si])
            P.act(ss[:, 1:2], ss[:, 0:1], AF.Ln, bias=EPS, scale=1.0 / D, r=['nss%d' % si], w=['nss%db' % si])
            P.act(ss[:, 2:3], ss[:, 1:2], AF.Exp, scale=-0.5, r=['nss%db' % si], w=['nss%dc' % si])
            return ss[:, 2:3], 'nss%dc' % si

        def norm_tile(src, rk, wn, wnk, dst, wk):
            si = nstate['i'] % 2
            nstate['i'] += 1
            rstd, rsk = rstd_of(src, rk, si)
            P.act(nxs[si][:], src, AF.Copy, scale=rstd, r=list(rk) + [rsk], w=['nxs%d' % si])
            pt, ptk = npt()
            ptv = pt[:].rearrange("p (k c) -> p k c", k=8)
            for k in range(8):
                P.tr(ptv[:, k, :], nxs[si][:, k * 128:(k + 1) * 128], identb, r=['nxs%d' % si, 'cstb'], w=[ptk],
                     inc=(k == 7))
            P.tt('dve', dst, ptv, wn[:].unsqueeze(2).to_broadcast([128, 8, 128]), ALU.mult,
                 r=[ptk, wnk], w=wk)

        def load_x(i, src_d):
            xi = nstate['x'] % 2
            nstate['x'] += 1
            P.dma('sp', xts[xi][:], src_d[i * 128:(i + 1) * 128, :], w=['xt%d' % xi])
            return xts[xi], 'xt%d' % xi

        def proj_fm(bank, bk, wsel, act_t, act_keys, tok0, n, wkeys):
            for kc in range(8):
                P.mm(bank[:, 0:n], wsel(kc), act_t[:, kc, tok0:tok0 + n], start=(kc == 0), stop=(kc == 7),
                     r=list(wkeys) + list(act_keys), w=[bk])

        try:
            with ExitStack() as es:
                Wkv = sbt(es, "Wkv", [128, 8, 2 * D], BF16)
                memnT = sbt(es, "memnT", [128, 8, 256], BF16)
                kvtok = [sbt(es, "kvtok%d" % i, [128, 2 * D], F32) for i in range(2)]
                P.rec_begin()
                P.dma('pool', Wkv[:], W["w_xkv"].rearrange("(k p) n -> p k n", p=128), w=['Wkv'])
                for st in range(2):
                    xt, xk = load_x(st, mem_d)
                    norm_tile(xt[:], [xk], wn_mem, "mem_norm_w", memnT[:, :, st * 128:(st + 1) * 128], ['memnT%d' % st])
                for st in range(2):
                    for cb in range(4):
                        bank, bk = nbank()
                        for kc in range(8):
                            P.mm(bank, memnT[:, kc, st * 128:(st + 1) * 128], Wkv[:, kc, cb * 512:(cb + 1) * 512],
                                 start=(kc == 0), stop=(kc == 7), r=['memnT%d' % st, 'Wkv'], w=[bk])
                        P.cp('act', kvtok[st][:, cb * 512:(cb + 1) * 512], bank, r=[bk], w=['kvtok%d_%d' % (st, cb)])
                    P.dma('sp', mk_d[st * 128:(st + 1) * 128, :], kvtok[st][:, 0:D],
                          r=['kvtok%d_0' % st, 'kvtok%d_1' % st], w=['mk_d'])
                    P.dma('sp', mv_d[st * 128:(st + 1) * 128, :], kvtok[st][:, D:2 * D],
                          r=['kvtok%d_2' % st, 'kvtok%d_3' % st], w=['mv_d'])
                    P.cp('dve', Vb[:, st, :], kvtok[st][:, D:2 * D], r=['kvtok%d_2' % st, 'kvtok%d_3' % st], w=['Vb'])
                for c in range(8):
                    bank, bk = nbank()
                    for kc in range(8):
                        P.mm(bank[:, 0:256], Wkv[:, kc, c * 128:(c + 1) * 128], memnT[:, kc, :],
                             start=(kc == 0), stop=(kc == 7), r=['memnT0', 'memnT1', 'Wkv'], w=[bk])
                    P.cp('act', KT[:, c, :], bank[:, 0:256], r=[bk], w=['KT'])
                for i in range(NT):
                    xt, xk = load_x(i, x_d)
                    norm_tile(xt[:], [xk], wn_mix, "norm_mix_w", xnT[:, :, i * 128:(i + 1) * 128], [('xnT', i)])
                P.emit_scheduled(P.rec_end())
                P.barrier()
            _chk("M")

            XN_ALL = [('xnT', i) for i in range(NT)]
            esN.close()
            _chk("A")

            def xn_keys(tok0, n):
                return [('xnT', i) for i in range(tok0 // 128, (tok0 + n) // 128)]

            with ExitStack() as esB:
                oTn = sbt(esB, "oTn", [128, 8, TOK], BF16)

                with ExitStack() as es:
                    gtab = {}
                    for nm in ("g", "beta", "gc", "negc", "egc", "kdec", "glast", "gt"):
                        gtab[nm] = sbt(es, "g_" + nm, [128, NT, 8], F32)
                    gcT = sbt(es, "gcT", [8, TOK], F32)
                    gtS = sbt(es, "gtS", [128, 8, 16], F32)
                    cw = sbt(es, "cw", [128, 24, 4], F32)
                    won = sbt(es, "won", [128, 1], F32)
                    es0 = es.enter_context(ExitStack())
                    for nm in ("tmpm", "tmpn"):
                        gtab[nm] = sbt(es0, "g_" + nm, [128, NT, 8], F32)
                    ab = sbt(es0, "ab", [128, NT, 16], F32)
                    Wab = sbt(es0, "Wab", [128, 8, 16], BF16)
                    dtb = sbt(es0, "dtb", [128, 8], F32)
                    negA = sbt(es0, "negA", [128, 8], F32)
                    grep = sbt(es0, "grep", [128, 8, 128], F32)

                    P.dma('pool', Wab[:], W["w_in"][:, 4096:4112].rearrange("(k p) n -> p k n", p=128), w=['Wab'])
                    P.dma('sp', dtb[:], W["gdn_dt_bias"].partition_broadcast(128), w=['dtb'])
                    P.dma('sp', negA[:], W["gdn_A_log"].partition_broadcast(128), w=['negA0'])
                    for i4 in range(4):
                        P.dma('sp', cw[:, :, i4], W["gdn_conv_w"][i4].rearrange("(c p) -> p c", p=128), w=['cw'], slow=True)
                    P.dma('sp', won[:], W["gdn_out_norm_w"].rearrange("(p o) -> p o", o=1), w=['won'], slow=True)
                    P.act(negA[:], negA[:], AF.Exp, r=['negA0'], w=['negA1'])
                    P.ts('dve', negA[:], negA[:], -1.0, ALU.mult, r=['negA1'], w=['negA'])

                    bank, bk = nbank()
                    for i in range(NT):
                        for kc in range(8):
                            P.mm(bank[:, i * 16:(i + 1) * 16], xnT[:, kc, i * 128:(i + 1) * 128], Wab[:, kc, :],
                                 start=(kc == 0), stop=(kc == 7), r=[('xnT', i), 'Wab'], w=[bk])
                    P.cp('act', ab[:].rearrange("p a b -> p (a b)"), bank[:, 0:NT * 16], r=[bk], w=['ab'])
                    _chk("B0a")
                    a_v = ab[:, :, 0:8]
                    b_v = ab[:, :, 8:16]
                    g = gtab["g"]
                    tm = gtab["tmpm"]
                    tn = gtab["tmpn"]
                    P.tt('dve', tn[:], a_v, dtb[:].unsqueeze(1).to_broadcast([128, NT, 8]), ALU.add, r=['ab', 'dtb'], w=['tn'])
                    P.ts('dve', tm[:], tn[:], 0.0, ALU.max, r=['tn'], w=['tm'])
                    P.stt('dve', tn[:], tm[:], -2.0, tn[:], ALU.mult, ALU.add, r=['tm', 'tn'], w=['tn2'])
                    P.act(tn[:], tn[:], AF.Exp, r=['tn2'], w=['tn3'])
                    P.act(tn[:], tn[:], AF.Ln, bias=1.0, r=['tn3'], w=['tn4'])
                    P.tt('dve', tn[:], tn[:], tm[:], ALU.add, r=['tn4', 'tm'], w=['tn5'])
                    P.tt('dve', g[:], tn[:], negA[:].unsqueeze(1).to_broadcast([128, NT, 8]), ALU.mult, r=['tn5', 'negA'], w=['g'])
                    P.act(gtab["beta"][:], b_v, AF.Sigmoid, r=['ab'], w=['beta'])
                    _chk("B0b")
                    bank, bk = nbank()
                    gp = g[:, 0:16, :].rearrange("p a b -> p (a b)")
                    gs = g[:, 16, :]
                    P.mm(bank[:, 0:128], cst[:, C_CUMP:C_CUMP + 128], gp, r=['cst', 'g'], w=[bk])
                    _chk("B0c0")
                    P.mm(bank[:, 128:136], cst[:, C_CUMS:C_CUMS + 128], gs, r=['cst', 'g'], w=[bk])
                    _chk("B0c00")
                    P.mm(bank[:, 256:384], cst[:, C_ONEP:C_ONEP + 128], gp, r=['cst', 'g'], w=[bk])
                    P.mm(bank[:, 384:392], cst[:, C_ONES:C_ONES + 128], gs, r=['cst', 'g'], w=[bk])
                    _chk("B0c01")
                    gc = gtab["gc"]
                    gl = gtab["glast"]
                    P.cp('act', gc[:].rearrange("p a b -> p (a b)"), bank[:, 0:136], r=[bk], w=['gc'])
                    _chk("B0c02")
                    P.cp('dve', gl[:].rearrange("p a b -> p (a b)"), bank[:, 256:392], r=[bk], w=['glast'])
                    _chk("B0c1")
                    P.ts('dve', gtab["negc"][:], gc[:], -1.0, ALU.mult, r=['gc'], w=['negc'])
                    P.act(gtab["egc"][:], gc[:], AF.Exp, r=['gc'], w=['egc'])
                    P.tt('dve', tm[:], gl[:], gc[:], ALU.subtract, r=['glast', 'gc', 'tn5'], w=['tm2'])
                    P.act(gtab["kdec"][:], tm[:], AF.Exp, r=['tm2'], w=['kdec'])
                    P.act(gtab["gt"][:], gl[:], AF.Exp, r=['glast'], w=['gt'])
                    _chk("B0c")
                    for g0 in range(0, NT, 4):
                        bank, bk = nbank()
                        ng = min(4, NT - g0)
                        for ii in range(ng):
                            i = g0 + ii
                            cm = C_CUMS if i == 16 else C_CUMP
                            P.mm(bank[0:8, ii * 128:(ii + 1) * 128], g[:, i, :], cst[:, cm:cm + 128], r=['g', 'cst'], w=[bk])
                        P.cp('act', gcT[0:8, g0 * 128:(g0 + ng) * 128], bank[0:8, 0:ng * 128], r=[bk], w=['gcT'])
                    _chk("B0d")
                    P.cp('dve', grep[:], g[:, 16, :].unsqueeze(2).to_broadcast([128, 8, 128]), r=['g'], w=['grep'])
                    bank, bk = nbank()
                    for h in range(8):
                        P.mm(bank[:, h * 16:(h + 1) * 16], grep[:, h, :], cst[:, C_BCOL:C_BCOL + 16], r=['grep', 'cst'], w=[bk])
                    P.act(gtS[:].rearrange("p a b -> p (a b)"), bank[:, 0:128], AF.Exp, r=[bk], w=['gtS'])

                    P.barrier()
                    es0.close()
                    _chk("B0")
                    wh = [sbt(es, "wh%d" % i, [128, 8, 4, 128], BF16) for i in range(2)]
                    pre2 = [sbt(es, "pre%d" % i_, [128, TP + 3], F32) for i_ in range(2)]
                    pres3 = [sbt(es, "pres%d" % i_, [128, 16, 11], F32) for i_ in range(3)]
                    tail = sbt(es, "tail", [128, 48], F32)
                    acc2 = [sbt(es, "acc%d" % i_, [128, TOK], F32) for i_ in range(2)]
                    acc = acc2[1]
                    qT = sbt(es, "qT", [128, TOK], BF16)
                    kT = sbt(es, "kT", [128, TOK], BF16)
                    vT = sbt(es, "vT", [128, TOK], BF16)
                    sq = sbt(es, "sq", [128, TOK], BF16)
                    rsbs = [sbt(es, "rsb%d" % i_, [128, 512], F32) for i_ in range(2)]
                    zsg = sbt(es, "zsg", [128, 512], BF16)
                    gst = sbt(es, "gst", [48, 3, 128], F32)
                    cso = [sbt(es, "cso%d" % i, [48, 128], F32) for i in range(2)]
                    cso3 = [sbt(es, "cso3_%d" % i, [3, 128], F32) for i in range(2)]
                    HB = []
                    for q_ in range(4):
                        HB.append({nm: sbt(es, "%s_%d" % (nm, q_), [128, 128], BF16)
                                   for nm in ("k_eg", "ke", "v_tok", "qe_t", "PTm", "R2")})
                    IB = []
                    for q_ in range(2):
                        d_ = {nm: sbt(es, "%s_%d" % (nm, q_), [128, 128], BF16) for nm in ("DT", "DTs", "egb", "Nm", "NTm")}
                        d_["YB"] = [sbt(es, "YB%d_%d" % (i_, q_), [128, 384], BF16) for i_ in range(2)]
                        IB.append(d_)
                    nW2T = sbt(es, "nW2T", [128, 128], BF16)
                    nW2Tm = sbt(es, "nW2Tm", [128, 16, 128], BF16)
                    Ut = sbt(es, "Ut", [128, 128], BF16)
                    Sf = sbt(es, "Sf", [128, 128], F32)
                    Sb = sbt(es, "Sb", [128, 128], BF16)
                    S0f = sbt(es, "S0f", [128, 16, 128], F32)
                    S0b = sbt(es, "S0b", [128, 16, 128], BF16)
                    kem = [sbt(es, "kem%d" % i, [128, 128], BF16) for i in range(2)]
                    for i_ in range(2):
                        P.memset('dve', pre2[i_][:, 0:3], 0.0, w=['pre%d' % i_])
                    P.memset('dve', nW2Tm[:].rearrange("p a b -> p (a b)"), 0.0, w=['nW2Tm'])

                    w_qkvz = W["w_in"][:, 0:4096].rearrange("(k p) (j hh c) -> p k j hh c", p=128, j=4, hh=8)

                    sgc_v = sgc_d.rearrange("r (j hh c) -> r j hh c", j=3, hh=8)

                    def load_wh(h_):
                        for j in range(4):
                            P.dma('pool', wh[h_ % 2][:, :, j, :], w_qkvz[:, :, j, h_, :], w=['wh%d' % (h_ % 2)])

                    load_wh(0)
                    P.dma('sp', gst[:], sgc_v[:, :, 0, :], w=['gst'])
                    for h in range(8):
                        whb = wh[h % 2]
                        whk = 'wh%d' % (h % 2)
                        if h + 1 < 8:
                            load_wh(h + 1)
                        P.dma('sp', S0f[:], sS_d[:, h, :, :].rearrange("s k v -> k s v"), w=['S0f'])
                        P.dma('pool', S0b[:], sS_d[:, h, :, :].rearrange("s k v -> k s v"), w=['S0b'])
                        for j in range(3):
                            P.mm(PK[:, j * 48:(j + 1) * 48], gst[0:48, j, :], cst[0:48, C_ID:C_ID + 48], r=['gst', 'cst'], w=['PKs'],
                                 inc=(j == 2))
                        for j in range(3):
                            P.cp('dve', pres3[j][:, :, 0:3], PK[:, j * 48:(j + 1) * 48].rearrange("p (s r) -> p s r", r=3),
                                 r=['PKs'], w=['pres%d' % j])
                        if h + 1 < 8:
                            P.dma('sp', gst[:], sgc_v[:, :, h + 1, :], w=['gst'])

                        def PA(j):
                            pre, prk = pre2[j % 2], 'pre%d' % (j % 2)
                            pres, psk = pres3[j], 'pres%d' % j
                            for (t0, n) in GROUPS:
                                bank, bk = nbank()
                                proj_fm(bank, bk, lambda kc: whb[:, kc, j, :], xnT, xn_keys(t0, n), t0, n, [whk])
                                if t0 < TP:
                                    P.cp('act', pre[:, 3 + t0:3 + t0 + n], bank[:, 0:n], r=[bk], w=[prk])
                                else:
                                    P.cp('act', pres[:, :, 3:11], bank[:, 0:128].rearrange("p (s t) -> p s t", t=8), r=[bk], w=[psk])

                        def RA(j):
                            pre, prk = pre2[j % 2], 'pre%d' % (j % 2)
                            pres, psk = pres3[j], 'pres%d' % j
                            P.cp('dve', tail[:].rearrange("p (s r) -> p s r", r=3), pres[:, :, 8:11], r=[psk], w=['tail'])
                            P.mm(PU[0:3, 0:128], pre[:, TP:TP + 3], identf, r=[prk, 'cst'], w=['PU0'])
                            P.mm(PU[0:48, 128:256], tail[:], identf, r=['tail', 'cst'], w=['PU1'])

                        def RB(j):
                            ch = j * 8 + h
                            c3 = cso3[j % 2]
                            P.cp('dve', c3[:], PU[0:3, 0:128], r=['PU0'], w=['cso3_%d' % (j % 2)])
                            P.dma('sp', gcp_d[:, ch * 128:(ch + 1) * 128], c3[:], r=['cso3_%d' % (j % 2)], w=['gcp_d'])
                            c48 = cso[j % 2]
                            P.cp('dve', c48[:], PU[0:48, 128:256], r=['PU1'], w=['cso%d' % (j % 2)])
                            P.dma('sp', gcs_d[:, ch * 128:(ch + 1) * 128], c48[:], r=['cso%d' % (j % 2)], w=['gcs_d'])

                        def CV(j):
                            ch = j * 8 + h
                            pre, prk = pre2[j % 2], 'pre%d' % (j % 2)
                            pres, psk = pres3[j], 'pres%d' % j
                            ac, ack = acc2[j % 2], 'acc%d' % (j % 2)
                            P.ts('dve', ac[:, 0:TP], pre[:, 0:TP], cw[:, ch, 0:1], ALU.mult, r=[prk, 'cw'], w=[ack])
                            for i4 in range(1, 4):
                                P.stt('dve', ac[:, 0:TP], pre[:, i4:i4 + TP], cw[:, ch, i4:i4 + 1], ac[:, 0:TP],
                                      ALU.mult, ALU.add, r=[prk, 'cw', ack], w=[ack])
                            accs = ac[:, TP:TOK].rearrange("p (s t) -> p s t", t=8)
                            P.ts('dve', accs, pres[:, :, 0:8], cw[:, ch, 0:1], ALU.mult, r=[psk, 'cw'], w=[ack])
                            for i4 in range(1, 4):
                                P.stt('dve', accs, pres[:, :, i4:i4 + 8], cw[:, ch, i4:i4 + 1], accs,
                                      ALU.mult, ALU.add, r=[psk, 'cw', ack], w=[ack])
                            dst = (qT, kT, vT)[j]
                            dk_ = ('qT', 'kT', 'vT')[j]
                            P.act(dst[:], ac[:], AF.Silu, r=[ack], w=[dk_])
                            if j < 2:
                                P.act(sq[:], dst[:], AF.Square, r=[dk_], w=['sq'])

                        def PC(j):
                            dst = (qT, kT, vT)[j]
                            dk_ = ('qT', 'kT', 'vT')[j]
                            for gi, (t0, n) in enumerate(GROUPS):
                                rsb = rsbs[gi % 2]
                                rk_ = 'rsb%d' % (gi % 2)
                                bank, bk = nbank()
                                P.mm(bank[:, 0:n], (ones128b if j == 0 else ones1b)[:], sq[:, t0:t0 + n],
                                     r=['sq', 'ones1b', 'ones128b'], w=[bk])
                                P.act(rsb[:, 0:n], bank[:, 0:n], AF.Ln, bias=(128.0 * EPS if j == 0 else EPS),
                                      r=[bk], w=[rk_])
                                P.act(rsb[:, 0:n], rsb[:, 0:n], AF.Exp, scale=-0.5, r=[rk_], w=[rk_])
                                P.tt('pool', dst[:, t0:t0 + n], dst[:, t0:t0 + n], rsb[:, 0:n], ALU.mult, r=[dk_, rk_], w=[dk_])

                        PA(0)
                        PA(1)
                        RA(0)
                        CV(0)
                        RB(0)
                        PA(2)
                        PC(0)
                        RA(1)
                        CV(1)
                        RB(1)
                        PC(1)
                        RA(2)
                        CV(2)
                        RB(2)
                        def prep(i):
                            smp = (i == 16)
                            c0 = i * 128
                            nmk = C_NMS if smp else C_NMP
                            stk = C_STS if smp else C_STP
                            hs = i % 4
                            q_ = i % 2
                            hb = HB[hs]
                            ib = IB[q_]
                            hk = lambda nm: '%s_%d' % (nm, hs)
                            ik = lambda nm: '%s_i%d' % (nm, q_)
                            sc = lambda nm: gtab[nm][:, i, h:h + 1]
                            PKb = PK if q_ == 0 else banks[2]
                            pkk = (lambda x_: 'PK%d' % x_) if q_ == 0 else (lambda x_: 'bank2k%d' % x_)
                            PDb = banks[q_]
                            pdk = 'bank%dd' % q_
                            pt = PTs[q_]
                            ptk = 'pt%d' % q_
                            P.tr(pt[:, 0:128], kT[:, c0:c0 + 128], identb, r=['kT', 'cstb'], w=[ptk], inc=False)
                            P.tr(pt[:, 128:256], vT[:, c0:c0 + 128], identb, r=['vT', 'cstb'], w=[ptk])
                            P.act(hb["k_eg"][:], pt[:, 0:128], AF.Copy, scale=sc("egc"), r=[ptk, 'egc'], w=[hk("k_eg")])
                            P.ts('dve', hb["ke"][:], pt[:, 0:128], sc("kdec"), ALU.mult, r=[ptk, 'kdec'], w=[hk("ke")])
                            P.cp('dve', hb["v_tok"][:], pt[:, 128:256], r=[ptk], w=[hk("v_tok")])
                            P.mm(PKb[:, 0:128], kT[:, c0:c0 + 128], kT[:, c0:c0 + 128], r=['kT'], w=[pkk(0)])
                            P.mm(PKb[:, 128:256], kT[:, c0:c0 + 128], qT[:, c0:c0 + 128], r=['kT', 'qT'], w=[pkk(1)])
                            P.mm(PKb[:, 256:384], cst[0:8, C_SEL + h * 128:C_SEL + (h + 1) * 128], gcT[0:8, c0:c0 + 128],
                                 start=True, stop=False, r=['cst', 'gcT'], w=[pkk(2)])
                            P.mm(PKb[:, 256:384], identb, cstb[:, nmk:nmk + 128], start=False, stop=True, r=['cstb'], w=[pkk(2)])
                            P.mm(PKb[:, 384:512], cst[0:8, C_SEL + h * 128:C_SEL + (h + 1) * 128], gcT[0:8, c0:c0 + 128],
                                 r=['cst', 'gcT'], w=[pkk(3)])
                            P.act(ib["DT"][:], PKb[:, 256:384], AF.Exp, bias=sc("negc"), r=[pkk(2), 'negc'], w=[ik("DT")])
                            P.act(ib["egb"][:], PKb[:, 384:512], AF.Exp, r=[pkk(3)], w=[ik("egb")])
                            P.tt('pool', hb["qe_t"][:], qT[:, c0:c0 + 128], ib["egb"][:], ALU.mult, r=['qT', ik("egb")], w=[hk("qe_t")])
                            P.tt('dve', ib["DTs"][:], ib["DT"][:], cstb[:, stk:stk + 128], ALU.mult, r=[ik("DT"), 'cstb'], w=[ik("DTs")])
                            P.stt('dve', ib["Nm"][:], PKb[:, 0:128], sc("beta"), ib["DTs"][:], ALU.mult, ALU.mult,
                                  r=[pkk(0), 'beta', ik("DTs")], w=[ik("Nm")])
                            P.tt('dve', hb["PTm"][:], PKb[:, 128:256], ib["DT"][:], ALU.mult, r=[pkk(1), ik("DT")], w=[hk("PTm")])
                            P.tr(pt[:, 256:384], ib["Nm"][:], identb, r=[ik("Nm"), 'cstb'], w=[ptk])
                            P.cp('act', ib["NTm"][:], pt[:, 256:384], r=[ptk], w=[ik("NTm")])
                            YB = ib["YB"]
                            yk = lambda nm, c_: '%s%d_i%d' % (nm, c_, q_)
                            P.tt('pool', YB[0][:, 256:384], identb, ib["Nm"][:], ALU.subtract, r=['cstb', ik("Nm")], w=[yk('R', 0)])
                            nst = 2 if smp else 6
                            P.mm(PDb[:, 0:128], ib["Nm"][:], ib["NTm"][:], r=[ik("NTm"), ik("Nm")], w=[pdk + 'a'], inc=False)
                            P.mm(PDb[:, 128:256], ib["NTm"][:], ib["Nm"][:], r=[ik("NTm"), ik("Nm")], w=[pdk + 'a'])
                            P.cp('act', YB[0][:, 0:256], PDb[:, 0:256], r=[pdk + 'a'], w=[yk('Y', 0)])
                            cur = 0
                            for kst in range(1, nst + 1):
                                nx = 1 - cur
                                needY = kst < nst - 1
                                needYT = kst < nst
                                last = (kst == nst)
                                if needYT:
                                    P.mm(PDb[:, 0:128], YB[cur][:, 128:256], YB[cur][:, 0:128], r=[yk('Y', cur)], w=[pdk + 'a'], inc=False)
                                if needY:
                                    P.mm(PDb[:, 128:384], YB[cur][:, 0:128], YB[cur][:, 128:384],
                                         r=[yk('Y', cur), yk('R', cur)], w=[pdk + 'a'])
                                else:
                                    P.mm(PDb[:, 256:384], YB[cur][:, 0:128], YB[cur][:, 256:384],
                                         r=[yk('Y', cur), yk('R', cur)], w=[pdk + 'a'])
                                if needY:
                                    P.cp('act', YB[nx][:, 0:256], PDb[:, 0:256], r=[pdk + 'a'], w=[yk('Y', nx)])
                                elif needYT:
                                    P.cp('act', YB[nx][:, 0:128], PDb[:, 0:128], r=[pdk + 'a'], w=[yk('Y', nx)])
                                if last:
                                    P.tt('dve', hb["R2"][:], YB[cur][:, 256:384], PDb[:, 256:384], ALU.add,
                                         r=[pdk + 'a', yk('R', cur)], w=[hk("R2")])
                                else:
                                    P.tt('dve', YB[nx][:, 256:384], YB[cur][:, 256:384], PDb[:, 256:384], ALU.add,
                                         r=[pdk + 'a', yk('R', cur)], w=[yk('R', nx)])
                                cur = nx

                        def rec(i):
                            smp = (i == 16)
                            c0 = i * 128
                            hs = i % 4
                            hb = HB[hs]
                            hk = lambda nm: '%s_%d' % (nm, hs)
                            sc = lambda nm: gtab[nm][:, i, h:h + 1]
                            R2 = hb["R2"][:]
                            R2k = hk("R2")
                            k_eg, ke, v_tok, qe_t, PTm = hb["k_eg"], hb["ke"], hb["v_tok"], hb["qe_t"], hb["PTm"]
                            P.mm(PU[:, 0:128], k_eg[:], R2, r=[hk('k_eg'), R2k], w=['PU0'])
                            P.act(nW2T[:], PU[:, 0:128], AF.Copy, scale=-1.0, r=['PU0'], w=['nW2T'])
                            if not smp:
                                P.mm(PU[:, 128:256], R2, v_tok[:], start=True, stop=(i == 0), r=[R2k, hk('v_tok')], w=['PU1'], inc=True)
                                if i > 0:
                                    P.mm(PU[:, 128:256], nW2T[:], Sb[:], start=False, stop=True, r=['nW2T', 'Sb'], w=['PU1'])
                            else:
                                for s in range(16):
                                    P.cp('dve', nW2Tm[:, s, 8 * s:8 * s + 8], nW2T[:, 8 * s:8 * s + 8], r=['nW2T'], w=['nW2Tm'])
                                P.mm(PU[:, 128:256], R2, v_tok[:], start=True, stop=False, r=[R2k, hk('v_tok')], w=['PU1'], inc=False)
                                for s in range(16):
                                    P.mm(PU[:, 128:256], nW2Tm[:, s, :], S0b[:, s, :], start=False, stop=(s == 15),
                                         r=['nW2Tm', 'S0b'], w=['PU1'])
                            P.act(Ut[:], PU[:, 128:256], AF.Copy, scale=sc("beta"), r=['PU1', 'beta'], w=['Ut'])
                            if not smp:
                                if i > 0:
                                    P.mm(PU[:, 256:384], Sb[:], qe_t[:], start=True, stop=False, r=['Sb', hk('qe_t')], w=['PU2'], inc=False)
                                P.mm(PU[:, 256:384], Ut[:], PTm[:], start=(i == 0), stop=True, r=['Ut', hk('PTm')], w=['PU2'])
                            else:
                                for s in range(16):
                                    P.mm(PU[:, 256 + 8 * s:256 + 8 * s + 8], Ut[:], PTm[:, 8 * s:8 * s + 8], start=True, stop=False,
                                         r=['Ut', hk('PTm')], w=['PU2'], inc=False)
                                    P.mm(PU[:, 256 + 8 * s:256 + 8 * s + 8], S0b[:, s, :], qe_t[:, 8 * s:8 * s + 8],
                                         start=False, stop=True, r=['S0b', hk('qe_t')], w=['PU2'], inc=(s == 15))
                            P.cp('act', acc[:, c0:c0 + 128], PU[:, 256:384], r=['PU2'], w=['acc1'])
                            if not smp:
                                P.mm(PU[:, 384:512], ke[:], Ut[:], r=[hk('ke'), 'Ut'], w=['PU3'])
                                if i == 0:
                                    P.cp('dve', Sf[:], PU[:, 384:512], r=['PU3'], w=['Sf'])
                                else:
                                    P.stt('dve', Sf[:], Sf[:], sc("gt"), PU[:, 384:512], ALU.mult, ALU.add, r=['PU3', 'gt', 'Sf'], w=['Sf'])
                                if i < 15:
                                    P.cp('act', Sb[:], Sf[:], r=['Sf'], w=['Sb'])
                                else:
                                    P.dma('sp', Sp_d[h, :, :], Sf[:], r=['Sf'], w=['Sp_d'])
                            else:
                                P.tt('pool', S0f[:], S0f[:], gtS[:, h, :].unsqueeze(2).to_broadcast([128, 16, 128]), ALU.mult,
                                     r=['S0f', 'gtS'], w=['S0f'])
                                for s4 in range(4):
                                    bank, bk = banks[3], 'bank3'
                                    for s_ in range(4):
                                        s = s4 * 4 + s_
                                        km = kem[s % 2]
                                        kmk = 'kem%d' % (s % 2)
                                        P.ts('dve', km[:], ke[:], cst[:, C_BCOL + s:C_BCOL + s + 1], ALU.mult, r=[hk('ke'), 'cst'], w=[kmk])
                                        P.mm(bank[:, s_ * 128:(s_ + 1) * 128], km[:], Ut[:], r=[kmk, 'Ut'], w=[bk])
                                    P.tt('dve', S0f[:, s4 * 4:(s4 + 1) * 4, :], S0f[:, s4 * 4:(s4 + 1) * 4, :],
                                         bank.rearrange("p (s v) -> p s v", s=4), ALU.add, r=[bk, 'S0f'], w=['S0f'])
                                P.dma('sp', Ss_d[:, h, :, :].rearrange("s k v -> k s v"), S0f[:], r=['S0f'], w=['Ss_d'])

                        P.rec_begin()
                        prep(0)
                        prep(1)
                        for j2 in range(0, NT, 2):
                            for i_ in (j2 + 2, j2 + 3):
                                if i_ < NT:
                                    prep(i_)
                            rec(j2)
                            if j2 + 1 < NT:
                                rec(j2 + 1)
                        P.emit_scheduled(P.rec_end())
                        P.act(sq[:], acc[:], AF.Square, r=['acc1'], w=['sq'])
                        for gi, (t0, n) in enumerate(GROUPS):
                            rsb = rsbs[gi % 2]
                            rk_ = 'rsb%d' % (gi % 2)
                            bank, bk = nbank()
                            P.mm(bank[:, 0:n], onesi128b[:], sq[:, t0:t0 + n], r=['sq', 'onesi128b'], w=[bk])
                            P.act(rsb[:, 0:n], bank[:, 0:n], AF.Ln, bias=EPS, r=[bk], w=[rk_])
                            P.act(rsb[:, 0:n], rsb[:, 0:n], AF.Exp, scale=-0.5, r=[rk_], w=[rk_])
                            P.tt('dve', rsb[:, 0:n], rsb[:, 0:n], acc[:, t0:t0 + n], ALU.mult, r=[rk_, 'acc1'], w=[rk_])
                            bank, bk = nbank()
                            proj_fm(bank, bk, lambda kc: whb[:, kc, 3, :], xnT, xn_keys(t0, n), t0, n, [whk])
                            P.act(zsg[:, 0:n], bank[:, 0:n], AF.Silu, r=[bk], w=['zsg'])
                            P.stt('dve', oTn[:, h, t0:t0 + n], rsb[:, 0:n], won[:, 0:1], zsg[:, 0:n], ALU.mult, ALU.mult,
                                  r=[rk_, 'won', 'zsg'], w=[('oTn', h)])
                    P.barrier()

                buT = sbt(esB, "buT", [128, 8, TOK], BF16)
                with ExitStack() as es:
                    wc = [sbt(es, "wc%d" % i, [128, 8, 3, 128], BF16) for i in range(2)]
                    cx = sbt(es, "cx", [128, TP + 2], F32)
                    cxs = sbt(es, "cxs", [128, 16, 10], F32)
                    cgf = [sbt(es, "cgf%d" % i, [128, 512], F32) for i in range(2)]
                    bgf = sbt(es, "bgf", [128, TOK], F32)
                    u = sbt(es, "u", [128, TOK], F32)
                    sst = sbt(es, "sst", [32, D], F32)
                    scw = sbt(es, "scw", [128, 8, 3], F32)
                    tail2 = sbt(es, "tail2", [128, 32], F32)
                    so2 = [sbt(es, "so2_%d" % i, [2, 128], F32) for i in range(2)]
                    so32 = [sbt(es, "so32_%d" % i, [32, 128], F32) for i in range(2)]
                    P.dma('sp', sst[:], ssc_d, w=['sst'])
                    for i3 in range(3):
                        P.dma('sp', scw[:, :, i3], W["sc_conv_w"][i3].rearrange("(c p) -> p c", p=128), w=['scw'], slow=True)
                    P.memset('dve', cx[:, 0:2], 0.0, w=['cx'])
                    w_sc = W["w_in"][:, 4112:7184].rearrange("(k p) (j cc n) -> p k j cc n", p=128, j=3, cc=8)
                    def load_wc(c_):
                        for j in range(3):
                            P.dma('pool', wc[c_ % 2][:, :, j, :], w_sc[:, :, j, c_, :], w=['wc%d' % (c_ % 2)])

                    load_wc(0)
                    for c in range(8):
                        wcb = wc[c % 2]
                        wck = 'wc%d' % (c % 2)
                        if c + 1 < 8:
                            load_wc(c + 1)
                        bank, bk = nbank()
                        P.mm(bank[:, 0:32], sst[0:32, c * 128:(c + 1) * 128], cst[0:32, C_ID:C_ID + 32], r=['sst', 'cst'], w=[bk])
                        P.cp('dve', cxs[:, :, 0:2], bank[:, 0:32].rearrange("p (s r) -> p s r", r=2), r=[bk], w=['cxs'])
                        for gi, (t0, n) in enumerate(GROUPS):
                            cg_ = cgf[gi % 2]
                            cgk = 'cgf%d' % (gi % 2)
                            bank, bk = nbank()
                            proj_fm(bank, bk, lambda kc: wcb[:, kc, 1, :], xnT, xn_keys(t0, n), t0, n, [wck])
                            P.cp('act', cg_[:, 0:n], bank[:, 0:n], r=[bk], w=[cgk])
                            bank, bk = nbank()
                            proj_fm(bank, bk, lambda kc: wcb[:, kc, 2, :], xnT, xn_keys(t0, n), t0, n, [wck])
                            if t0 < TP:
                                P.tt('dve', cx[:, 2 + t0:2 + t0 + n], bank[:, 0:n], cg_[:, 0:n], ALU.mult, r=[bk, cgk], w=['cx'])
                            else:
                                P.tt('dve', cxs[:, :, 2:10], bank[:, 0:128].rearrange("p (s t) -> p s t", t=8),
                                     cg_[:, 0:128].rearrange("p (s t) -> p s t", t=8), ALU.mult, r=[bk, cgk], w=['cxs'])
                            bank, bk = nbank()
                            proj_fm(bank, bk, lambda kc: wcb[:, kc, 0, :], xnT, xn_keys(t0, n), t0, n, [wck])
                            P.cp('act', bgf[:, t0:t0 + n], bank[:, 0:n], r=[bk], w=['bgf'])
                        bank, bk = nbank()
                        P.mm(bank[0:2, 0:128], cx[:, TP:TP + 2], identf, r=['cx', 'cst'], w=[bk])
                        P.cp('dve', so2[c % 2][:], bank[0:2, 0:128], r=[bk], w=['so2_%d' % (c % 2)])
                        P.dma('sp', scp_d[:, c * 128:(c + 1) * 128], so2[c % 2][:], r=['so2_%d' % (c % 2)], w=['scp_d'])
                        P.cp('dve', tail2[:].rearrange("p (s r) -> p s r", r=2), cxs[:, :, 8:10], r=['cxs'], w=['tail2'])
                        bank, bk = nbank()
                        P.mm(bank[0:32, 0:128], tail2[:], identf, r=['tail2', 'cst'], w=[bk])
                        P.cp('dve', so32[c % 2][:], bank[0:32, 0:128], r=[bk], w=['so32_%d' % (c % 2)])
                        P.dma('sp', scs_d[:, c * 128:(c + 1) * 128], so32[c % 2][:], r=['so32_%d' % (c % 2)], w=['scs_d'])
                        P.ts('dve', u[:, 0:TP], cx[:, 0:TP], scw[:, c, 0:1], ALU.mult, r=['cx', 'scw'], w=['u'])
                        for i3 in range(1, 3):
                            P.stt('dve', u[:, 0:TP], cx[:, i3:i3 + TP], scw[:, c, i3:i3 + 1], u[:, 0:TP], ALU.mult, ALU.add,
                                  r=['cx', 'scw', 'u'], w=['u'])
                        us = u[:, TP:TOK].rearrange("p (s t) -> p s t", t=8)
                        P.ts('dve', us, cxs[:, :, 0:8], scw[:, c, 0:1], ALU.mult, r=['cxs', 'scw'], w=['us'])
                        for i3 in range(1, 3):
                            P.stt('dve', us, cxs[:, :, i3:i3 + 8], scw[:, c, i3:i3 + 1], us, ALU.mult, ALU.add,
                                  r=['cxs', 'scw', 'us'], w=['us'])
                        P.tt('dve', buT[:, c, :], bgf[:], u[:], ALU.mult, r=['bgf', 'u', 'us'], w=[('buT', c)])
                    P.barrier()

                with ExitStack() as esM:
                    mT = sbt(esM, "mT", [128, 8, TOK], BF16)
                    Wmix = sbt(esM, "Wmix", [128, 8, D], BF16)
                    with ExitStack() as es:
                        w3 = [sbt(es, "w3_%d" % i, [128, 8, 4, 128], BF16) for i in range(2)]
                        sga = [sbt(es, "sga%d" % i, [128, 512], F32) for i in range(4)]
                        m1 = [sbt(es, "m1_%d" % i, [128, 512], F32) for i in range(4)]
                        OT_ALL = [('oTn', hh) for hh in range(8)]
                        BU_ALL = [('buT', cc) for cc in range(8)]
                        def load_w3(c_):
                            wb_ = w3[c_ % 2]
                            k_ = 'w3_%d' % (c_ % 2)
                            cs_ = slice(c_ * 128, (c_ + 1) * 128)
                            P.dma('pool', wb_[:, :, 2, :], W["w_in"][:, 7184 + c_ * 128:7184 + (c_ + 1) * 128].rearrange("(k p) n -> p k n", p=128), w=[k_])
                            P.dma('pool', wb_[:, :, 0, :], W["w_gdn_o"][:, cs_].rearrange("(k p) n -> p k n", p=128), w=[k_])
                            P.dma('pool', wb_[:, :, 3, :], W["w_in"][:, 8208 + c_ * 128:8208 + (c_ + 1) * 128].rearrange("(k p) n -> p k n", p=128), w=[k_])
                            P.dma('pool', wb_[:, :, 1, :], W["w_sc_o"][:, cs_].rearrange("(k p) n -> p k n", p=128), w=[k_])

                        load_w3(0)
                        load_w3(1)
                        P.dma('pool', Wmix[:], W["w_mix_out"].rearrange("(k p) n -> p k n", p=128), w=['Wmix'])
                        for c in range(8):
                            wb3 = w3[c % 2]
                            w3k = 'w3_%d' % (c % 2)
                            if c >= 1 and c + 1 < 8:
                                load_w3(c + 1)
                            for gi, (t0, n) in enumerate(GROUPS):
                                for br in range(2):
                                    src_t, src_k = (oTn, OT_ALL) if br == 0 else (buT, BU_ALL)
                                    bank, bk = nbank()
                                    proj_fm(bank, bk, lambda kc: wb3[:, kc, 2 + br, :], xnT, xn_keys(t0, n), t0, n, [w3k])
                                    bi_ = 2 * (gi % 2) + br
                                    sg = sga[bi_]
                                    sgk = 'sga%d' % bi_
                                    P.act(sg[:, 0:n], bank[:, 0:n], AF.Sigmoid, r=[bk], w=[sgk])
                                    bank, bk = nbank()
                                    proj_fm(bank, bk, lambda kc: wb3[:, kc, br, :], src_t, src_k, t0, n, [w3k])
                                    P.tt('dve', m1[bi_][:, 0:n], bank[:, 0:n], sg[:, 0:n], ALU.mult, r=[bk, sgk], w=['m1_%d' % bi_])
                                g0_ = 2 * (gi % 2)
                                P.tt('pool', mT[:, c, t0:t0 + n], m1[g0_][:, 0:n], m1[g0_ + 1][:, 0:n], ALU.add,
                                     r=['m1_%d' % g0_, 'm1_%d' % (g0_ + 1)], w=[('mT', c)])
                        P.barrier()

                    with ExitStack() as es:
                        alloc_norm(es, "b")
                        xo = [sbt(es, "xo%d" % i, [128, D], F32) for i in range(2)]
                        MT_ALL = [('mT', cc) for cc in range(8)]
                        for i in range(NT):
                            xt, xk = load_x(i, x_d)
                            xob = xo[i % 2]
                            xok = 'xo%d' % (i % 2)
                            for half in range(2):
                                bank, bk = nbank()
                                for kc in range(8):
                                    P.mm(bank, mT[:, kc, i * 128:(i + 1) * 128], Wmix[:, kc, half * 512:(half + 1) * 512],
                                         start=(kc == 0), stop=(kc == 7), r=MT_ALL + ['Wmix'], w=[bk])
                                P.tt('dve', xob[:, half * 512:(half + 1) * 512], xt[:, half * 512:(half + 1) * 512], bank, ALU.add,
                                     r=[bk, xk], w=[xok])
                            P.dma('pool', xres_d[i * 128:(i + 1) * 128, :], xob[:], r=[xok], w=[('xres', i)])
                        P.barrier()
            esX.close()

            with ExitStack() as es:
                alloc_norm(es, "c", False)
                Wxq = sbt(es, "Wxq", [128, 8, D], BF16)
                Wxo = sbt(es, "Wxo", [128, 8, D], BF16)
                xcTs = [sbt(es, "xcT%d" % i, [128, 8, 512], BF16) for i in range(2)]
                hqTs = [sbt(es, "hqT%d" % i, [128, 8, 512], BF16) for i in range(2)]
                hqm = sbt(es, "hqm", [128, 8, 16, 128], BF16)
                xg = [sbt(es, "xg%d" % i, [128, D], F32) for i in range(8)]
                efs = [sbt(es, "ef%d" % i, [128, 4, 256], F32) for i in range(2)]
                pbs = [sbt(es, "pb%d" % i, [128, 4, 256], BF16) for i in range(2)]
                pTbs = [sbt(es, "pTb%d" % i, [128, 8, 128], BF16) for i in range(2)]
                ctxTs = [sbt(es, "ctxT%d" % i, [128, 8, 128], BF16) for i in range(2)]
                smxs = [sbt(es, "smx%d" % i, [128, 16], F32) for i in range(2)]

                def SR(par_, hh_):
                    if par_ == 0:
                        return PS2[:, hh_ * 256:(hh_ + 1) * 256], 'PS2'
                    t_ = PK if hh_ < 2 else PU
                    return t_[:, (hh_ % 2) * 256:(hh_ % 2 + 1) * 256], ('PKx' if hh_ < 2 else 'PUx')

                def SRS(hh_):
                    return [(PS2[:, 0:256], 'PS2a'), (PK[:, 0:256], 'PKx'), (PS2[:, 512:768], 'PS2b'), (PU[:, 0:256], 'PUx')][hh_]

                def CR(par_, c_):
                    if par_ == 0:
                        return PS2[:, c_ * 128:(c_ + 1) * 128], 'PS2'
                    t_ = PK if c_ < 4 else PU
                    return t_[:, (c_ % 4) * 128:(c_ % 4 + 1) * 128], ('PKx' if c_ < 4 else 'PUx')
                ckfs = [sbt(es, "ckf%d" % i, [128, 2, D], F32) for i in range(1)]
                ckbs = [sbt(es, "ckb%d" % i, [128, 2, D], BF16) for i in range(2)]
                cvb = [sbt(es, "cvb%d" % i, [128, 2, D], BF16) for i in range(2)]
                KTs = [sbt(es, "KTs%d" % i, [128, 8, 256], BF16) for i in range(2)]
                P.dma('pool', Wxq[:], W["w_xq"].rearrange("(k p) n -> p k n", p=128), w=['Wxq'])
                P.dma('pool', Wxo[:], W["w_xo"].rearrange("(k p) n -> p k n", p=128), w=['Wxo'])
                P.memset('dve', hqm[:].rearrange("p a b c -> p (a b c)"), 0.0, w=['hqm'])
                tile_groups = [[0, 1, 2, 3], [4, 5, 6, 7], [8, 9, 10, 11], [12, 13, 14, 15], [16]]
                P.rec_begin()
                for gi_, tg in enumerate(tile_groups):
                    n = 128 * len(tg)
                    xcT = xcTs[gi_ % 2]
                    hqT = hqTs[gi_ % 2]
                    xck = 'xcT%d' % (gi_ % 2)
                    hqk = 'hqT%d' % (gi_ % 2)
                    xo_ = 4 * (gi_ % 2)
                    for sl, i in enumerate(tg):
                        P.dma('sp', xg[xo_ + sl][:], xres_d[i * 128:(i + 1) * 128, :], r=[('xres', i)], w=['xg%d' % (xo_ + sl)])
                        norm_tile(xg[xo_ + sl][:], ['xg%d' % (xo_ + sl)], wn_x, "norm_x_w", xcT[:, :, sl * 128:(sl + 1) * 128], [xck])
                    for c in range(8):
                        bank, bk = nbank2()
                        for kc in range(8):
                            P.mm(bank[:, 0:n], Wxq[:, kc, c * 128:(c + 1) * 128], xcT[:, kc, 0:n], start=(kc == 0), stop=(kc == 7),
                                 r=['Wxq', xck], w=[bk])
                        P.cp('act', hqT[:, c, 0:n], bank[:, 0:n], r=[bk], w=[hqk])
                    for sl, i in enumerate(tg):
                        smp = (i == 16)
                        t0 = sl * 128
                        par = 0 if smp else (i % 2)
                        ef, pb, pTb, ctxT, smx = efs[par], pbs[par], pTbs[par], ctxTs[par], smxs[par]
                        efk, pbk, pTk, ctk, sk_ = 'ef%d' % par, 'pb%d' % par, 'pTb%d' % par, 'ctxT%d' % par, 'smx%d' % par
                        if not smp:
                            for hh in range(4):
                                sr_, srk = SR(par, hh)
                                for dc in range(2):
                                    P.mm(sr_, hqT[:, 2 * hh + dc, t0:t0 + 128], KT[:, 2 * hh + dc, :],
                                         start=(dc == 0), stop=(dc == 1), r=[hqk, 'KT'], w=[srk])
                        else:
                            for s in range(16):
                                P.cp('dve', hqm[:, :, s, 8 * s:8 * s + 8], hqT[:, :, 8 * s:8 * s + 8], r=[hqk], w=['hqm'])
                            for s in range(16):
                                cf = ckfs[0]
                                cfk = 'ckf0'
                                cb = ckbs[s % 2]
                                cbk = 'ckb%d' % (s % 2)
                                P.dma('sp', cf[:], ck_d[s].rearrange("(sc p) n -> p sc n", p=128), w=[cfk])
                                P.cp('pool', cb[:, 0, :], cf[:, 0, :], r=[cfk], w=[cbk + 'a'])
                                P.cp('dve', cb[:, 1, :], cf[:, 1, :], r=[cfk], w=[cbk + 'b'])
                                kts = KTs[s % 2]
                                ktk = 'KTs%d' % (s % 2)
                                for half in range(2):
                                    pt, ptk = npt()
                                    ptv = pt[:].rearrange("p (c s) -> p c s", c=4)
                                    for cc in range(4):
                                        c = half * 4 + cc
                                        for scn in range(2):
                                            P.tr(ptv[:, cc, scn * 128:(scn + 1) * 128], cb[:, scn, c * 128:(c + 1) * 128], identb,
                                                 r=[cbk + 'a', cbk + 'b', 'cstb'], w=[ptk], inc=(cc == 3 and scn == 1))
                                    P.cp('act' if half == 0 else 'dve', kts[:, half * 4:(half + 1) * 4, :], ptv, r=[ptk], w=[ktk])
                                for hh in range(4):
                                    sr_, srk = SRS(hh)
                                    for dc in range(2):
                                        P.mm(sr_, hqm[:, 2 * hh + dc, s, :], kts[:, 2 * hh + dc, :],
                                             start=(s == 0 and dc == 0), stop=(s == 15 and dc == 1),
                                             r=['hqm', ktk], w=[srk], inc=(dc == 1), skip=True)
                        if smp:
                            for hh in range(4):
                                sr_, srk = SRS(hh)
                                P.add('dve', lambda o_=smx[:, hh:hh + 1], i_=sr_: nc.vector.tensor_reduce(
                                    out=o_, in_=i_.rearrange("p (h s) -> p h s", h=1), axis=AX.X, op=ALU.max),
                                    r=[srk], w=[sk_ + 'm'])
                        elif par == 0:
                            P.add('dve', lambda o_=smx[:, 0:4]: nc.vector.tensor_reduce(out=o_, in_=PS2[:].rearrange("p (h s) -> p h s", h=4),
                                                                                        axis=AX.X, op=ALU.max), r=['PS2'], w=[sk_ + 'm'])
                        else:
                            P.add('dve', lambda o_=smx[:, 0:2]: nc.vector.tensor_reduce(out=o_, in_=PK[:].rearrange("p (h s) -> p h s", h=2),
                                                                                        axis=AX.X, op=ALU.max), r=['PKx'], w=[sk_ + 'm'])
                            P.add('dve', lambda o_=smx[:, 2:4]: nc.vector.tensor_reduce(out=o_, in_=PU[:].rearrange("p (h s) -> p h s", h=2),
                                                                                        axis=AX.X, op=ALU.max), r=['PUx'], w=[sk_ + 'm2'])
                        P.ts('dve', smx[:, 4:8], smx[:, 0:4], -1.0 / 16.0, ALU.mult, r=[sk_ + 'm', sk_ + 'm2'], w=[sk_ + 'n'])
                        for hh in range(4):
                            sr_, srk = SRS(hh) if smp else SR(par, hh)
                            P.act(ef[:, hh, :], sr_, AF.Exp, bias=smx[:, 4 + hh:5 + hh], scale=1.0 / 16.0,
                                  accum=smx[:, 8 + hh:9 + hh], r=[srk, sk_ + 'n'], w=[efk, sk_ + 's'])
                        P.recip(smx[:, 12:16], smx[:, 8:12], r=[sk_ + 's'], w=[sk_ + 'r'])
                        P.tt('dve', pb[:], ef[:], smx[:, 12:16].unsqueeze(2).to_broadcast([128, 4, 256]), ALU.mult,
                             r=[efk, sk_ + 'r'], w=[pbk])
                        pt, ptk = npt()
                        ptv = pt[:].rearrange("p (k c) -> p k c", k=8)
                        for hh in range(4):
                            for scn in range(2):
                                P.tr(ptv[:, hh * 2 + scn, :], pb[:, hh, scn * 128:(scn + 1) * 128], identb, r=[pbk, 'cstb'], w=[ptk],
                                     inc=(hh == 3 and scn == 1))
                        P.cp('act', pTb[:], ptv, r=[ptk], w=[pTk])
                        PSc = PS2[:].rearrange("p (c t) -> p c t", c=8)
                        if not smp:
                            for hh in range(4):
                                for dc in range(2):
                                    cr_, crk = CR(par, 2 * hh + dc)
                                    for scn in range(2):
                                        P.mm(cr_, Vb[:, scn, hh * 256 + dc * 128:hh * 256 + (dc + 1) * 128],
                                             pTb[:, hh * 2 + scn, :], start=(scn == 0), stop=(scn == 1), r=['Vb', pTk], w=[crk],
                                             inc=(hh == 3 and dc == 1 and scn == 1))
                        else:
                            for s in range(16):
                                cvt = cvb[s % 2]
                                cvk = 'cvb%d' % (s % 2)
                                P.dma('pool', cvt[:], cv_d[s].rearrange("(sc p) n -> p sc n", p=128), w=[cvk])
                                for hh in range(4):
                                    for dc in range(2):
                                        for scn in range(2):
                                            P.mm(PSc[:, 2 * hh + dc, 8 * s:8 * s + 8],
                                                 cvt[:, scn, hh * 256 + dc * 128:hh * 256 + (dc + 1) * 128],
                                                 pTb[:, hh * 2 + scn, 8 * s:8 * s + 8], start=(scn == 0), stop=(scn == 1),
                                                 r=[cvk, pTk], w=['PS2'], inc=(hh == 3 and dc == 1 and scn == 1))
                        if par == 0:
                            P.cp('act', ctxT[:], PSc, r=['PS2'], w=[ctk])
                        else:
                            P.cp('act', ctxT[:, 0:4, :], PK[:].rearrange("p (c t) -> p c t", c=4), r=['PKx'], w=[ctk])
                            P.cp('act', ctxT[:, 4:8, :], PU[:].rearrange("p (c t) -> p c t", c=4), r=['PUx'], w=[ctk + 'b'])
                        xgb = xg[xo_ + sl]
                        xgk = 'xg%d' % (xo_ + sl)
                        for half in range(2):
                            bank, bk = nbank2()
                            for kc in range(8):
                                P.mm(bank, ctxT[:, kc, :], Wxo[:, kc, half * 512:(half + 1) * 512], start=(kc == 0), stop=(kc == 7),
                                     r=[ctk, ctk + 'b', 'Wxo'], w=[bk])
                            P.tt('dve', xgb[:, half * 512:(half + 1) * 512], xgb[:, half * 512:(half + 1) * 512], bank, ALU.add,
                                 r=[bk, xgk], w=[xgk])
                        P.dma('sp', xres_d[i * 128:(i + 1) * 128, :], xgb[:], r=[xgk, ('xres', i)], w=[('xres2', i)])
                P.emit_scheduled(P.rec_end())
                P.barrier()

            with ExitStack() as es:
                alloc_norm(es, "d", False)
                Wup = sbt(es, "Wup", [128, 8, 4 * D], BF16)
                Wdn = sbt(es, "Wdn", [128, 32, D], BF16)
                hT = sbt(es, "hT", [128, 32, 256], BF16)
                rl = [sbt(es, "rl%d" % i, [128, 512], F32) for i in range(2)]
                xg2 = [sbt(es, "xd%d" % i, [128, D], F32) for i in range(4)]
                xdT = [sbt(es, "xdT%d" % i, [128, 8, 256], BF16) for i in range(2)]
                wf_bc = sbt(es, "wf_bc", [128, D], F32)
                P.dma('sp', wf_bc[:], W["norm_f_w"].partition_broadcast(128), w=['wf_bc'])
                for q4 in range(4):
                    P.dma('pool', Wup[:, :, q4 * D:(q4 + 1) * D], W["w_mlp_up"][:, q4 * D:(q4 + 1) * D].rearrange("(k p) n -> p k n", p=128),
                          w=['Wup%d' % q4])
                for q4 in range(4):
                    P.dma('pool', Wdn[:, q4 * 8:(q4 + 1) * 8, :], W["w_mlp_down"][q4 * D:(q4 + 1) * D, :].rearrange("(k p) n -> p k n", p=128),
                          w=['Wdn%d' % q4])
                pairs = [[2 * p_, 2 * p_ + 1] for p_ in range(8)] + [[16]]
                HT_ALL = [('hT', f2) for f2 in range(16)]

                def de_norm(pi):
                    tl = pairs[pi]
                    xd = xdT[pi % 2]
                    xdk = 'xdT%d' % (pi % 2)
                    for t_, i in enumerate(tl):
                        xb = xg2[i % 4]
                        xbk = 'xd%d' % (i % 4)
                        P.dma('sp', xb[:], xres_d[i * 128:(i + 1) * 128, :], r=[('xres2', i)], w=[xbk])
                        norm_tile(xb[:], [xbk], wn_mlp, "norm_mlp_w", xd[:, :, t_ * 128:(t_ + 1) * 128], [xdk])

                def de_up(pi):
                    tl = pairs[pi]
                    n = 128 * len(tl)
                    xd = xdT[pi % 2]
                    xdk = 'xdT%d' % (pi % 2)
                    for f2 in range(16):
                        bank, bk = nbank()
                        for fc_ in range(2):
                            fc = f2 * 2 + fc_
                            for kc in range(8):
                                P.mm(bank[:, fc_ * 256:fc_ * 256 + n], Wup[:, kc, fc * 128:(fc + 1) * 128], xd[:, kc, 0:n],
                                     start=(kc == 0), stop=(kc == 7), r=['Wup%d' % (fc // 8), xdk], w=[bk], inc=(kc == 7 and fc_ == 1))
                        rlb = rl[f2 % 2]
                        rlk = 'rl%d' % (f2 % 2)
                        bv = bank.rearrange("p (c t) -> p c t", c=2)[:, :, 0:n]
                        rv = rlb[:].rearrange("p (c t) -> p c t", c=2)[:, :, 0:n]
                        P.act(rv, bv, AF.Relu, r=[bk], w=[rlk])
                        P.tt('pool', hT[:, f2 * 2:(f2 + 1) * 2, 0:n], rv, rv, ALU.mult, r=[rlk], w=[('hT', f2)])

                def de_down(pi):
                    tl = pairs[pi]
                    for t_, i in enumerate(tl):
                        xb = xg2[i % 4]
                        xbk = 'xd%d' % (i % 4)
                        for half in range(2):
                            bank, bk = nbank()
                            for fc in range(32):
                                P.mm(bank, hT[:, fc, t_ * 128:(t_ + 1) * 128], Wdn[:, fc, half * 512:(half + 1) * 512],
                                     start=(fc == 0), stop=(fc == 31), r=HT_ALL + ['Wdn%d' % (fc // 8)], w=[bk])
                            P.tt('dve', xb[:, half * 512:(half + 1) * 512], xb[:, half * 512:(half + 1) * 512], bank, ALU.add,
                                 r=[bk, xbk], w=[xbk])
                        si = nstate['i'] % 2
                        nstate['i'] += 1
                        rstd, rsk = rstd_of(xb[:], [xbk], si)
                        P.stt('dve', xb[:], xb[:], rstd, wf_bc[:], ALU.mult, ALU.mult, r=[xbk, rsk, 'wf_bc'], w=[xbk])
                        P.dma('sp', y_d[i * 128:(i + 1) * 128, :], xb[:], r=[xbk], w=['y_d'])

                de_norm(0)
                for pi in range(len(pairs)):
                    de_up(pi)
                    if pi + 1 < len(pairs):
                        de_norm(pi + 1)
                    de_down(pi)
                P.barrier()
        except _StopBuild:
            P.barrier()
            outer.pop_all()
        P.barrier()
    return nc, P


_CACHE = {}


def kernel(**inputs):
    f32 = lambda a: np.ascontiguousarray(np.asarray(a, dtype=np.float32))
    xp = f32(inputs["x_prompt"])
    xs = f32(inputs["x_sample"])
    memp = f32(inputs["mem_prompt"])
    ck = f32(inputs["cache_mem_k"])[0]
    cv = f32(inputs["cache_mem_v"])[0]
    sgc = f32(inputs["state_gdn_conv"])[0]
    sS = f32(inputs["state_gdn"])[0]
    ssc = f32(inputs["state_sc_conv"])[0]
    wnames = ["norm_mix_w", "w_in", "gdn_conv_w", "gdn_A_log", "gdn_dt_bias", "gdn_out_norm_w", "w_gdn_o", "sc_conv_w",
              "w_sc_o", "w_mix_out", "norm_x_w", "mem_norm_w", "w_xq", "w_xkv", "w_xo", "norm_mlp_w", "w_mlp_up",
              "w_mlp_down"]
    wd = {nm: f32(inputs[nm])[0] for nm in wnames}
    wd["norm_f_w"] = f32(inputs["norm_f_w"])
    cst = _consts()
    if "nc" not in _CACHE:
        _CACHE["nc"] = build()
    nc, P = _CACHE["nc"]
    in_maps = []
    for b in range(NCORES):
        sl = slice(16 * b, 16 * (b + 1))
        m = dict(wd)
        m["x_all"] = np.ascontiguousarray(np.concatenate([xp[b], xs[sl].reshape(128, D)], axis=0))
        m["mem"] = memp[b]
        m["ck"] = np.ascontiguousarray(ck[sl].reshape(16, 256, D))
        m["cv"] = np.ascontiguousarray(cv[sl].reshape(16, 256, D))
        m["sgc"] = np.ascontiguousarray(sgc[sl].reshape(48, 3072))
        m["sS"] = np.ascontiguousarray(sS[sl])
        m["ssc"] = np.ascontiguousarray(ssc[sl].reshape(32, D))
        m["cst"] = cst
        in_maps.append(m)
    import os
    ncore_run = int(os.environ.get("KCORES", NCORES))
    res = run_bass_kernel_spmd(nc, in_maps[:ncore_run], core_ids=list(range(ncore_run)))
    R = list(res.results)
    while len(R) < NCORES:
        R.append({k: np.zeros_like(v) for k, v in R[0].items()})
    y = np.stack([r["y"] for r in R])
    y_prompt = np.ascontiguousarray(y[:, :TP, :])
    y_sample = np.ascontiguousarray(y[:, TP:, :].reshape(128, 8, D))
    mk = np.stack([r["mk"] for r in R]).reshape(1, 8, 256, 4, 256)
    mv = np.stack([r["mv"] for r in R]).reshape(1, 8, 256, 4, 256)
    gcp = np.stack([r["gcp"] for r in R]).reshape(1, 8, 3, 3072)
    Sp = np.stack([r["Sp"] for r in R]).reshape(1, 8, 8, 128, 128)
    scp = np.stack([r["scp"] for r in R]).reshape(1, 8, 2, D)
    gcs = np.stack([r["gcs"] for r in R]).reshape(1, 128, 3, 3072)
    Ss = np.stack([r["Ss"] for r in R]).reshape(1, 128, 8, 128, 128)
    scs = np.stack([r["scs"] for r in R]).reshape(1, 128, 2, D)
    return (y_prompt, y_sample, mk, mv, gcp, Sp, scp, gcs, Ss, scs)
```
